# Optimizing a Trainium2 kernel written in Bass

```python
import jax, jax.numpy as jnp
from jax import lax
import numpy as np

D_MODEL = 2048
BATCH = 4
SEQ = 2048
DEPTH = 1

CTX_LEN = 256
GRID_W = 64
MIX_W = D_MODEL
HEAD_SIZE = 128
ATTN_W = MIX_W // 2
N_HEADS = ATTN_W // HEAD_SIZE
QK_NOPE = 128
QK_ROPE = 64
V_DIM = HEAD_SIZE
Q_RANK = D_MODEL // 4
KV_RANK = D_MODEL // 8
CONV_W = MIX_W - ATTN_W
CONV_GROUPS = CONV_W // HEAD_SIZE
N_GROUPS = MIX_W // HEAD_SIZE
FFN_DIM = ((8 * D_MODEL // 3 + 127) // 128) * 128
IN_COLS = Q_RANK + KV_RANK + QK_ROPE + 3 * CONV_W
ROPE_BASE = 10000.0
Q_BLOCK = 128
EPS = 1e-6

kernel_name = "hybrid_mla_shortconv_dit_block"


def rms_norm(x, g):
    xf = x.astype(jnp.float32)
    y = xf * lax.rsqrt(jnp.mean(xf * xf, axis=-1, keepdims=True) + EPS)
    return (y * g.astype(jnp.float32)).astype(x.dtype)


def modulate(h, shift, scale):
    return h * (1 + scale) + shift


def axial_rope_tables(n_tok, dtype):
    rows = n_tok // GRID_W
    row = jnp.repeat(jnp.arange(rows), GRID_W).astype(jnp.float32)
    col = jnp.tile(jnp.arange(GRID_W), rows).astype(jnp.float32)
    axis_dim = QK_ROPE // 2
    inv = ROPE_BASE ** (-jnp.arange(0, axis_dim, 2, dtype=jnp.float32) / axis_dim)
    ang = jnp.concatenate([row[:, None] * inv, col[:, None] * inv], axis=-1)
    return jnp.cos(ang).astype(dtype), jnp.sin(ang).astype(dtype)


def apply_rope(x, cos, sin):
    xp = x.reshape(*x.shape[:-1], QK_ROPE // 2, 2)
    x1, x2 = xp[..., 0], xp[..., 1]
    return jnp.stack([x1 * cos - x2 * sin, x1 * sin + x2 * cos], axis=-1).reshape(x.shape)


def dwconv3(x, w, b):
    xp = jnp.pad(x, ((0, 0), (1, 1), (0, 0)))
    return xp[:, :-2] * w[0] + xp[:, 1:-1] * w[1] + xp[:, 2:] * w[2] + b


def split_projection(p):
    cuts = [Q_RANK, Q_RANK + KV_RANK, Q_RANK + KV_RANK + QK_ROPE,
            Q_RANK + KV_RANK + QK_ROPE + CONV_W, Q_RANK + KV_RANK + QK_ROPE + 2 * CONV_W]
    return jnp.split(p, cuts, axis=-1)


def mla_query(q_a, g_q_a, w_q_b, rope):
    b, l, _ = q_a.shape
    q = (rms_norm(q_a, g_q_a) @ w_q_b).reshape(b, l, N_HEADS, QK_NOPE + QK_ROPE).transpose(0, 2, 1, 3)
    q_nope, q_rope = q[..., :QK_NOPE], q[..., QK_NOPE:]
    if rope is not None:
        q_rope = apply_rope(q_rope, *rope)
    return jnp.concatenate([q_nope, q_rope], axis=-1)


def mla_key_value(kv_a, k_rope, g_kv_a, w_kv_b, rope):
    b, l, _ = kv_a.shape
    kv = (rms_norm(kv_a, g_kv_a) @ w_kv_b).reshape(b, l, N_HEADS, QK_NOPE + V_DIM).transpose(0, 2, 1, 3)
    k_nope, v = kv[..., :QK_NOPE], kv[..., QK_NOPE:]
    if rope is not None:
        k_rope = apply_rope(k_rope, *rope)
    k_rope = jnp.broadcast_to(k_rope[:, None], (b, N_HEADS, l, QK_ROPE))
    return jnp.concatenate([k_nope, k_rope], axis=-1), v


def block_attention(q, k, v):
    b, h, l, dqk = q.shape
    nb = l // Q_BLOCK
    scale = (QK_NOPE + QK_ROPE) ** -0.5
    qb = q.reshape(b, h, nb, Q_BLOCK, dqk).transpose(2, 0, 1, 3, 4)

    def one_block(qi):
        s = jnp.einsum("bhqd,bhkd->bhqk", qi, k).astype(jnp.float32) * scale
        p = jax.nn.softmax(s, axis=-1).astype(v.dtype)
        return jnp.einsum("bhqk,bhkd->bhqd", p, v)

    o = lax.map(one_block, qb)
    return o.transpose(1, 0, 3, 2, 4).reshape(b, l, h * V_DIM)


def short_conv(gate_b, gate_c, hc, w, bias):
    return gate_b * dwconv3(gate_c * hc, w, bias)


def merge_heads(att, conv, g_mix, w_out):
    b, l, _ = att.shape
    mix = jnp.concatenate([att, conv], axis=-1).reshape(b, l, N_GROUPS, HEAD_SIZE)
    mix = rms_norm(mix, g_mix.reshape(N_GROUPS, HEAD_SIZE)).reshape(b, l, MIX_W)
    return mix @ w_out


def conv_ffn(h, w_up, cw, cb, w_down):
    u = dwconv3(h @ w_up, cw, cb)
    a, g = jnp.split(u, 2, axis=-1)
    return (a * jax.nn.silu(g)) @ w_down


def setup_inputs(seed: int = 0) -> dict:
    key = jax.random.key(seed)
    ks = jax.random.split(key, 24)
    f32 = jnp.float32
    nrm = lambda k, shape, s: jax.random.normal(k, shape, f32) * s
    gain = lambda k, shape: 1.0 + 0.02 * jax.random.normal(k, shape, f32)
    return {
        "x": nrm(ks[0], (BATCH, SEQ, D_MODEL), 1.0),
        "c": nrm(ks[1], (BATCH, D_MODEL), 1.0),
        "ctx": nrm(ks[2], (BATCH, CTX_LEN, D_MODEL), 1.0),
        "c_ctx": nrm(ks[3], (D_MODEL,), 1.0),
        "w_ada": nrm(ks[4], (DEPTH, D_MODEL, 6 * D_MODEL), D_MODEL ** -0.5),
        "b_ada": nrm(ks[5], (DEPTH, 6 * D_MODEL), 0.01),
        "g_mix_norm": gain(ks[6], (DEPTH, D_MODEL)),
        "w_in": nrm(ks[7], (DEPTH, D_MODEL, IN_COLS), D_MODEL ** -0.5),
        "g_q_a": gain(ks[8], (DEPTH, Q_RANK)),
        "w_q_b": nrm(ks[9], (DEPTH, Q_RANK, N_HEADS * (QK_NOPE + QK_ROPE)), Q_RANK ** -0.5),
        "g_kv_a": gain(ks[10], (DEPTH, KV_RANK)),
        "w_kv_b": nrm(ks[11], (DEPTH, KV_RANK, N_HEADS * (QK_NOPE + V_DIM)), KV_RANK ** -0.5),
        "conv_w": nrm(ks[12], (DEPTH, 3, CONV_W), 3 ** -0.5),
        "conv_b": nrm(ks[13], (DEPTH, CONV_W), 0.01),
        "g_mix_out": gain(ks[14], (DEPTH, MIX_W)),
        "w_out": nrm(ks[15], (DEPTH, MIX_W, D_MODEL), MIX_W ** -0.5),
        "g_ffn_norm": gain(ks[16], (DEPTH, D_MODEL)),
        "w_up": nrm(ks[17], (DEPTH, D_MODEL, 2 * FFN_DIM), D_MODEL ** -0.5),
        "ffn_conv_w": nrm(ks[18], (DEPTH, 3, 2 * FFN_DIM), 3 ** -0.5),
        "ffn_conv_b": nrm(ks[19], (DEPTH, 2 * FFN_DIM), 0.01),
        "w_down": nrm(ks[20], (DEPTH, FFN_DIM, D_MODEL), FFN_DIM ** -0.5),
        "g_final": gain(ks[21], (D_MODEL,)),
    }


def reference(x, c, ctx, c_ctx, w_ada, b_ada, g_mix_norm, w_in, g_q_a, w_q_b, g_kv_a, w_kv_b,
              conv_w, conv_b, g_mix_out, w_out, g_ffn_norm, w_up, ffn_conv_w, ffn_conv_b,
              w_down, g_final):
    n_tok = x.shape[1]
    rope_lat = axial_rope_tables(n_tok, x.dtype)
    xc = ctx
    for l in range(DEPTH):
        last = l == DEPTH - 1
        mod_l = (jax.nn.silu(c) @ w_ada[l] + b_ada[l])[:, None, :]
        mod_c = (jax.nn.silu(c_ctx) @ w_ada[l] + b_ada[l])[None, None, :]
        sh_a, sc_a, gt_a, sh_f, sc_f, gt_f = jnp.split(mod_l, 6, axis=-1)
        csh_a, csc_a, cgt_a, csh_f, csc_f, cgt_f = jnp.split(mod_c, 6, axis=-1)

        h_l = modulate(rms_norm(x, g_mix_norm[l]), sh_a, sc_a)
        h_c = modulate(rms_norm(xc, g_mix_norm[l]), csh_a, csc_a)
        qa_l, kva_l, kr_l, cb_l, cc_l, ch_l = split_projection(h_l @ w_in[l])
        qa_c, kva_c, kr_c, cb_c, cc_c, ch_c = split_projection(h_c @ w_in[l])

        k_l, v_l = mla_key_value(kva_l, kr_l, g_kv_a[l], w_kv_b[l], rope_lat)
        k_c, v_c = mla_key_value(kva_c, kr_c, g_kv_a[l], w_kv_b[l], None)
        q_l = mla_query(qa_l, g_q_a[l], w_q_b[l], rope_lat)
        att_l = block_attention(q_l, jnp.concatenate([k_l, k_c], axis=2),
                                jnp.concatenate([v_l, v_c], axis=2))
        conv_l = short_conv(cb_l, cc_l, ch_l, conv_w[l], conv_b[l])
        x = x + gt_a * merge_heads(att_l, conv_l, g_mix_out[l], w_out[l])

        if not last:
            q_c = mla_query(qa_c, g_q_a[l], w_q_b[l], None)
            att_c = block_attention(q_c, k_c, v_c)
            conv_c = short_conv(cb_c, cc_c, ch_c, conv_w[l], conv_b[l])
            xc = xc + cgt_a * merge_heads(att_c, conv_c, g_mix_out[l], w_out[l])
            hf_c = modulate(rms_norm(xc, g_ffn_norm[l]), csh_f, csc_f)
            xc = xc + cgt_f * conv_ffn(hf_c, w_up[l], ffn_conv_w[l], ffn_conv_b[l], w_down[l])

        hf_l = modulate(rms_norm(x, g_ffn_norm[l]), sh_f, sc_f)
        x = x + gt_f * conv_ffn(hf_l, w_up[l], ffn_conv_w[l], ffn_conv_b[l], w_down[l])
    return rms_norm(x, g_final)
```

```python
import numpy as np
import ml_dtypes
from contextlib import ExitStack
import concourse.bass as bass
import concourse.mybir as mybir
from concourse.bass_utils import run_bass_kernel_spmd

F32 = mybir.dt.float32
BF16 = mybir.dt.bfloat16
AF = mybir.ActivationFunctionType
ALU = mybir.AluOpType

D = 2048
KC = 16
T = 1024
BW = 1028
NKEY = 2304
FFN = 5504
NJ = 43
EPS = 1e-6
PC = [(1, 342), (343, 342), (685, 342)]
QP = [(1, 384, [0, 1, 2]), (385, 384, [3, 4, 5]), (769, 258, [6, 7, 8])]
SCALE = 192.0 ** -0.5


class Sched:
    def __init__(self, nc, stack, n_dma_sems=40):
        self.nc = nc
        self.eng = {"pe": nc.tensor, "act": nc.scalar, "dve": nc.vector, "pool": nc.gpsimd, "sp": nc.sync}
        self.sem = {}
        self.cnt = {}
        for e in ("pe", "act", "dve", "pool"):
            self.sem[e] = stack.enter_context(nc.semaphore("s_" + e))
            self.cnt[e] = 0
        self.dsem = [stack.enter_context(nc.semaphore("d%d" % i)) for i in range(n_dma_sems)]
        self.dcnt = [0] * n_dma_sems
        self.dnext = 0
        self.dnext_pool = 0
        self.known = {e: {} for e in self.eng}
        self.last_w = {}
        self.readers = {}
        self.out_events = []

    def _deps(self, reads, writes):
        deps = {}

        def add(ev):
            if ev is None:
                return
            s, v = ev
            k = id(s)
            if k not in deps or deps[k][1] < v:
                deps[k] = (s, v)

        for r in reads:
            add(self.last_w.get(r))
        for w in writes:
            add(self.last_w.get(w))
            for ev in self.readers.get(w, {}).values():
                add(ev)
        return deps

    def _wait(self, e, deps, skip_own=False):
        kn = self.known[e]
        own = self.sem.get(e)
        for k, (s, v) in deps.items():
            if skip_own and own is not None and s is own:
                continue
            if kn.get(k, 0) >= v:
                continue
            self.eng[e].wait_ge(s, v)
            kn[k] = v

    def _record(self, ev, reads, writes):
        s, v = ev
        for r in reads:
            self.readers.setdefault(r, {})[id(s)] = (s, v)
        for w in writes:
            self.last_w[w] = ev
            self.readers[w] = {}

    def op(self, e, fn, reads=(), writes=(), inc=True):
        psr = [r for r in reads if r.startswith("ps")]
        if psr:
            writes = list(writes) + psr
            reads = [r for r in reads if not r.startswith("ps")]
        deps = self._deps(reads, writes)
        self._wait(e, deps, skip_own=(e == "pe"))
        ins = fn()
        if inc:
            self.cnt[e] += 1
            ins.then_inc(self.sem[e], 1)
            ev = (self.sem[e], self.cnt[e])
        else:
            ev = (self.sem[e], self.cnt[e] + 1)
        self._record(ev, reads, writes)
        return ins

    def dma(self, q, out, in_, reads=(), writes=(), is_output=False):
        deps = self._deps(reads, writes)
        self._wait(q, deps)
        half = len(self.dsem) // 2
        if q == "pool":
            i = half + self.dnext_pool
            self.dnext_pool = (self.dnext_pool + 1) % half
        else:
            i = self.dnext
            self.dnext = (self.dnext + 1) % half
        s = self.dsem[i]
        if self.dcnt[i] > 0:
            self._wait(q, {id(s): (s, self.dcnt[i])})
        self.dcnt[i] += 16
        self.eng[q].dma_start(out=out, in_=in_).then_inc(s, 16)
        ev = (s, self.dcnt[i])
        self._record(ev, reads, writes)
        if is_output:
            self.out_events.append(ev)
        return ev

    def barrier(self):
        deps = {}
        for e in ("pe", "act", "dve", "pool"):
            if self.cnt[e] > 0:
                deps[id(self.sem[e])] = (self.sem[e], self.cnt[e])
        for i, s in enumerate(self.dsem):
            if self.dcnt[i] > 0:
                deps[id(s)] = (s, self.dcnt[i])
        for e in self.eng:
            self._wait(e, deps)

    def finish(self):
        deps = {}
        for (s, v) in self.out_events:
            k = id(s)
            if k not in deps or deps[k][1] < v:
                deps[k] = (s, v)
        self._wait("sp", deps)


class _Stop(Exception):
    pass


def build_nc(stage=99):
    nc = bass.Bass("TRN2", target_bir_lowering=False)

    def din(name, shape, dt=F32):
        return nc.dram_tensor(name, list(shape), dt, kind="ExternalInput").ap()

    xw = din("xw", [1026, D])
    xo = din("xo", [1280, D])
    cs = din("cs", [128, KC, 2])
    w_ada = din("w_ada", [D, 6 * D])
    b_ada = din("b_ada", [1, 6 * D])
    w_in = din("w_in", [D, 3904])
    w_krs = din("w_krs", [D, 64])
    w_q_b = din("w_q_b", [512, 1536])
    w_q_sw = din("w_q_sw", [512, 512])
    w_kv_b = din("w_kv_b", [256, 2048])
    w_out = din("w_out", [D, D])
    w_up = din("w_up", [D, 2 * FFN])
    w_down = din("w_down", [FFN, D])
    vec = din("vec", [128, 512])
    gfin = din("gfin", [128, D])
    ropew = din("ropew", [64, 2, BW])
    ropeo = din("ropeo", [64, 2, T])
    ident_d = din("ident", [128, 128], BF16)
    identf_d = din("identf", [128, 128])
    y = nc.dram_tensor("y", [T, D], F32, kind="ExternalOutput").ap()
    dbg = None
    if stage != 99:
        dbg = nc.dram_tensor("dbg", [128, 16384], F32, kind="ExternalOutput").ap()

    w_ada_v = w_ada.rearrange("(k p) n -> p k n", p=128)
    w_in_v = w_in.rearrange("(k p) n -> p k n", p=128)
    w_krs_v = w_krs.rearrange("(k p) n -> p k n", p=128)
    w_q_b_v = w_q_b.rearrange("(k p) n -> p k n", p=128)
    w_q_sw_v = w_q_sw.rearrange("(k p) n -> p k n", p=128)
    w_kv_b_v = w_kv_b.rearrange("(k p) n -> p k n", p=128)
    w_out_v = w_out.rearrange("(k p) n -> p k n", p=128)
    w_up_v = w_up.rearrange("(k p) n -> p k n", p=128)
    w_down_v = w_down.rearrange("(j p) n -> p j n", p=128)

    with ExitStack() as st:
      S = Sched(nc, st)
      try:

            def sb(name, shape, dt):
                return st.enter_context(nc.sbuf_tensor(name, list(shape), dt))

            def ps_dump(b):
                S.barrier()
                cp("dve", gt_bc[:, 512 * (b % 4):512 * (b % 4) + 512], ps[:, b, :], [], [])
                return gt_bc[:, 512 * (b % 4):512 * (b % 4) + 512]

            def checkpoint(k, items):
                if stage != k:
                    return
                if callable(items):
                    items = items()
                S.barrier()
                for ap, col0 in items:
                    S.dma("sp", dbg[0:ap.shape[0], col0:col0 + ap.shape[1]], ap, is_output=True)
                raise _Stop()

            BIG = sb("BIG", [128, 16384], F32)
            P1 = sb("P1", [128, KC * BW], BF16)
            P2 = sb("P2", [128, KC * BW], BF16)
            WS = sb("WS", [128, 6, 4096], BF16)
            gt_bc = sb("gt_bc", [128, D], F32)
            modrow_t = sb("modrow", [128, 2064], F32)
            modrow = modrow_t[0:3, 0:D]
            rawb = [modrow_t[:, r * 1032:(r + 1) * 1032].rearrange("p (a n) -> p a n", a=3) for r in range(2)]
            ident = sb("identb", [128, 128], BF16)
            identf = sb("identf32", [128, 128], F32)
            ones128 = sb("ones128", [128, 128], F32)
            ones256 = sb("ones256", [128, 128], F32)
            ones512 = sb("ones512", [128, 128], F32)
            ones1 = sb("ones1", [128, 128], F32)
            onesb = sb("onesb", [128, 128], BF16)
            vecs = sb("vecs", [128, 512], F32)
            sel3 = sb("sel3", [3, 128], F32)
            rhs3 = sb("rhs3", [3, 2], F32)
            epst = sb("epst", [128, 1], F32)
            cst = sb("cst", [128, KC, 2], F32)
            sT = sb("sT", [128, KC, 2], BF16)
            modT = sb("modT", [128, KC, 2], F32)
            AB = sb("AB", [128, 6, KC], F32)
            gtfT = sb("gtfT", [128, KC], F32)
            ssq = sb("ssq", [128, 16], F32)
            rs = sb("rs", [128, 16], F32)
            ost = sb("ost", [128, 4, 9], F32)
            rsc = sb("rsc", [64, 2, 342], F32)
            tmpb = sb("tmpb", [128, 2, 512], F32)
            ps = st.enter_context(nc.psum_tensor("ps", [128, 8, 512], F32))

            def carve(arena, col0, parts, shape, dt):
                n = 1
                for s_ in shape:
                    n *= s_
                nf = n if dt == F32 else n // 2
                v = arena[0:parts, col0:col0 + nf]
                if dt != F32:
                    v = v.bitcast(dt)
                if len(shape) == 2:
                    v = v.rearrange("p (a b) -> p a b", a=shape[0])
                elif len(shape) == 3:
                    v = v.rearrange("p (a b c) -> p a b c", a=shape[0], b=shape[1])
                return v

            P1f = P1[:].bitcast(F32)
            P2f = P2[:].bitcast(F32)
            hT = P1[:].rearrange("p (c n) -> p c n", c=KC)
            mixT = P2[:].rearrange("p (c n) -> p c n", c=KC)
            XN = carve(P2f, 0, 128, [4, D], BF16)
            kvnT = carve(BIG, 0, 128, [2, NKEY], BF16)
            krT = carve(BIG, 2304, 64, [NKEY], BF16)
            krT128 = carve(BIG, 2304, 128, [NKEY], BF16)
            cosw = BIG[0:64, 3456:3456 + BW]
            sinw = BIG[0:64, 4484:4484 + BW]
            coso = BIG[0:64, 5512:5512 + T]
            sino = BIG[0:64, 6536:6536 + T]
            qnT = carve(BIG, 7560, 128, [4, BW], BF16)
            XT = carve(BIG, 9616, 128, [2, D], F32)
            PT = carve(BIG, 13712, 128, [4, 384], BF16)
            osb = carve(BIG, 14480, 128, [9, 129], F32)
            mixn = carve(BIG, 15641, 128, [9, 128], BF16)
            x1 = BIG[:].rearrange("p (t n) -> p t n", t=8)
            scr = [P2f[:, i * BW:(i + 1) * BW] for i in range(4)] + \
                  [BIG[:, 9616 + i * BW:9616 + (i + 1) * BW] for i in range(3)]
            x1h = P1f[0:1, 6176:6176 + D]
            xh = P1f[0:1, 4000:4000 + D]
            Kh = [carve(P1f, 0 + i * 1152, 128, [NKEY], BF16) for i in range(2)]
            Vh = [carve(P1f, 2304 + i * 1161, 128, [18, 129], BF16) for i in range(2)]
            qhn = [carve(P1f, 4626 + i * 514, 128, [BW], BF16) for i in range(2)]
            qhr = [carve(P1f, 5654 + i * 514, 64, [BW], BF16) for i in range(2)]
            qhr128 = [carve(P1f, 5654 + i * 514, 128, [BW], BF16) for i in range(2)]
            actT = carve(P2f, 0, 128, [8, T], BF16)
            cbuf = [P2f[:, 4096 + i * 1026:4096 + (i + 1) * 1026] for i in range(4)]
            wsf = WS[:].bitcast(F32)
            XNf = WS[:, 2:4, 0:D]
            scrB = [P2f[:, 4112 + i * BW:4112 + (i + 1) * BW] for i in range(4)]

            def ws_view(h0, nh, k, n):
                v = WS[:, h0:h0 + nh, :].rearrange("p h n -> p (h n)")[:, 0:k * n]
                return v.rearrange("p (k n) -> p k n", k=k)

            def wkeys(h0, nh):
                return ["ws%d" % i for i in range(h0, h0 + nh)]

            def mm(out, lhsT, rhs, start, stop, reads, writes, inc):
                S.op("pe", lambda: nc.tensor.matmul(out, lhsT=lhsT, rhs=rhs, start=start, stop=stop),
                     reads, writes, inc)

            def act(out, in_, func, reads, writes, **kw):
                S.op("act", lambda: nc.scalar.activation(out=out, in_=in_, func=func, **kw), reads, writes)

            def tt(e, out, in0, in1, op, reads, writes):
                eng = nc.vector if e == "dve" else nc.gpsimd
                S.op(e, lambda: eng.tensor_tensor(out=out, in0=in0, in1=in1, op=op), reads, writes)

            def stt(e, out, in0, scalar, in1, op0, op1, reads, writes):
                e = "dve"
                eng = nc.vector
                S.op(e, lambda: eng.scalar_tensor_tensor(out=out, in0=in0, scalar=scalar, in1=in1, op0=op0, op1=op1),
                     reads, writes)

            def ts(e, out, in0, s1, s2, op0, op1, reads, writes):
                eng = nc.vector if e == "dve" else nc.gpsimd
                if op1 is None:
                    S.op(e, lambda: eng.tensor_scalar(out=out, in0=in0, scalar1=s1, scalar2=None, op0=op0), reads, writes)
                else:
                    S.op(e, lambda: eng.tensor_scalar(out=out, in0=in0, scalar1=s1, scalar2=s2, op0=op0, op1=op1),
                         reads, writes)

            def cp(e, out, in_, reads, writes):
                if e == "act":
                    S.op("act", lambda: nc.scalar.copy(out=out, in_=in_), reads, writes)
                else:
                    eng = nc.vector if e == "dve" else nc.gpsimd
                    S.op(e, lambda: eng.tensor_copy(out=out, in_=in_), reads, writes)

            def memset(e, ap, val, writes):
                eng = nc.vector if e == "dve" else nc.gpsimd
                S.op(e, lambda: eng.memset(ap, val), (), writes)

            def rsqrt_from(out, in_, scale, reads, writes):
                np_ = out.shape[0]
                act(out, in_, AF.Ln, reads + ["epst"], writes, scale=scale, bias=epst[0:np_, :])
                act(out, out, AF.Exp, writes, writes, scale=-0.5)

            V_GMN, V_GFN, V_GQA, V_GKV, V_CW, V_CB, V_GMO, V_FCW, V_FCB = 0, 16, 32, 36, 38, 62, 70, 86, 344
            V_BADA = 430

            S.dma("sp", ident[:], ident_d, writes=["ident"])
            S.dma("sp", identf[:], identf_d, writes=["identf"])
            S.dma("sp", vecs[:], vec, writes=["vecs"])
            S.dma("sp", cst[:], cs, writes=["cst"])
            S.dma("sp", BIG[0:64, 3456:3456 + 2 * BW].rearrange("p (a n) -> p a n", a=2), ropew, writes=["rope"])
            S.dma("sp", BIG[0:64, 5512:5512 + 2 * T].rearrange("p (a n) -> p a n", a=2), ropeo, writes=["rope"])
            memset("dve", ones128[:], 1.0 / 128, ["ones"])
            memset("dve", ones256[:], 1.0 / 256, ["ones"])
            memset("dve", ones512[:], 1.0 / 512, ["ones"])
            memset("dve", ones1[:], 1.0, ["ones"])
            memset("dve", onesb[:], 1.0, ["ones"])
            memset("dve", epst[:], EPS, ["epst"])
            memset("dve", sel3[:], 1.0, ["sel3"])
            memset("dve", rhs3[:], 1.0, ["rhs3"])
            S.op("dve", lambda: nc.vector.tensor_copy(out=rhs3[0:2, 0:2], in_=identf[0:2, 0:2]), ["identf"], ["rhs3"])
            S.op("dve", lambda: nc.vector.tensor_copy(out=sel3[0:2, :], in_=identf[0:2, 0:1].to_broadcast([2, 128])),
                 ["identf"], ["sel3"])
            act(sT[:], cst[:], AF.Silu, ["cst"], ["sT"])
            memset("pool", krT128[64:128, :], 0.0, ["krT"])

            ws_rot = [0]

            ws_allowed = [[0, 2, 4]]

            def next_pair():
                al = ws_allowed[0]
                ws_rot[0] = (ws_rot[0] + 1) % len(al)
                return al[ws_rot[0]]

            ada_bank = [0]

            def ada_seg(seg):
                for blk in range(4):
                    h0 = next_pair()
                    wv = ws_view(h0, 2, KC, 512)
                    c0 = seg * D + blk * 512
                    S.dma("pool", wv, w_ada_v[:, :, c0:c0 + 512], writes=wkeys(h0, 2))
                    b = ada_bank[0]
                    ada_bank[0] ^= 1
                    for k in range(KC):
                        mm(ps[0:2, b, :], sT[:, k, :], wv[:, k, :], k == 0, k == KC - 1,
                           wkeys(h0, 2) + ["sT"], ["ps%d" % b], k == KC - 1)
                    cp("dve", modrow[0:2, blk * 512:(blk + 1) * 512], ps[0:2, b, :], ["ps%d" % b], ["modrow"])

            ada_tasks = [(seg_, hb_) for seg_ in (2, 3, 4, 5) for hb_ in range(8)]
            ada_pending = []
            ada_half = [0]
            ada_misc = [False]

            def ada_issue():
                if not ada_tasks:
                    return
                seg, hb = ada_tasks.pop(0)
                h = 0 + ada_half[0]
                ada_half[0] ^= 1
                wv = ws_view(h, 1, KC, 256)
                c0 = seg * D + hb * 256
                S.dma("pool", wv, w_ada_v[:, :, c0:c0 + 256], writes=wkeys(h, 1))
                ada_pending.append((seg, hb, wv, wkeys(h, 1)))

            def ada_step():
                if not ada_pending:
                    return
                seg, hb, wv, wk_ = ada_pending.pop(0)
                if hb == 0 and seg in (2, 5):
                    S.dma("sp", modrow[2:3, :], b_ada[0:1, seg * D:(seg + 1) * D], writes=["modrow"])
                b = misc_bank()
                for k in range(KC):
                    mm(ps[0:2, b, 0:256], sT[:, k, :], wv[:, k, :], k == 0, k == KC - 1,
                       wk_ + ["sT"], ["ps%d" % b], k == KC - 1)
                cp("dve", modrow[0:2, hb * 256:(hb + 1) * 256], ps[0:2, b, 0:256], ["ps%d" % b], ["modrow"])
                ada_issue()
                if hb == 7:
                    ada_misc[0] = True
                    if seg == 2:
                        ada_bc()
                    elif seg == 3:
                        ada_T(misc_bank(), 3)
                        cp("dve", AB[:, 5, :], modT[:, :, 0], ["modT"], ["AB"])
                    elif seg == 4:
                        ada_T(misc_bank(), 4)
                        ada_AB(V_GFN, 4, None)
                    ada_misc[0] = False

            def ada_T(bank=2, seg=0):
                pv = ps[:, bank, 0:32].rearrange("p (c n) -> p c n", c=KC)
                for c in range(KC):
                    mm(pv[:, c, :], modrow[0:2, c * 128:(c + 1) * 128], rhs3[0:2, :], True, True,
                       ["modrow", "rhs3"], ["ps%d" % bank], c == KC - 1)
                si = {0: 0, 1: 1, 3: 2, 4: 3}[seg]
                bcol = vecs[:, V_BADA + si * 16:V_BADA + si * 16 + 16]
                for j_ in range(2):
                    tt("dve", modT[:, :, j_], pv[:, :, j_], bcol, ALU.add, ["ps%d" % bank, "vecs"], ["modT"])

            def ada_bc():
                for i in range(4):
                    b = misc_bank() if ada_misc[0] else 2 + (i % 2)
                    mm(ps[:, b, :], sel3[:], modrow[0:3, i * 512:(i + 1) * 512], True, True,
                       ["modrow", "sel3"], ["ps%d" % b], True)
                    cp("dve", gt_bc[:, i * 512:(i + 1) * 512], ps[:, b, :], ["ps%d" % b], ["gt_bc"])

            def ada_AB(gcol, ia, ictx):
                ts("dve", AB[:, ia, :], modT[:, :, 0], 1.0, None, ALU.add, None, ["modT"], ["AB"])
                tt("dve", AB[:, ia, :], AB[:, ia, :], vecs[:, gcol:gcol + KC], ALU.mult, ["AB", "vecs"], ["AB"])
                if ictx is not None:
                    ts("dve", AB[:, ictx, :], modT[:, :, 1], 1.0, None, ALU.add, None, ["modT"], ["AB"])
                    tt("dve", AB[:, ictx, :], AB[:, ictx, :], vecs[:, gcol:gcol + KC], ALU.mult, ["AB", "vecs"], ["AB"])

            ada_seg(0)
            ada_T(2, 0)
            cp("dve", AB[:, 1, :], modT[:, :, 0], ["modT"], ["AB"])
            cp("dve", AB[:, 3, :], modT[:, :, 1], ["modT"], ["AB"])
            ada_seg(1)
            ada_T(2, 1)
            ada_AB(V_GMN, 0, 2)
            checkpoint(1, [(AB[:].rearrange("p a b -> p (a b)"), 0), (modT[:].rearrange("p a b -> p (a b)"), 96)])

            tp_rot = [0]
            tp_banks = [[5, 6, 7]]

            def next_tp():
                al = tp_banks[0]
                tp_rot[0] = (tp_rot[0] + 1) % len(al)
                return al[tp_rot[0]]

            def norm_to_T(tiles, iA, iB, dst_fn, dst_keys_fn, gcol_keys=()):
                n = len(tiles)
                rows = tiles[0][1]
                for i, (ap, r, keys) in enumerate(tiles):
                    act(XN[0:r, i, :], ap, AF.Square, keys, ["xn%d" % i, "ssq"], accum_out=ssq[0:r, i:i + 1])
                rsqrt_from(rs[0:rows, 0:n], ssq[0:rows, 0:n], 1.0 / D, ["ssq"], ["rs"])
                for i, (ap, r, keys) in enumerate(tiles):
                    act(XN[0:r, i, :], ap, AF.Copy, keys + ["rs"], ["xn%d" % i], scale=rs[0:r, i:i + 1])
                ncols = sum(t[1] for t in tiles)
                for c in range(KC):
                    bank = next_tp()
                    q = bank
                    tpv = ps[:, bank, :].bitcast(BF16)[:, 0:512]
                    col = 0
                    for i, (ap, r, keys) in enumerate(tiles):
                        S.op("pe", lambda i=i, r=r, col=col: nc.tensor.transpose(
                            tpv[:, col:col + r], XN[0:r, i, c * 128:(c + 1) * 128], ident[0:r, 0:r]),
                            ["xn%d" % i, "ident"], ["ps%d" % q], i == n - 1)
                        col += r
                    dst = dst_fn(c)
                    if c % 2 == 0:
                        act(dst, tpv[:, 0:ncols], AF.Identity, ["ps%d" % q, "AB"], dst_keys_fn(c),
                            scale=AB[:, iA, c:c + 1], bias=AB[:, iB, c:c + 1])
                    else:
                        ts("dve", dst, tpv[:, 0:ncols], AB[:, iA, c:c + 1], AB[:, iB, c:c + 1], ALU.mult, ALU.add,
                           ["ps%d" % q, "AB"], dst_keys_fn(c))

            def kv_project(hsrc, hkeys, pieces, wkv, wk, key_cols, rope, scr_sq, scr_rstd, scr_t):
                outs = [("kva0", 0, 128), ("kva1", 128, 128), ("krA", 256, 64), ("krB", 320, 64)]
                if rope is None:
                    outs = outs[:3]
                np_ = len(pieces)
                banks = {}
                for oi, (nm, wc, M) in enumerate(outs):
                    for pi, (c0, n) in enumerate(pieces):
                        b = (oi * np_ + pi)
                        banks[(nm, pi)] = b
                        for k in range(KC):
                            mm(ps[0:M, b, 0:n], wkv[:, k, wc:wc + M], hsrc(k, c0, n), k == 0, k == KC - 1,
                               wk + hkeys(k), ["ps%d" % b], k == KC - 1)
                return banks

            h0kv = 0
            wkv = ws_view(0, 2, KC, 384)
            ws_allowed[0] = [2, 4]
            ws_rot[0] = 0
            S.dma("pool", wkv[:, :, 0:320], w_in_v[:, :, 512:832], writes=wkeys(0, 2))
            S.dma("pool", wkv[:, :, 320:384], w_krs_v, writes=wkeys(0, 2))
            WKV = wkeys(0, 2)

            xt_rot = [0]

            def load_x(src_ap, rows):
                i = xt_rot[0]
                xt_rot[0] ^= 1
                S.dma("sp", XT[0:rows, i, :], src_ap, writes=["xt%d" % i])
                return (XT[0:rows, i, :], rows, ["xt%d" % i])

            sq_s = scr[2]
            for g in range(3):
                ntile = 4 if g < 2 else 2
                ncols = ntile * 128
                iA, iB = (0, 1) if g < 2 else (2, 3)
                tiles = []
                n = ntile
                for i in range(n):
                    tl = load_x(xo[(g * 4 + i) * 128:(g * 4 + i + 1) * 128, :], 128)
                    tiles.append(tl)
                    S.op("dve", lambda i=i, tl=tl: nc.vector.scalar_tensor_tensor(
                        out=XN[:, i, :], in0=tl[0], scalar=1.0, in1=tl[0], op0=ALU.mult, op1=ALU.mult,
                        accum_out=ssq[:, i:i + 1]), tl[2], ["xn%d" % i, "ssq"])
                    rsqrt_from(rs[:, i:i + 1], ssq[:, i:i + 1], 1.0 / D, ["ssq"], ["rs"])
                    act(XN[:, i, :], tl[0], AF.Copy, tl[2] + ["rs"], ["xn%d" % i], scale=rs[:, i:i + 1])
                for c in range(KC):
                    bank = next_tp()
                    q = bank
                    tpv = ps[:, bank, :].bitcast(BF16)[:, 0:512]
                    for i in range(n):
                        S.op("pe", lambda i=i, c=c, tpv=tpv: nc.tensor.transpose(
                            tpv[:, i * 128:(i + 1) * 128], XN[:, i, c * 128:(c + 1) * 128], ident[:]),
                            ["xn%d" % i, "ident"], ["ps%d" % q], i == n - 1)
                    ho = (g % 2) * 512
                    hk = "hTB%d_%d" % (c, g % 2)
                    dst = hT[:, c, ho:ho + ncols]
                    if c % 2 == 0:
                        act(dst, tpv[:, 0:ncols], AF.Identity, ["ps%d" % q, "AB"], [hk],
                            scale=AB[:, iA, c:c + 1], bias=AB[:, iB, c:c + 1])
                    else:
                        ts("dve", dst, tpv[:, 0:ncols], AB[:, iA, c:c + 1], AB[:, iB, c:c + 1], ALU.mult, ALU.add,
                           ["ps%d" % q, "AB"], [hk])
                if g == 0:
                    checkpoint(21, [(P1f, 0)])
                rope = g < 2
                outs = [(0, 128, 0), (128, 128, 1), (256, 64, 2)] + ([(320, 64, 3)] if rope else [])
                for (wc, M, b) in outs:
                    for k in range(KC):
                        mm(ps[0:M, b, 0:ncols], wkv[:, k, wc:wc + M], hT[:, k, ho:ho + ncols], k == 0, k == KC - 1,
                           WKV + ["hTB%d_%d" % (k, g % 2)], ["ps%d" % b], k == KC - 1)
                if g == 0:
                    checkpoint(22, lambda: [(ps_dump(0), 0), (ps_dump(1), 512), (ps_dump(2), 1024), (ps_dump(3), 1536)])
                kc0 = g * 512
                for c in range(2):
                    sqb = scrB[c]
                    act(sqb[:, 0:ncols], ps[:, c, 0:ncols], AF.Square, ["ps%d" % c], ["scrB%d" % c])
                    mm(ps[:, 4, 0:ncols], ones256[:], sqb[:, 0:ncols], c == 0, c == 1, ["scrB%d" % c, "ones"], ["ps4"], c == 1)
                rsqrt_from(scrB[2][:, 0:ncols], ps[:, 4, 0:ncols], 1.0, ["ps4"], ["scrB2"])
                for c in range(2):
                    stt("dve", kvnT[:, c, kc0:kc0 + ncols], ps[:, c, 0:ncols], vecs[:, V_GKV + c:V_GKV + c + 1],
                        scrB[2][:, 0:ncols], ALU.mult, ALU.mult, ["ps%d" % c, "scrB2", "vecs"], ["kvnT"])
                if g == 0:
                    checkpoint(23, [(BIG[:, 0:2304], 0), (scrB[2], 2304)])
                if rope:
                    t1 = scrB[3][0:64, 0:ncols]
                    t2 = scrB[0][0:64, 0:ncols]
                    tt("dve", t1, ps[0:64, 2, 0:ncols], coso[:, g * 512:g * 512 + ncols], ALU.mult,
                       ["ps2", "rope"], ["scrB3"])
                    tt("dve", t2, ps[0:64, 3, 0:ncols], sino[:, g * 512:g * 512 + ncols], ALU.mult,
                       ["ps3", "rope"], ["scrB0"])
                    tt("pool", krT[:, kc0:kc0 + ncols], t1, t2, ALU.add, ["scrB3", "scrB0"], ["krT"])
                else:
                    cp("act", krT[:, kc0:kc0 + ncols], ps[0:64, 2, 0:ncols], ["ps2"], ["krT"])

            checkpoint(2, [(BIG[:, 0:2304], 0), (BIG[0:64, 2304:3456], 2304)])
            tp_banks[0] = [4, 5, 6, 7]
            wq = ws_view(2, 2, KC, 512)
            S.dma("pool", wq, w_in_v[:, :, 0:512], writes=wkeys(2, 2))
            WQ = wkeys(2, 2)
            conv_w_q = {}
            for which_ in range(2):
                h_ = (4, 5)[which_]
                base_ = (1856, 2880)[which_]
                wv__ = ws_view(h_, 1, KC, 256)
                S.dma("pool", wv__, w_in_v[:, :, base_:base_ + 256], writes=wkeys(h_, 1))
                conv_w_q[(0, which_)] = (wv__, wkeys(h_, 1))
            for c in range(KC):
                memset("pool", hT[:, c, 0:1], 0.0, ["hT%d" % c, "hTB%d_0" % c, "hTB%d_1" % c])
                memset("pool", hT[:, c, 1027:1028], 0.0, ["hT%d" % c])
            for g in range(3):
                if g < 2:
                    tiles = []
                    for i in range(4):
                        t_ = g * 4 + i
                        tl = load_x(xw[t_ * 128:(t_ + 1) * 128, :], 128)
                        tiles.append(tl)
                        S.op("dve", lambda i=i, tl=tl: nc.vector.scalar_tensor_tensor(
                        out=XN[:, i, :], in0=tl[0], scalar=1.0, in1=tl[0], op0=ALU.mult, op1=ALU.mult,
                        accum_out=ssq[:, i:i + 1]), tl[2], ["xn%d" % i, "ssq"])
                        rsqrt_from(rs[:, i:i + 1], ssq[:, i:i + 1], 1.0 / D, ["ssq"], ["rs"])
                        act(XN[:, i, :], tl[0], AF.Copy, tl[2] + ["rs"], ["xn%d" % i], scale=rs[:, i:i + 1])
                    rws = [128] * 4
                else:
                    tl = load_x(xw[1024:1026, :], 2)
                    act(XN[0:2, 0, :], tl[0], AF.Square, tl[2], ["xn0", "ssq"], accum_out=ssq[0:2, 0:1])
                    rsqrt_from(rs[0:2, 0:1], ssq[0:2, 0:1], 1.0 / D, ["ssq"], ["rs"])
                    act(XN[0:2, 0, :], tl[0], AF.Copy, tl[2] + ["rs"], ["xn0"], scale=rs[0:2, 0:1])
                    rws = [2]
                ncols = sum(rws)
                col0 = 1 + g * 512
                for c in range(KC):
                    bank = next_tp()
                    q = bank
                    tpv = ps[:, bank, :].bitcast(BF16)[:, 0:512]
                    for i, r in enumerate(rws):
                        S.op("pe", lambda i=i, c=c, r=r, tpv=tpv: nc.tensor.transpose(
                            tpv[:, i * 128:i * 128 + r], XN[0:r, i, c * 128:(c + 1) * 128], ident[0:r, 0:r]),
                            ["xn%d" % i, "ident"], ["ps%d" % q], i == len(rws) - 1)
                    dst = hT[:, c, col0:col0 + ncols]
                    hks = ["hT%d" % c, "hTB%d_0" % c, "hTB%d_1" % c]
                    if c % 2 == 0:
                        act(dst, tpv[:, 0:ncols], AF.Identity, ["ps%d" % q, "AB"], hks,
                            scale=AB[:, 0, c:c + 1], bias=AB[:, 1, c:c + 1])
                    else:
                        ts("dve", dst, tpv[:, 0:ncols], AB[:, 0, c:c + 1], AB[:, 1, c:c + 1], ALU.mult, ALU.add,
                           ["ps%d" % q, "AB"], hks)

            checkpoint(31, [(P1f, 0)])
            HK = ["hT%d" % k for k in range(KC)]
            ob_rot = [0]

            def proj3(wv, wc, M, wk):
                o = ob_rot[0]
                ob_rot[0] ^= 1
                keys = ["ps%d" % (o * 3 + i) for i in range(3)]
                for pi, (c0, n) in enumerate(PC):
                    b = o * 3 + pi
                    for k in range(KC):
                        mm(ps[0:M, b, 0:n], wv[:, k, wc:wc + M], hT[:, k, c0:c0 + n], k == 0, k == KC - 1,
                           wk + ["hT%d" % k], ["ps%d" % b], k == KC - 1)
                return ps[0:M, o * 3:o * 3 + 3, 0:342], keys

            def v3(ap2d):
                return ap2d.rearrange("p (a n) -> p a n", a=3)

            st_rot = [0]

            def stats3(srcs, ones_m, dst_rstd, dst_key, sq_scr, sq_key):
                for pi in range(3):
                    b = 6 + st_rot[0]
                    st_rot[0] ^= 1
                    for si, (pv, keys) in enumerate(srcs):
                        act(sq_scr[si][:, pi * 342:(pi + 1) * 342], pv[:, pi, :], AF.Square, [keys[pi]], [sq_key[si]])
                        mm(ps[:, b, 0:342], ones_m[:], sq_scr[si][:, pi * 342:(pi + 1) * 342], si == 0, si == len(srcs) - 1,
                           [sq_key[si], "ones"], ["ps%d" % b], si == len(srcs) - 1)
                    rsqrt_from(dst_rstd[:, pi * 342:(pi + 1) * 342], ps[:, b, 0:342], 1.0, ["ps%d" % b], [dst_key])

            sqq = scr[4][:, 0:1026]
            for c in range(4):
                ob_rot[0] = 0
                pv, keys = proj3(wq, c * 128, 128, WQ)
                ts("dve", v3(qnT[:, c, 1:1027]), pv, vecs[:, V_GQA + c:V_GQA + c + 1], None, ALU.mult, None,
                   keys + ["vecs"], ["qnT"])
                act(v3(sqq), pv, AF.Square, keys, ["scr4"])
                for pi in range(3):
                    mm(ps[:, 3 + pi, 0:342], ones512[:], sqq[:, pi * 342:(pi + 1) * 342], c == 0, c == 3,
                       ["scr4", "ones"], ["ps%d" % (3 + pi)], c == 3)
            rq = scr[5][:, 0:1026]
            for pi in range(3):
                rsqrt_from(rq[:, pi * 342:(pi + 1) * 342], ps[:, 3 + pi, 0:342], 1.0, ["ps%d" % (3 + pi)], ["scr5"])
            for c in range(4):
                tt("pool" if c % 2 else "dve", qnT[:, c, 1:1027], qnT[:, c, 1:1027], rq, ALU.mult, ["qnT", "scr5"], ["qnT"])
            ob_rot[0] = 0

            def load_conv(gp, which, h):
                base = (1856, 2880, 832)[which]
                wv_ = ws_view(h, 1, KC, 256)
                S.dma("pool", wv_, w_in_v[:, :, base + gp * 256:base + gp * 256 + 256], writes=wkeys(h, 1))
                conv_w_q[(gp, which)] = (wv_, wkeys(h, 1))

            CONV_SLOTS = {0: (4, 5, 2), 1: (3, 0, 1), 2: (4, 5, 2), 3: (3, 0, 1)}
            load_conv(0, 2, 2)

            pk0, k0 = proj3(wkv, 0, 128, WKV)
            pk1, k1 = proj3(wkv, 128, 128, WKV)
            stats3([(pk0, k0), (pk1, k1)], ones256, scr[5][:, 0:1026], "scr5",
                   [scr[4][:, 0:1026], scr[6][:, 0:1026]], ["scr4", "scr6"])
            tmpk = scr[6][:, 0:1026]
            for c, (pv, keys) in enumerate([(pk0, k0), (pk1, k1)]):
                stt("dve", v3(tmpk), pv, vecs[:, V_GKV + c:V_GKV + c + 1], v3(scr[5][:, 0:1026]), ALU.mult, ALU.mult,
                    keys + ["scr5", "vecs"], ["scr6"])
                cp("pool", kvnT[:, c, 1280:2304], tmpk[:, 0:1024], ["scr6"], ["kvnT"])
            pA, kA = proj3(wkv, 256, 64, WKV)
            pB, kB = proj3(wkv, 320, 64, WKV)
            t1 = scr[4][0:64, 0:1026]
            t2 = scr[6][0:64, 0:1026]
            tt("dve", v3(t1), pA, v3(cosw[:, 1:1027]), ALU.mult, kA + ["rope"], ["scr4"])
            tt("dve", v3(t2), pB, v3(sinw[:, 1:1027]), ALU.mult, kB + ["rope"], ["scr6"])
            tt("pool", krT[:, 1280:2304], t1[:, 0:1024], t2[:, 0:1024], ALU.add, ["scr4", "scr6"], ["krT"])

            scr7 = BIG[:, 13712:13712 + BW]
            scr8 = BIG[:, 14740:14740 + BW]
            pb = scr[1]
            memset("pool", pb[:, 0:1], 0.0, ["scr1"])
            memset("pool", pb[:, 1027:1028], 0.0, ["scr1"])

            def conv_fin(g):
                co, cok = ((scr[3], "scr3"), (scr7, "scr7"))[g % 2]
                sqc, sqk = ((scr[4], "scr4"), (scr8, "scr8"))[g % 2]
                co = co[:, 0:1026]
                sqc = sqc[:, 0:1026]
                rc = scr[5][:, 0:1026]
                for pi in range(3):
                    b = 6 + st_rot[0]
                    st_rot[0] ^= 1
                    mm(ps[:, b, 0:342], ones128[:], sqc[:, pi * 342:(pi + 1) * 342], True, True,
                       [sqk, "ones"], ["ps%d" % b], True)
                    rsqrt_from(rc[:, pi * 342:(pi + 1) * 342], ps[:, b, 0:342], 1.0, ["ps%d" % b], ["scr5"])
                stt("dve", mixT[:, 8 + g, 1:1027], co, vecs[:, V_GMO + 8 + g:V_GMO + 9 + g], rc, ALU.mult, ALU.mult,
                    [cok, "scr5", "vecs"], ["mixT%d" % (8 + g)])

            for gp in range(4):
                if gp + 1 < 4:
                    for which in range(3):
                        load_conv(gp + 1, which, CONV_SLOTS[gp + 1][which])
                else:
                    S.dma("pool", ws_view(4, 2, 4, 1536), w_q_b_v, writes=wkeys(4, 2))
                    S.dma("pool", ws_view(2, 1, 4, 512), w_q_sw_v, writes=wkeys(2, 1))
                wts = [conv_w_q[(gp, w_)] for w_ in range(3)]
                for gl in range(2):
                    g = gp * 2 + gl
                    pcc, kcc = proj3(wts[0][0], gl * 128, 128, wts[0][1])
                    cp("act", v3(scr[0][:, 0:1026]), pcc, kcc, ["scr0"])
                    pch, kch = proj3(wts[1][0], gl * 128, 128, wts[1][1])
                    tt("dve", v3(pb[:, 1:1027]), pch, v3(scr[0][:, 0:1026]), ALU.mult, kch + ["scr0"], ["scr1"])
                    tb = scr[2][:, 0:1026]
                    cwc = V_CW + g * 3
                    act(tb, pb[:, 1:1027], AF.Identity, ["scr1", "vecs"], ["scr2"],
                        scale=vecs[:, cwc + 1:cwc + 2], bias=vecs[:, V_CB + g:V_CB + g + 1])
                    stt("dve", tb, pb[:, 0:1026], vecs[:, cwc:cwc + 1], tb, ALU.mult, ALU.add, ["scr1", "scr2", "vecs"], ["scr2"])
                    stt("dve", tb, pb[:, 2:1028], vecs[:, cwc + 2:cwc + 3], tb, ALU.mult, ALU.add, ["scr1", "scr2", "vecs"], ["scr2"])
                    pcb, kcb = proj3(wts[2][0], gl * 128, 128, wts[2][1])
                    co, cok = ((scr[3], "scr3"), (scr7, "scr7"))[g % 2]
                    sqc, sqk = ((scr[4], "scr4"), (scr8, "scr8"))[g % 2]
                    tt("dve", v3(co[:, 0:1026]), pcb, v3(tb), ALU.mult, kcb + ["scr2"], [cok])
                    act(sqc[:, 0:1026], co[:, 0:1026], AF.Square, [cok], [sqk])
                    if g > 0:
                        conv_fin(g - 1)
            conv_fin(7)
            S.dma("pool", ws_view(3, 1, 2, 2048), w_kv_b_v, writes=wkeys(3, 1))


            checkpoint(3, [(P2f, 0), (BIG[:, 7560:9616], 8224), (BIG[:, 0:2304], 10280), (BIG[0:64, 2304:3456], 12584)])
            S.barrier()
            wqb = ws_view(4, 2, 4, 1536)
            wqsw = ws_view(2, 1, 4, 512)
            wkvb = ws_view(3, 1, 2, 2048)
            WQB, WQS, WKB = wkeys(4, 2), wkeys(2, 1), wkeys(3, 1)
            tp_banks[0] = [6, 7]
            memset("dve", osb[:], 1.0, ["osb"])
            for i in range(2):
                memset("pool", qhr128[i][64:128, :], 0.0, ["qhr%d" % i])

            def fixed_pair():
                return 4
            mb = [6]

            def misc_bank():
                b = mb[0]
                mb[0] = 13 - b
                return b
            wout_q = {}

            def load_wout(cb, h0):
                wv_ = ws_view(h0, 2, KC, 512)
                S.dma("pool", wv_, w_out_v[:, :, cb * 512:(cb + 1) * 512], writes=wkeys(h0, 2))
                wout_q[cb] = (wv_, h0)

            sb_rot = [0]
            pt_rot = [0]
            acc_rot = [0]
            KP = [(0, 512), (512, 512), (1024, 512), (1536, 512), (2048, 256)]
            eb_rot = [0]
            EXPB = [0, 1, 2, 6, 7]

            def exp_bank():
                eb_rot[0] = (eb_rot[0] + 1) % len(EXPB)
                return EXPB[eb_rot[0]]

            def att_expand(h, bank_fn):
                i = h % 2
                for pi, (c0, n) in enumerate(KP):
                    b = bank_fn()
                    for kc in range(2):
                        mm(ps[:, b, 0:n], wkvb[:, kc, h * 256:h * 256 + 128], kvnT[:, kc, c0:c0 + n], kc == 0, kc == 1,
                           WKB + ["kvnT"], ["ps%d" % b], kc == 1)
                    cp("dve", Kh[i][:, c0:c0 + n], ps[:, b, 0:n], ["ps%d" % b], ["Kh%d" % i])
                    yield
                for k0_ in range(0, 18, 4):
                    nk = min(4, 18 - k0_)
                    b = bank_fn()
                    for kk in range(nk):
                        kt = k0_ + kk
                        for kc in range(2):
                            mm(ps[:, b, kk * 128:(kk + 1) * 128], kvnT[:, kc, kt * 128:(kt + 1) * 128],
                               wkvb[:, kc, h * 256 + 128:h * 256 + 256], kc == 0, kc == 1,
                               WKB + ["kvnT"], ["ps%d" % b], kk == nk - 1 and kc == 1)
                    cp("dve" if (k0_ // 4) % 2 else "act", Vh[i][:, k0_:k0_ + nk, 0:128],
                       ps[:, b, 0:nk * 128].rearrange("p (a n) -> p a n", a=nk), ["ps%d" % b], ["Vh%d" % i])
                    yield
                for pi, (c0, n) in enumerate(PC):
                    b = bank_fn()
                    for kc in range(4):
                        mm(ps[:, b, 0:n], wqb[:, kc, h * 192:h * 192 + 128], qnT[:, kc, c0:c0 + n], kc == 0, kc == 3,
                           WQB + ["qnT"], ["ps%d" % b], kc == 3)
                    cp("dve", qhn[i][:, c0:c0 + n], ps[:, b, 0:n], ["ps%d" % b], ["qhn%d" % i])
                    yield
                    bA = bank_fn()
                    for kc in range(4):
                        mm(ps[0:64, bA, 0:n], wqb[:, kc, h * 192 + 128:h * 192 + 192], qnT[:, kc, c0:c0 + n], kc == 0, kc == 3,
                           WQB + ["qnT"], ["ps%d" % bA], kc == 3)
                    tt("dve", rsc[:, 0, 0:n], ps[0:64, bA, 0:n], cosw[:, c0:c0 + n], ALU.mult, ["ps%d" % bA, "rope"], ["rsc0"])
                    yield
                    bB = bank_fn()
                    for kc in range(4):
                        mm(ps[0:64, bB, 0:n], wqsw[:, kc, h * 64:h * 64 + 64], qnT[:, kc, c0:c0 + n], kc == 0, kc == 3,
                           WQS + ["qnT"], ["ps%d" % bB], kc == 3)
                    tt("dve", rsc[:, 1, 0:n], ps[0:64, bB, 0:n], sinw[:, c0:c0 + n], ALU.mult, ["ps%d" % bB, "rope"], ["rsc1"])
                    tt("pool", qhr[i][:, c0:c0 + n], rsc[:, 0, 0:n], rsc[:, 1, 0:n], ALU.add, ["rsc0", "rsc1"], ["qhr%d" % i])
                    yield

            FB = [BIG[:, 9616 + k_ * 342:9616 + (k_ + 1) * 342] for k_ in range(4)]
            DACC = [BIG[:, 9616 + 1368 + k_ * 342:9616 + 1368 + (k_ + 1) * 342] for k_ in range(4)]
            PT2 = carve(BIG, 13712, 128, [4, 342], BF16)
            LAG = 2
            pend = []
            deferred = []

            def fin_a(h, ab, q0, nq, dk):
                S.op("pe", lambda: nc.tensor.matmul(ps[:, 3, 0:nq], lhsT=ones1[:], rhs=DACC[dk], start=False, stop=True),
                     ["dacc%d" % dk, "ones"], ["ps3"], True)
                cp("dve", FB[2], ps[:, 3, 0:nq], ["ps3"], ["F2"])
                cp("dve", FB[0], ps[:, ab, 0:nq], ["ps%d" % ab], ["F0"])
                deferred.append((h, q0, nq))

            def fin_b(h, q0, nq):
                tt("dve", FB[1], FB[0], FB[0], ALU.mult, ["F0"], ["F1"])
                b1 = misc_bank()
                mm(ps[:, b1, 0:nq], ones128[:], FB[1], True, True, ["F1", "ones"], ["ps%d" % b1], True)
                stt("dve", FB[3], FB[2], EPS, FB[2], ALU.mult, ALU.mult, ["F2"], ["F3"])
                tt("dve", FB[3], ps[:, b1, 0:nq], FB[3], ALU.add, ["ps%d" % b1, "F3"], ["F3"])
                act(FB[3], FB[3], AF.Ln, ["F3"], ["F3"])
                act(FB[3], FB[3], AF.Exp, ["F3"], ["F3"], scale=-0.5)
                stt("dve", mixT[:, h, q0:q0 + nq], FB[0], vecs[:, V_GMO + h:V_GMO + h + 1], FB[3], ALU.mult, ALU.mult,
                    ["F0", "F3", "vecs"], ["mixT%d" % h])

            def pv_step(h, ab, q0, nq, kt, slot, dk):
                i = h % 2
                mm(ps[:, ab, 0:nq], Vh[i][:, kt, 0:128], PT2[:, slot, 0:nq], kt == 0, kt == 17,
                   ["PT%d" % slot, "Vh%d" % i], ["ps%d" % ab], True)
                if kt % 2 == 1:
                    mm(ps[:, 3, 0:nq], onesb[:], PT2[:, slot, 0:nq], kt == 1, False, ["PT%d" % slot, "ones"], ["ps3"], True)
                elif kt < 1:
                    cp("dve", DACC[dk], PT2[:, slot, 0:nq], ["PT%d" % slot], ["dacc%d" % dk])
                else:
                    tt("dve", DACC[dk], DACC[dk], PT2[:, slot, 0:nq], ALU.add, ["PT%d" % slot, "dacc%d" % dk], ["dacc%d" % dk])
                if kt == 17:
                    fin_a(h, ab, q0, nq, dk)

            ada_issue()
            ada_issue()
            for _ in att_expand(0, exp_bank):
                pass
            pass_ctr = [0]
            for h in range(8):
                i = h % 2
                gen = att_expand(h + 1, misc_bank) if h + 1 < 8 else None
                step_idx = 0
                for (q0, nq) in PC:
                    ab = 4 + acc_rot[0]
                    acc_rot[0] ^= 1
                    dk = pass_ctr[0] % 2
                    pass_ctr[0] += 1
                    for kt in range(18):
                        sbk = sb_rot[0]
                        sb_rot[0] = (sbk + 1) % 3
                        mm(ps[:, sbk, 0:nq], Kh[i][:, kt * 128:(kt + 1) * 128], qhn[i][:, q0:q0 + nq], True, False,
                           ["Kh%d" % i, "qhn%d" % i], ["ps%d" % sbk], False)
                        mm(ps[:, sbk, 0:nq], krT128[:, kt * 128:(kt + 1) * 128], qhr128[i][:, q0:q0 + nq], False, True,
                           ["krT", "qhr%d" % i], ["ps%d" % sbk], True)
                        slot = pt_rot[0]
                        pt_rot[0] = (slot + 1) % 4
                        act(PT2[:, slot, 0:nq], ps[:, sbk, 0:nq], AF.Exp, ["ps%d" % sbk], ["PT%d" % slot], scale=SCALE)
                        pend.append((h, ab, q0, nq, kt, slot, dk))
                        if len(pend) > LAG:
                            pv_step(*pend.pop(0))
                        if kt == 6 and deferred:
                            fin_b(*deferred.pop(0))
                        step_idx += 1
                        if gen is not None and step_idx >= 8 and step_idx % 2 == 0:
                            if next(gen, "done") == "done":
                                gen = None
                                if h + 1 == 7:
                                    load_wout(0, 4)
                                    load_wout(1, 2)
                    ada_step()
                    if q0 == 1:
                        ada_step()
                if gen is not None:
                    for _ in gen:
                        pass
                    if h + 1 == 7:
                        load_wout(0, 4)
                        load_wout(1, 2)
            while pend:
                pv_step(*pend.pop(0))
            while deferred:
                fin_b(*deferred.pop(0))
            while ada_pending:
                ada_step()

            checkpoint(4, [(P2f, 0)])
            S.barrier()
            ws_allowed[0] = [0, 2, 4]
            ws_rot[0] = 0
            for t in range(8):
                S.dma("sp", x1[:, t, :], xw[t * 128:(t + 1) * 128, :], writes=["x1_%d" % t])
            S.dma("sp", xh, xw[1024:1025, :], writes=["xh"])
            ob8 = [0]
            for cb in range(4):
                wv, h0 = wout_q[cb]
                for t in range(9):
                    M = 128 if t < 8 else 1
                    c0 = 1 + 128 * t if t < 8 else 1025
                    b = ob8[0]
                    ob8[0] = (b + 1) % 8
                    for k in range(KC):
                        mm(ps[0:M, b, :], mixT[:, k, c0:c0 + M], wv[:, k, :], k == 0, k == KC - 1,
                           wkeys(h0, 2) + ["mixT%d" % k], ["ps%d" % b], k == KC - 1)
                    tb_ = tmpb[0:M, b % 2, :]
                    tt("dve", tb_, ps[0:M, b, :], gt_bc[0:M, cb * 512:(cb + 1) * 512], ALU.mult,
                       ["ps%d" % b, "gt_bc"], ["tmpb%d" % (b % 2)])
                    if t < 8:
                        tt("pool", x1[:, t, cb * 512:(cb + 1) * 512], x1[:, t, cb * 512:(cb + 1) * 512], tb_, ALU.add,
                           ["tmpb%d" % (b % 2), "x1_%d" % t], ["x1_%d" % t])
                    else:
                        tt("pool", x1h[:, cb * 512:(cb + 1) * 512], xh[:, cb * 512:(cb + 1) * 512], tb_, ALU.add,
                           ["tmpb%d" % (b % 2), "xh"], ["x1h"])
                if cb + 2 < 4:
                    load_wout(cb + 2, h0)
            checkpoint(5, [(BIG[:], 0)])
            S.barrier()
            ada_bc()

            hs_rot = [0]

            def next_half4():
                h = hs_rot[0]
                hs_rot[0] = (h + 1) % 4
                return h
            wd_rot = [0]
            db_rot = [0]
            up_w = {}

            def load_up(u):
                j0 = u * 2
                nj = min(2, NJ - j0)
                res = []
                for base in (0, FFN):
                    h = next_half4()
                    wv = ws_view(h, 1, KC, 256)
                    S.dma("pool", wv[:, :, 0:nj * 128], w_up_v[:, :, base + j0 * 128:base + (j0 + nj) * 128], writes=wkeys(h, 1))
                    res.append((wv, wkeys(h, 1)))
                up_w[u] = res

            wdq = {}
            tm_rot = [0]

            def load_wd(grp, cb):
                G_ = min(8, NJ - grp * 8)
                h = 4 + wd_rot[0]
                wd_rot[0] ^= 1
                wd = ws_view(h, 1, 8, 512)
                S.dma("pool", wd[:, 0:G_, :], w_down_v[:, grp * 8:grp * 8 + G_, cb * 512:(cb + 1) * 512], writes=wkeys(h, 1))
                wdq[(grp, cb)] = (wd, wkeys(h, 1))

            load_up(0)
            tp_banks[0] = [4, 5, 6, 7]
            act(XN[0:1, 0, :], x1h, AF.Square, ["x1h"], ["xn0", "ssq"], accum_out=ssq[0:1, 0:1])
            rsqrt_from(rs[0:1, 0:1], ssq[0:1, 0:1], 1.0 / D, ["ssq"], ["rs"])
            act(XN[0:1, 0, :], x1h, AF.Copy, ["x1h", "rs"], ["xn0"], scale=rs[0:1, 0:1])
            for c in range(KC):
                bank = next_tp()
                q = bank
                tpv = ps[:, bank, :].bitcast(BF16)[:, 0:512]
                S.op("pe", lambda c=c, tpv=tpv: nc.tensor.transpose(tpv[:, 0:1], XN[0:1, 0, c * 128:(c + 1) * 128], ident[0:1, 0:1]),
                     ["xn0", "ident"], ["ps%d" % q, "ps%d" % bank], True)
                ts("dve", hT[:, c, 1025:1026], tpv[:, 0:1], AB[:, 4, c:c + 1], AB[:, 5, c:c + 1], ALU.mult, ALU.add,
                   ["ps%d" % q, "ps%d" % bank, "AB"], ["hT%d" % c])
                memset("pool", hT[:, c, 0:1], 0.0, ["hT%d" % c])
                memset("pool", hT[:, c, 1026:1028], 0.0, ["hT%d" % c])
            for g in range(2):
                for i in range(4):
                    t = g * 4 + i
                    S.op("dve", lambda i=i, t=t: nc.vector.scalar_tensor_tensor(
                        out=XN[:, i, :], in0=x1[:, t, :], scalar=1.0, in1=x1[:, t, :], op0=ALU.mult, op1=ALU.mult,
                        accum_out=ssq[:, i:i + 1]), ["x1_%d" % t], ["xn%d" % i, "ssq"])
                rsqrt_from(rs[:, 0:4], ssq[:, 0:4], 1.0 / D, ["ssq"], ["rs"])
                for i in range(4):
                    t = g * 4 + i
                    act(XN[:, i, :], x1[:, t, :], AF.Copy, ["x1_%d" % t, "rs"], ["xn%d" % i], scale=rs[:, i:i + 1])
                for c in range(KC):
                    bank = next_tp()
                    q = bank
                    tpv = ps[:, bank, :].bitcast(BF16)[:, 0:512]
                    for i in range(4):
                        S.op("pe", lambda i=i, c=c, tpv=tpv: nc.tensor.transpose(
                            tpv[:, i * 128:(i + 1) * 128], XN[:, i, c * 128:(c + 1) * 128], ident[:]),
                            ["xn%d" % i, "ident"], ["ps%d" % q, "ps%d" % bank], i == 3)
                    dst = hT[:, c, 1 + g * 512:1 + g * 512 + 512]
                    if c % 2 == 0:
                        act(dst, tpv[:, 0:512], AF.Identity, ["ps%d" % q, "ps%d" % bank, "AB"], ["hT%d" % c],
                            scale=AB[:, 4, c:c + 1], bias=AB[:, 5, c:c + 1])
                    else:
                        ts("dve", dst, tpv[:, 0:512], AB[:, 4, c:c + 1], AB[:, 5, c:c + 1], ALU.mult, ALU.add,
                           ["ps%d" % q, "ps%d" % bank, "AB"], ["hT%d" % c])
            checkpoint(6, [(P1f, 0)])
            S.barrier()

            for j in range(NJ):
                u, jl2 = j // 2, j % 2
                if j % 8 == 0:
                    load_wd(j // 8, 0)
                    load_wd(j // 8, 1)
                if jl2 == 0 and u + 1 <= (NJ - 1) // 2:
                    load_up(u + 1)
                grp, jl = j // 8, j % 8
                cA, cG = cbuf[(j % 2) * 2], cbuf[(j % 2) * 2 + 1]
                kA_, kG_ = "cb%d" % ((j % 2) * 2), "cb%d" % ((j % 2) * 2 + 1)
                for which, (cbf, ck) in enumerate(((cA, kA_), (cG, kG_))):
                    wv, wk = up_w[u][which]
                    o = ob_rot[0]
                    ob_rot[0] ^= 1
                    keys = ["ps%d" % (o * 3 + pi) for pi in range(3)]
                    for pi in range(3):
                        b = o * 3 + pi
                        for k in range(KC):
                            mm(ps[:, b, 0:344], wv[:, k, jl2 * 128:(jl2 + 1) * 128], hT[:, k, 342 * pi:342 * pi + 344],
                               k == 0, k == KC - 1, wk + ["hT%d" % k], ["ps%d" % b], k == KC - 1)
                    jj = j + which * NJ
                    fw = V_FCW + jj * 3
                    pvw = ps[:, o * 3:o * 3 + 3, 0:344]
                    rb = (2 * j + which) % 2
                    rw, rk = rawb[rb], "raw%d" % rb
                    cp("act", rw, pvw, keys, [rk])
                    act(v3(cbf), rw[:, :, 1:343], AF.Identity, [rk, "vecs"], [ck],
                        scale=vecs[:, fw + 1:fw + 2], bias=vecs[:, V_FCB + jj:V_FCB + jj + 1])
                    stt("dve", v3(cbf), rw[:, :, 0:342], vecs[:, fw:fw + 1], v3(cbf), ALU.mult, ALU.add, [rk, ck, "vecs"], [ck])
                    stt("dve", v3(cbf), rw[:, :, 2:344], vecs[:, fw + 2:fw + 3], v3(cbf), ALU.mult, ALU.add, [rk, ck, "vecs"], [ck])
                act(cG, cG, AF.Silu, [kG_], [kG_])
                tt("pool", actT[:, jl, :], cA[:, 0:T], cG[:, 0:T], ALU.mult, [kA_, kG_], ["act%d" % jl])
                G = min(8, NJ - grp * 8)
                if jl == G - 1:
                    last_o = ob_rot[0] ^ 1
                    order = [6, 7] + [(1 - last_o) * 3 + i_ for i_ in range(3)] + [last_o * 3 + i_ for i_ in range(3)]
                    for cb in range(4):
                        wd, wk_ = wdq.pop((grp, cb))

                        def evac(t, b, cb=cb):
                            tsl = tm_rot[0]
                            tm_rot[0] ^= 1
                            tb_ = tmpb[:, tsl, :]
                            tt("dve", tb_, ps[:, b, :], gt_bc[:, cb * 512:(cb + 1) * 512], ALU.mult,
                               ["ps%d" % b, "gt_bc"], ["tmpb%d" % tsl])
                            tt("pool", x1[:, t, cb * 512:(cb + 1) * 512], x1[:, t, cb * 512:(cb + 1) * 512], tb_, ALU.add,
                               ["tmpb%d" % tsl, "x1_%d" % t], ["x1_%d" % t])

                        if cb == 0 and G > 1:
                            for t in range(8):
                                b = order[t]
                                for q_ in range(G - 1):
                                    mm(ps[:, b, :], actT[:, q_, t * 128:(t + 1) * 128], wd[:, q_, :], q_ == 0, False,
                                       wk_ + ["act%d" % q_], ["ps%d" % b], q_ == G - 2)
                            for t in range(8):
                                b = order[t]
                                mm(ps[:, b, :], actT[:, G - 1, t * 128:(t + 1) * 128], wd[:, G - 1, :], False, True,
                                   wk_ + ["act%d" % (G - 1)], ["ps%d" % b], True)
                                evac(t, b)
                        else:
                            for t in range(8):
                                b = 6 + db_rot[0]
                                db_rot[0] ^= 1
                                for q_ in range(G):
                                    mm(ps[:, b, :], actT[:, q_, t * 128:(t + 1) * 128], wd[:, q_, :], q_ == 0, q_ == G - 1,
                                       wk_ + ["act%d" % q_], ["ps%d" % b], q_ == G - 1)
                                evac(t, b)
                        if cb + 2 < 4:
                            load_wd(grp, cb + 2)

            gfv = wsf[:, 0, :]
            S.dma("sp", gfv, gfin, writes=["ws0"])
            for t in range(8):
                act(XNf[:, t % 2, :], x1[:, t, :], AF.Square,
                    ["x1_%d" % t], ["ws%d" % (2 + t % 2), "ssq"], accum_out=ssq[:, t:t + 1])
                rsqrt_from(rs[:, t:t + 1], ssq[:, t:t + 1], 1.0 / D, ["ssq"], ["rs"])
                stt("dve", x1[:, t, :], x1[:, t, :], rs[:, t:t + 1], gfv, ALU.mult, ALU.mult, ["x1_%d" % t, "rs", "ws0"], ["x1_%d" % t])
                S.dma("sp", y[t * 128:(t + 1) * 128, :], x1[:, t, :], reads=["x1_%d" % t], is_output=True)
            S.finish()
      except _Stop:
            S.finish()
    return nc


def _rope_tables(pos):
    n = len(pos)
    inv = (10000.0 ** (-np.arange(0, 32, 2, dtype=np.float32) / 32.0)).astype(np.float32)
    row = (pos // 64).astype(np.float32)
    col = (pos % 64).astype(np.float32)
    ang = np.concatenate([row[:, None] * inv[None, :], col[:, None] * inv[None, :]], axis=-1).astype(np.float32)
    cos = np.cos(ang).astype(np.float32)
    sin = np.sin(ang).astype(np.float32)
    cos2 = np.repeat(cos, 2, axis=1).T
    sin2 = np.repeat(sin, 2, axis=1).T.copy()
    sin2[0::2, :] *= -1.0
    return np.ascontiguousarray(cos2), np.ascontiguousarray(sin2)


_NC_CACHE = {}


def kernel(x, c, ctx, c_ctx, w_ada, b_ada, g_mix_norm, w_in, g_q_a, w_q_b, g_kv_a, w_kv_b,
           conv_w, conv_b, g_mix_out, w_out, g_ffn_norm, w_up, ffn_conv_w, ffn_conv_b, w_down, g_final):
    f = lambda a: np.ascontiguousarray(np.asarray(a, dtype=np.float32))
    x, c, ctx, c_ctx = f(x), f(c), f(ctx), f(c_ctx)
    w_ada, b_ada, w_in, w_q_b, w_kv_b, w_out, w_up, w_down = (f(w_ada[0]), f(b_ada[0]), f(w_in[0]), f(w_q_b[0]),
                                                              f(w_kv_b[0]), f(w_out[0]), f(w_up[0]), f(w_down[0]))
    g_mix_norm, g_q_a, g_kv_a, conv_w, conv_b, g_mix_out, g_ffn_norm, ffn_conv_w, ffn_conv_b = (
        f(g_mix_norm[0]), f(g_q_a[0]), f(g_kv_a[0]), f(conv_w[0]), f(conv_b[0]), f(g_mix_out[0]),
        f(g_ffn_norm[0]), f(ffn_conv_w[0]), f(ffn_conv_b[0]))
    g_final = f(g_final)
    return _prep_and_run(locals())


def _prep_and_run(v, stage=99, cores=None):
    (x, c, ctx, c_ctx, w_ada, b_ada, w_in, w_q_b, w_kv_b, w_out, w_up, w_down, g_mix_norm, g_q_a, g_kv_a, conv_w,
     conv_b, g_mix_out, g_ffn_norm, ffn_conv_w, ffn_conv_b, g_final) = [v[k] for k in (
        "x", "c", "ctx", "c_ctx", "w_ada", "b_ada", "w_in", "w_q_b", "w_kv_b", "w_out", "w_up", "w_down",
        "g_mix_norm", "g_q_a", "g_kv_a", "conv_w", "conv_b", "g_mix_out", "g_ffn_norm", "ffn_conv_w", "ffn_conv_b",
        "g_final")]
    if stage not in _NC_CACHE:
        _NC_CACHE[stage] = build_nc(stage)
    nc = _NC_CACHE[stage]

    pm = lambda v: np.ascontiguousarray(v.reshape(-1, 128).T)
    perm = np.arange(64).reshape(32, 2)[:, ::-1].reshape(-1)
    w_krs = np.ascontiguousarray(w_in[:, 768:832][:, perm])
    qcols = np.concatenate([h * 192 + 128 + perm for h in range(8)])
    w_q_sw = np.ascontiguousarray(w_q_b[:, qcols])
    ident = np.eye(128, dtype=np.float32)
    in_maps = []
    for core in range(8):
        b, half = core // 2, core % 2
        rev = half == 1
        if not rev:
            own = np.arange(0, 1024)
            halo = np.array([1024, 1025])
            oth = np.arange(1024, 2048)
        else:
            own = np.arange(2047, 1023, -1)
            halo = np.array([1023, 1022])
            oth = np.arange(0, 1024)
        win = np.concatenate([own, halo])
        xw = np.ascontiguousarray(x[b][win])
        xo = np.ascontiguousarray(np.concatenate([x[b][oth], ctx[b]], axis=0))
        cs = np.ascontiguousarray(np.stack([c[b], c_ctx], axis=-1).reshape(16, 128, 2).transpose(1, 0, 2))
        vec = np.zeros((128, 512), np.float32)
        vec[:, 0:16] = pm(g_mix_norm)
        vec[:, 16:32] = pm(g_ffn_norm)
        vec[:, 32:36] = pm(g_q_a)
        vec[:, 36:38] = pm(g_kv_a)
        cw = conv_w[::-1] if rev else conv_w
        vec[:, 38:62] = cw.reshape(3, 8, 128).transpose(2, 1, 0).reshape(128, 24)
        vec[:, 62:70] = pm(conv_b)
        vec[:, 70:86] = pm(g_mix_out)
        fw = ffn_conv_w[::-1] if rev else ffn_conv_w
        vec[:, 86:344] = fw.reshape(3, 86, 128).transpose(2, 1, 0).reshape(128, 258)
        vec[:, 344:430] = pm(ffn_conv_b)
        for si_, sg_ in enumerate((0, 1, 3, 4)):
            vec[:, 430 + si_ * 16:430 + si_ * 16 + 16] = pm(b_ada[sg_ * 2048:(sg_ + 1) * 2048])
        cw_, sw_ = _rope_tables(win)
        ropew = np.zeros((64, 2, BW), np.float32)
        ropew[:, 0, 1:1027] = cw_
        ropew[:, 1, 1:1027] = sw_
        co_, so_ = _rope_tables(oth)
        ropeo = np.ascontiguousarray(np.stack([co_, so_], axis=1))
        in_maps.append({
            "xw": xw, "xo": xo, "cs": cs, "w_ada": w_ada, "b_ada": b_ada.reshape(1, -1), "w_in": w_in,
            "w_krs": w_krs, "w_q_b": w_q_b, "w_q_sw": w_q_sw, "w_kv_b": w_kv_b, "w_out": w_out, "w_up": w_up,
            "w_down": w_down, "vec": vec, "gfin": np.ascontiguousarray(np.broadcast_to(g_final, (128, D))),
            "ropew": ropew, "ropeo": ropeo, "ident": ident.astype(ml_dtypes.bfloat16), "identf": ident,
        })
    if cores is not None:
        res = run_bass_kernel_spmd(nc, [in_maps[i] for i in cores], core_ids=list(range(len(cores))))
        return res
    res = run_bass_kernel_spmd(nc, in_maps, core_ids=list(range(8)))
    out = np.zeros((4, 2048, D), np.float32)
    for core in range(8):
        b, half = core // 2, core % 2
        yv = res.results[core]["y"]
        if half == 0:
            out[b, 0:1024] = yv
        else:
            out[b, 1024:2048] = yv[::-1]
    return out
```

```python
import numpy as np
import ml_dtypes
from contextlib import ExitStack
import concourse.bass as bass
import concourse.mybir as mybir
from concourse.bass_utils import run_bass_kernel_spmd

F32 = mybir.dt.float32
BF16 = mybir.dt.bfloat16
AF = mybir.ActivationFunctionType
ALU = mybir.AluOpType

D = 2048
KC = 16
T = 1024
BW = 1028
NKEY = 2304
FFN = 5504
NJ = 43
EPS = 1e-6
PC = [(1, 342), (343, 342), (685, 342)]
QP = [(1, 384, [0, 1, 2]), (385, 384, [3, 4, 5]), (769, 258, [6, 7, 8])]
SCALE = 192.0 ** -0.5


class Sched:
    def __init__(self, nc, stack, n_dma_sems=40):
        self.nc = nc
        self.eng = {"pe": nc.tensor, "act": nc.scalar, "dve": nc.vector, "pool": nc.gpsimd, "sp": nc.sync}
        self.sem = {}
        self.cnt = {}
        for e in ("pe", "act", "dve", "pool"):
            self.sem[e] = stack.enter_context(nc.semaphore("s_" + e))
            self.cnt[e] = 0
        self.dsem = [stack.enter_context(nc.semaphore("d%d" % i)) for i in range(n_dma_sems)]
        self.dcnt = [0] * n_dma_sems
        self.dnext = 0
        self.dnext_pool = 0
        self.known = {e: {} for e in self.eng}
        self.last_w = {}
        self.readers = {}
        self.out_events = []

    def _deps(self, reads, writes):
        deps = {}

        def add(ev):
            if ev is None:
                return
            s, v = ev
            k = id(s)
            if k not in deps or deps[k][1] < v:
                deps[k] = (s, v)

        for r in reads:
            add(self.last_w.get(r))
        for w in writes:
            add(self.last_w.get(w))
            for ev in self.readers.get(w, {}).values():
                add(ev)
        return deps

    def _wait(self, e, deps, skip_own=False):
        kn = self.known[e]
        own = self.sem.get(e)
        for k, (s, v) in deps.items():
            if skip_own and own is not None and s is own:
                continue
            if kn.get(k, 0) >= v:
                continue
            self.eng[e].wait_ge(s, v)
            kn[k] = v

    def _record(self, ev, reads, writes):
        s, v = ev
        for r in reads:
            self.readers.setdefault(r, {})[id(s)] = (s, v)
        for w in writes:
            self.last_w[w] = ev
            self.readers[w] = {}

    def op(self, e, fn, reads=(), writes=(), inc=True):
        psr = [r for r in reads if r.startswith("ps")]
        if psr:
            writes = list(writes) + psr
            reads = [r for r in reads if not r.startswith("ps")]
        deps = self._deps(reads, writes)
        self._wait(e, deps, skip_own=(e == "pe"))
        ins = fn()
        if inc:
            self.cnt[e] += 1
            ins.then_inc(self.sem[e], 1)
            ev = (self.sem[e], self.cnt[e])
        else:
            ev = (self.sem[e], self.cnt[e] + 1)
        self._record(ev, reads, writes)
        return ins

    def dma(self, q, out, in_, reads=(), writes=(), is_output=False):
        deps = self._deps(reads, writes)
        self._wait(q, deps)
        half = len(self.dsem) // 2
        if q == "pool":
            i = half + self.dnext_pool
            self.dnext_pool = (self.dnext_pool + 1) % half
        else:
            i = self.dnext
            self.dnext = (self.dnext + 1) % half
        s = self.dsem[i]
        if self.dcnt[i] > 0:
            self._wait(q, {id(s): (s, self.dcnt[i])})
        self.dcnt[i] += 16
        self.eng[q].dma_start(out=out, in_=in_).then_inc(s, 16)
        ev = (s, self.dcnt[i])
        self._record(ev, reads, writes)
        if is_output:
            self.out_events.append(ev)
        return ev

    def barrier(self):
        deps = {}
        for e in ("pe", "act", "dve", "pool"):
            if self.cnt[e] > 0:
                deps[id(self.sem[e])] = (self.sem[e], self.cnt[e])
        for i, s in enumerate(self.dsem):
            if self.dcnt[i] > 0:
                deps[id(s)] = (s, self.dcnt[i])
        for e in self.eng:
            self._wait(e, deps)

    def finish(self):
        deps = {}
        for (s, v) in self.out_events:
            k = id(s)
            if k not in deps or deps[k][1] < v:
                deps[k] = (s, v)
        self._wait("sp", deps)


class _Stop(Exception):
    pass


def build_nc(stage=99):
    nc = bass.Bass("TRN2", target_bir_lowering=False)

    def din(name, shape, dt=F32):
        return nc.dram_tensor(name, list(shape), dt, kind="ExternalInput").ap()

    xw = din("xw", [1026, D])
    xo = din("xo", [1280, D])
    cs = din("cs", [128, KC, 2])
    w_ada = din("w_ada", [D, 6 * D])
    b_ada = din("b_ada", [1, 6 * D])
    w_in = din("w_in", [D, 3904])
    w_krs = din("w_krs", [D, 64])
    w_q_b = din("w_q_b", [512, 1536])
    w_q_sw = din("w_q_sw", [512, 512])
    w_kv_b = din("w_kv_b", [256, 2048])
    w_out = din("w_out", [D, D])
    w_up = din("w_up", [D, 2 * FFN])
    w_down = din("w_down", [FFN, D])
    vec = din("vec", [128, 512])
    gfin = din("gfin", [128, D])
    ropew = din("ropew", [64, 2, BW])
    ropeo = din("ropeo", [64, 2, T])
    ident_d = din("ident", [128, 128], BF16)
    identf_d = din("identf", [128, 128])
    y = nc.dram_tensor("y", [T, D], F32, kind="ExternalOutput").ap()
    dbg = None
    if stage != 99:
        dbg = nc.dram_tensor("dbg", [128, 16384], F32, kind="ExternalOutput").ap()

    w_ada_v = w_ada.rearrange("(k p) n -> p k n", p=128)
    w_in_v = w_in.rearrange("(k p) n -> p k n", p=128)
    w_krs_v = w_krs.rearrange("(k p) n -> p k n", p=128)
    w_q_b_v = w_q_b.rearrange("(k p) n -> p k n", p=128)
    w_q_sw_v = w_q_sw.rearrange("(k p) n -> p k n", p=128)
    w_kv_b_v = w_kv_b.rearrange("(k p) n -> p k n", p=128)
    w_out_v = w_out.rearrange("(k p) n -> p k n", p=128)
    w_up_v = w_up.rearrange("(k p) n -> p k n", p=128)
    w_down_v = w_down.rearrange("(j p) n -> p j n", p=128)

    with ExitStack() as st:
      S = Sched(nc, st)
      try:

            def sb(name, shape, dt):
                return st.enter_context(nc.sbuf_tensor(name, list(shape), dt))

            def ps_dump(b):
                S.barrier()
                cp("dve", gt_bc[:, 512 * (b % 4):512 * (b % 4) + 512], ps[:, b, :], [], [])
                return gt_bc[:, 512 * (b % 4):512 * (b % 4) + 512]

            def checkpoint(k, items):
                if stage != k:
                    return
                if callable(items):
                    items = items()
                S.barrier()
                for ap, col0 in items:
                    S.dma("sp", dbg[0:ap.shape[0], col0:col0 + ap.shape[1]], ap, is_output=True)
                raise _Stop()

            BIG = sb("BIG", [128, 16384], F32)
            P1 = sb("P1", [128, KC * BW], BF16)
            P2 = sb("P2", [128, KC * BW], BF16)
            WS = sb("WS", [128, 6, 4096], BF16)
            gt_bc = sb("gt_bc", [128, D], F32)
            modrow_t = sb("modrow", [128, 2064], F32)
            modrow = modrow_t[0:3, 0:D]
            rawb = [modrow_t[:, r * 1032:(r + 1) * 1032].rearrange("p (a n) -> p a n", a=3) for r in range(2)]
            ident = sb("identb", [128, 128], BF16)
            identf = sb("identf32", [128, 128], F32)
            ones128 = sb("ones128", [128, 128], F32)
            ones256 = sb("ones256", [128, 128], F32)
            ones512 = sb("ones512", [128, 128], F32)
            ones1 = sb("ones1", [128, 128], F32)
            onesb = sb("onesb", [128, 128], BF16)
            vecs = sb("vecs", [128, 512], F32)
            sel3 = sb("sel3", [3, 128], F32)
            rhs3 = sb("rhs3", [3, 2], F32)
            epst = sb("epst", [128, 1], F32)
            cst = sb("cst", [128, KC, 2], F32)
            sT = sb("sT", [128, KC, 2], BF16)
            modT = sb("modT", [128, KC, 2], F32)
            AB = sb("AB", [128, 6, KC], F32)
            gtfT = sb("gtfT", [128, KC], F32)
            ssq = sb("ssq", [128, 16], F32)
            rs = sb("rs", [128, 16], F32)
            ost = sb("ost", [128, 4, 9], F32)
            rsc = sb("rsc", [64, 2, 342], F32)
            tmpb = sb("tmpb", [128, 2, 512], F32)
            ps = st.enter_context(nc.psum_tensor("ps", [128, 8, 512], F32))

            def carve(arena, col0, parts, shape, dt):
                n = 1
                for s_ in shape:
                    n *= s_
                nf = n if dt == F32 else n // 2
                v = arena[0:parts, col0:col0 + nf]
                if dt != F32:
                    v = v.bitcast(dt)
                if len(shape) == 2:
                    v = v.rearrange("p (a b) -> p a b", a=shape[0])
                elif len(shape) == 3:
                    v = v.rearrange("p (a b c) -> p a b c", a=shape[0], b=shape[1])
                return v

            P1f = P1[:].bitcast(F32)
            P2f = P2[:].bitcast(F32)
            hT = P1[:].rearrange("p (c n) -> p c n", c=KC)
            mixT = P2[:].rearrange("p (c n) -> p c n", c=KC)
            XN = carve(P2f, 0, 128, [4, D], BF16)
            kvnT = carve(BIG, 0, 128, [2, NKEY], BF16)
            krT = carve(BIG, 2304, 64, [NKEY], BF16)
            krT128 = carve(BIG, 2304, 128, [NKEY], BF16)
            cosw = BIG[0:64, 3456:3456 + BW]
            sinw = BIG[0:64, 4484:4484 + BW]
            coso = BIG[0:64, 5512:5512 + T]
            sino = BIG[0:64, 6536:6536 + T]
            qnT = carve(BIG, 7560, 128, [4, BW], BF16)
            XT = carve(BIG, 9616, 128, [2, D], F32)
            PT = carve(BIG, 13712, 128, [4, 384], BF16)
            osb = carve(BIG, 14480, 128, [9, 129], F32)
            mixn = carve(BIG, 15641, 128, [9, 128], BF16)
            x1 = BIG[:].rearrange("p (t n) -> p t n", t=8)
            scr = [P2f[:, i * BW:(i + 1) * BW] for i in range(4)] + \
                  [BIG[:, 9616 + i * BW:9616 + (i + 1) * BW] for i in range(3)]
            x1h = P1f[0:1, 6176:6176 + D]
            xh = P1f[0:1, 4000:4000 + D]
            Kh = [carve(P1f, 0 + i * 1152, 128, [NKEY], BF16) for i in range(2)]
            Vh = [carve(P1f, 2304 + i * 1161, 128, [18, 129], BF16) for i in range(2)]
            qhn = [carve(P1f, 4626 + i * 514, 128, [BW], BF16) for i in range(2)]
            qhr = [carve(P1f, 5654 + i * 514, 64, [BW], BF16) for i in range(2)]
            qhr128 = [carve(P1f, 5654 + i * 514, 128, [BW], BF16) for i in range(2)]
            actT = carve(P2f, 0, 128, [8, T], BF16)
            cbuf = [P2f[:, 4096 + i * 1026:4096 + (i + 1) * 1026] for i in range(4)]
            wsf = WS[:].bitcast(F32)
            XNf = WS[:, 2:4, 0:D]
            scrB = [P2f[:, 6160 + i * 512:6160 + (i + 1) * 512] for i in range(4)]

            def ws_view(h0, nh, k, n):
                v = WS[:, h0:h0 + nh, :].rearrange("p h n -> p (h n)")[:, 0:k * n]
                return v.rearrange("p (k n) -> p k n", k=k)

            def wkeys(h0, nh):
                return ["ws%d" % i for i in range(h0, h0 + nh)]

            def mm(out, lhsT, rhs, start, stop, reads, writes, inc):
                S.op("pe", lambda: nc.tensor.matmul(out, lhsT=lhsT, rhs=rhs, start=start, stop=stop),
                     reads, writes, inc)

            def act(out, in_, func, reads, writes, **kw):
                S.op("act", lambda: nc.scalar.activation(out=out, in_=in_, func=func, **kw), reads, writes)

            def tt(e, out, in0, in1, op, reads, writes):
                eng = nc.vector if e == "dve" else nc.gpsimd
                S.op(e, lambda: eng.tensor_tensor(out=out, in0=in0, in1=in1, op=op), reads, writes)

            def stt(e, out, in0, scalar, in1, op0, op1, reads, writes):
                e = "dve"
                eng = nc.vector
                S.op(e, lambda: eng.scalar_tensor_tensor(out=out, in0=in0, scalar=scalar, in1=in1, op0=op0, op1=op1),
                     reads, writes)

            def ts(e, out, in0, s1, s2, op0, op1, reads, writes):
                eng = nc.vector if e == "dve" else nc.gpsimd
                if op1 is None:
                    S.op(e, lambda: eng.tensor_scalar(out=out, in0=in0, scalar1=s1, scalar2=None, op0=op0), reads, writes)
                else:
                    S.op(e, lambda: eng.tensor_scalar(out=out, in0=in0, scalar1=s1, scalar2=s2, op0=op0, op1=op1),
                         reads, writes)

            def cp(e, out, in_, reads, writes):
                if e == "act":
                    S.op("act", lambda: nc.scalar.copy(out=out, in_=in_), reads, writes)
                else:
                    eng = nc.vector if e == "dve" else nc.gpsimd
                    S.op(e, lambda: eng.tensor_copy(out=out, in_=in_), reads, writes)

            def memset(e, ap, val, writes):
                eng = nc.vector if e == "dve" else nc.gpsimd
                S.op(e, lambda: eng.memset(ap, val), (), writes)

            def rsqrt_from(out, in_, scale, reads, writes):
                np_ = out.shape[0]
                act(out, in_, AF.Ln, reads + ["epst"], writes, scale=scale, bias=epst[0:np_, :])
                act(out, out, AF.Exp, writes, writes, scale=-0.5)

            V_GMN, V_GFN, V_GQA, V_GKV, V_CW, V_CB, V_GMO, V_FCW, V_FCB = 0, 16, 32, 36, 38, 62, 70, 86, 344
            V_BADA = 430

            S.dma("sp", ident[:], ident_d, writes=["ident"])
            S.dma("sp", identf[:], identf_d, writes=["identf"])
            S.dma("sp", vecs[:], vec, writes=["vecs"])
            S.dma("sp", cst[:], cs, writes=["cst"])
            S.dma("sp", BIG[0:64, 3456:3456 + 2 * BW].rearrange("p (a n) -> p a n", a=2), ropew, writes=["rope"])
            S.dma("sp", BIG[0:64, 5512:5512 + 2 * T].rearrange("p (a n) -> p a n", a=2), ropeo, writes=["rope"])
            memset("dve", ones128[:], 1.0 / 128, ["ones"])
            memset("dve", ones256[:], 1.0 / 256, ["ones"])
            memset("dve", ones512[:], 1.0 / 512, ["ones"])
            memset("dve", ones1[:], 1.0, ["ones"])
            memset("dve", onesb[:], 1.0, ["ones"])
            memset("dve", epst[:], EPS, ["epst"])
            memset("dve", sel3[:], 1.0, ["sel3"])
            memset("dve", rhs3[:], 1.0, ["rhs3"])
            S.op("dve", lambda: nc.vector.tensor_copy(out=rhs3[0:2, 0:2], in_=identf[0:2, 0:2]), ["identf"], ["rhs3"])
            S.op("dve", lambda: nc.vector.tensor_copy(out=sel3[0:2, :], in_=identf[0:2, 0:1].to_broadcast([2, 128])),
                 ["identf"], ["sel3"])
            act(sT[:], cst[:], AF.Silu, ["cst"], ["sT"])
            memset("pool", krT128[64:128, :], 0.0, ["krT"])

            ws_rot = [0]

            ws_allowed = [[0, 2, 4]]

            def next_pair():
                al = ws_allowed[0]
                ws_rot[0] = (ws_rot[0] + 1) % len(al)
                return al[ws_rot[0]]

            ada_bank = [0]

            def ada_seg(seg):
                for blk in range(4):
                    h0 = next_pair()
                    wv = ws_view(h0, 2, KC, 512)
                    c0 = seg * D + blk * 512
                    S.dma("pool", wv, w_ada_v[:, :, c0:c0 + 512], writes=wkeys(h0, 2))
                    b = ada_bank[0]
                    ada_bank[0] ^= 1
                    for k in range(KC):
                        mm(ps[0:2, b, :], sT[:, k, :], wv[:, k, :], k == 0, k == KC - 1,
                           wkeys(h0, 2) + ["sT"], ["ps%d" % b], k == KC - 1)
                    cp("dve", modrow[0:2, blk * 512:(blk + 1) * 512], ps[0:2, b, :], ["ps%d" % b], ["modrow"])

            ada_tasks = [(seg_, hb_) for seg_ in (2, 3, 4, 5) for hb_ in range(8)]
            ada_pending = []
            ada_half = [0]
            ada_misc = [False]

            def ada_issue():
                if not ada_tasks:
                    return
                seg, hb = ada_tasks.pop(0)
                h = 0 + ada_half[0]
                ada_half[0] ^= 1
                wv = ws_view(h, 1, KC, 256)
                c0 = seg * D + hb * 256
                S.dma("pool", wv, w_ada_v[:, :, c0:c0 + 256], writes=wkeys(h, 1))
                ada_pending.append((seg, hb, wv, wkeys(h, 1)))

            def ada_step():
                if not ada_pending:
                    return
                seg, hb, wv, wk_ = ada_pending.pop(0)
                if hb == 0 and seg in (2, 5):
                    S.dma("sp", modrow[2:3, :], b_ada[0:1, seg * D:(seg + 1) * D], writes=["modrow"])
                b = misc_bank()
                for k in range(KC):
                    mm(ps[0:2, b, 0:256], sT[:, k, :], wv[:, k, :], k == 0, k == KC - 1,
                       wk_ + ["sT"], ["ps%d" % b], k == KC - 1)
                cp("dve", modrow[0:2, hb * 256:(hb + 1) * 256], ps[0:2, b, 0:256], ["ps%d" % b], ["modrow"])
                ada_issue()
                if hb == 7:
                    ada_misc[0] = True
                    if seg == 2:
                        ada_bc()
                    elif seg == 3:
                        ada_T(misc_bank(), 3)
                        cp("dve", AB[:, 5, :], modT[:, :, 0], ["modT"], ["AB"])
                    elif seg == 4:
                        ada_T(misc_bank(), 4)
                        ada_AB(V_GFN, 4, None)
                    ada_misc[0] = False

            def ada_T(bank=2, seg=0):
                pv = ps[:, bank, 0:32].rearrange("p (c n) -> p c n", c=KC)
                for c in range(KC):
                    mm(pv[:, c, :], modrow[0:2, c * 128:(c + 1) * 128], rhs3[0:2, :], True, True,
                       ["modrow", "rhs3"], ["ps%d" % bank], c == KC - 1)
                si = {0: 0, 1: 1, 3: 2, 4: 3}[seg]
                bcol = vecs[:, V_BADA + si * 16:V_BADA + si * 16 + 16]
                for j_ in range(2):
                    tt("dve", modT[:, :, j_], pv[:, :, j_], bcol, ALU.add, ["ps%d" % bank, "vecs"], ["modT"])

            def ada_bc():
                for i in range(4):
                    b = misc_bank() if ada_misc[0] else 2 + (i % 2)
                    mm(ps[:, b, :], sel3[:], modrow[0:3, i * 512:(i + 1) * 512], True, True,
                       ["modrow", "sel3"], ["ps%d" % b], True)
                    cp("dve", gt_bc[:, i * 512:(i + 1) * 512], ps[:, b, :], ["ps%d" % b], ["gt_bc"])

            def ada_AB(gcol, ia, ictx):
                ts("dve", AB[:, ia, :], modT[:, :, 0], 1.0, None, ALU.add, None, ["modT"], ["AB"])
                tt("dve", AB[:, ia, :], AB[:, ia, :], vecs[:, gcol:gcol + KC], ALU.mult, ["AB", "vecs"], ["AB"])
                if ictx is not None:
                    ts("dve", AB[:, ictx, :], modT[:, :, 1], 1.0, None, ALU.add, None, ["modT"], ["AB"])
                    tt("dve", AB[:, ictx, :], AB[:, ictx, :], vecs[:, gcol:gcol + KC], ALU.mult, ["AB", "vecs"], ["AB"])


            tp_rot = [0]
            tp_banks = [[5, 6, 7]]

            def next_tp():
                al = tp_banks[0]
                tp_rot[0] = (tp_rot[0] + 1) % len(al)
                return al[tp_rot[0]]

            def norm_to_T(tiles, iA, iB, dst_fn, dst_keys_fn, gcol_keys=()):
                n = len(tiles)
                rows = tiles[0][1]
                for i, (ap, r, keys) in enumerate(tiles):
                    act(XN[0:r, i, :], ap, AF.Square, keys, ["xn%d" % i, "ssq"], accum_out=ssq[0:r, i:i + 1])
                rsqrt_from(rs[0:rows, 0:n], ssq[0:rows, 0:n], 1.0 / D, ["ssq"], ["rs"])
                for i, (ap, r, keys) in enumerate(tiles):
                    act(XN[0:r, i, :], ap, AF.Copy, keys + ["rs"], ["xn%d" % i], scale=rs[0:r, i:i + 1])
                ncols = sum(t[1] for t in tiles)
                for c in range(KC):
                    bank = next_tp()
                    q = bank
                    tpv = ps[:, bank, :].bitcast(BF16)[:, 0:512]
                    col = 0
                    for i, (ap, r, keys) in enumerate(tiles):
                        S.op("pe", lambda i=i, r=r, col=col: nc.tensor.transpose(
                            tpv[:, col:col + r], XN[0:r, i, c * 128:(c + 1) * 128], ident[0:r, 0:r]),
                            ["xn%d" % i, "ident"], ["ps%d" % q], i == n - 1)
                        col += r
                    dst = dst_fn(c)
                    if c % 2 == 0:
                        act(dst, tpv[:, 0:ncols], AF.Identity, ["ps%d" % q, "AB"], dst_keys_fn(c),
                            scale=AB[:, iA, c:c + 1], bias=AB[:, iB, c:c + 1])
                    else:
                        ts("dve", dst, tpv[:, 0:ncols], AB[:, iA, c:c + 1], AB[:, iB, c:c + 1], ALU.mult, ALU.add,
                           ["ps%d" % q, "AB"], dst_keys_fn(c))

            def kv_project(hsrc, hkeys, pieces, wkv, wk, key_cols, rope, scr_sq, scr_rstd, scr_t):
                outs = [("kva0", 0, 128), ("kva1", 128, 128), ("krA", 256, 64), ("krB", 320, 64)]
                if rope is None:
                    outs = outs[:3]
                np_ = len(pieces)
                banks = {}
                for oi, (nm, wc, M) in enumerate(outs):
                    for pi, (c0, n) in enumerate(pieces):
                        b = (oi * np_ + pi)
                        banks[(nm, pi)] = b
                        for k in range(KC):
                            mm(ps[0:M, b, 0:n], wkv[:, k, wc:wc + M], hsrc(k, c0, n), k == 0, k == KC - 1,
                               wk + hkeys(k), ["ps%d" % b], k == KC - 1)
                return banks


            xt_rot = [0]

            def load_x(src_ap, rows):
                i = xt_rot[0]
                xt_rot[0] ^= 1
                S.dma("sp", XT[0:rows, i, :], src_ap, writes=["xt%d" % i])
                return (XT[0:rows, i, :], rows, ["xt%d" % i])

            hTc = carve(P2f, 4112, 128, [KC, 256], BF16)

            def gbuf(g, c, ncols):
                if g < 2:
                    return hT[:, c, g * 512:g * 512 + ncols]
                return hTc[:, c, 0:ncols]

            sq_s = scr[2]
            for g in range(3):
                ntile = 4 if g < 2 else 2
                ncols = ntile * 128
                iA, iB = (0, 1) if g < 2 else (2, 3)
                tiles = []
                n = ntile
                for i in range(n):
                    tl = load_x(xo[(g * 4 + i) * 128:(g * 4 + i + 1) * 128, :], 128)
                    tiles.append(tl)
                    S.op("dve", lambda i=i, tl=tl: nc.vector.scalar_tensor_tensor(
                        out=XN[:, i, :], in0=tl[0], scalar=1.0, in1=tl[0], op0=ALU.mult, op1=ALU.mult,
                        accum_out=ssq[:, i:i + 1]), tl[2], ["xn%d" % i, "ssq"])
                    rsqrt_from(rs[:, i:i + 1], ssq[:, i:i + 1], 1.0 / D, ["ssq"], ["rs"])
                    act(XN[:, i, :], tl[0], AF.Copy, tl[2] + ["rs"], ["xn%d" % i], scale=rs[:, i:i + 1])
                for c in range(KC):
                    bank = next_tp()
                    q = bank
                    tpv = ps[:, bank, :].bitcast(BF16)[:, 0:512]
                    for i in range(n):
                        S.op("pe", lambda i=i, c=c, tpv=tpv: nc.tensor.transpose(
                            tpv[:, i * 128:(i + 1) * 128], XN[:, i, c * 128:(c + 1) * 128], ident[:]),
                            ["xn%d" % i, "ident"], ["ps%d" % q], i == n - 1)
                    hk = "hTB%d_%d" % (c, g)
                    dst = gbuf(g, c, ncols)
                    cp("act" if c % 2 == 0 else "dve", dst, tpv[:, 0:ncols], ["ps%d" % q], [hk])

            ada_seg(0)
            ada_T(2, 0)
            cp("dve", AB[:, 1, :], modT[:, :, 0], ["modT"], ["AB"])
            cp("dve", AB[:, 3, :], modT[:, :, 1], ["modT"], ["AB"])
            ada_seg(1)
            ada_T(2, 1)
            ada_AB(V_GMN, 0, 2)
            checkpoint(1, [(AB[:].rearrange("p a b -> p (a b)"), 0), (modT[:].rearrange("p a b -> p (a b)"), 96)])
            h0kv = 0
            wkv = ws_view(0, 2, KC, 384)
            ws_allowed[0] = [2, 4]
            ws_rot[0] = 0
            S.dma("pool", wkv[:, :, 0:320], w_in_v[:, :, 512:832], writes=wkeys(0, 2))
            S.dma("pool", wkv[:, :, 320:384], w_krs_v, writes=wkeys(0, 2))
            WKV = wkeys(0, 2)
            for g in range(3):
                ncols = 512 if g < 2 else 256
                iA, iB = (0, 1) if g < 2 else (2, 3)
                for c in range(KC):
                    hk = "hTB%d_%d" % (c, g)
                    dst = gbuf(g, c, ncols)
                    if c % 2 == 0:
                        act(dst, dst, AF.Identity, [hk, "AB"], [hk], scale=AB[:, iA, c:c + 1], bias=AB[:, iB, c:c + 1])
                    else:
                        ts("dve", dst, dst, AB[:, iA, c:c + 1], AB[:, iB, c:c + 1], ALU.mult, ALU.add, [hk, "AB"], [hk])

            for g in range(3):
                ntile = 4 if g < 2 else 2
                ncols = ntile * 128
                rope = g < 2
                outs = [(0, 128, 0), (128, 128, 1), (256, 64, 2)] + ([(320, 64, 3)] if rope else [])
                for (wc, M, b) in outs:
                    for k in range(KC):
                        mm(ps[0:M, b, 0:ncols], wkv[:, k, wc:wc + M], gbuf(g, k, ncols), k == 0, k == KC - 1,
                           WKV + ["hTB%d_%d" % (k, g)], ["ps%d" % b], k == KC - 1)
                if g == 0:
                    checkpoint(22, lambda: [(ps_dump(0), 0), (ps_dump(1), 512), (ps_dump(2), 1024), (ps_dump(3), 1536)])
                kc0 = g * 512
                for c in range(2):
                    sqb = scrB[c]
                    act(sqb[:, 0:ncols], ps[:, c, 0:ncols], AF.Square, ["ps%d" % c], ["scrB%d" % c])
                    mm(ps[:, 4, 0:ncols], ones256[:], sqb[:, 0:ncols], c == 0, c == 1, ["scrB%d" % c, "ones"], ["ps4"], c == 1)
                rsqrt_from(scrB[2][:, 0:ncols], ps[:, 4, 0:ncols], 1.0, ["ps4"], ["scrB2"])
                for c in range(2):
                    stt("dve", kvnT[:, c, kc0:kc0 + ncols], ps[:, c, 0:ncols], vecs[:, V_GKV + c:V_GKV + c + 1],
                        scrB[2][:, 0:ncols], ALU.mult, ALU.mult, ["ps%d" % c, "scrB2", "vecs"], ["kvnT"])
                if g == 0:
                    checkpoint(23, [(BIG[:, 0:2304], 0), (scrB[2], 2304)])
                if rope:
                    t1 = scrB[3][0:64, 0:ncols]
                    t2 = scrB[0][0:64, 0:ncols]
                    tt("dve", t1, ps[0:64, 2, 0:ncols], coso[:, g * 512:g * 512 + ncols], ALU.mult,
                       ["ps2", "rope"], ["scrB3"])
                    tt("dve", t2, ps[0:64, 3, 0:ncols], sino[:, g * 512:g * 512 + ncols], ALU.mult,
                       ["ps3", "rope"], ["scrB0"])
                    tt("pool", krT[:, kc0:kc0 + ncols], t1, t2, ALU.add, ["scrB3", "scrB0"], ["krT"])
                else:
                    cp("act", krT[:, kc0:kc0 + ncols], ps[0:64, 2, 0:ncols], ["ps2"], ["krT"])

            checkpoint(2, [(BIG[:, 0:2304], 0), (BIG[0:64, 2304:3456], 2304)])
            tp_banks[0] = [4, 5, 6, 7]
            wq = ws_view(2, 2, KC, 512)
            S.dma("pool", wq, w_in_v[:, :, 0:512], writes=wkeys(2, 2))
            WQ = wkeys(2, 2)
            conv_w_q = {}
            for which_ in range(2):
                h_ = (4, 5)[which_]
                base_ = (1856, 2880)[which_]
                wv__ = ws_view(h_, 1, KC, 256)
                S.dma("pool", wv__, w_in_v[:, :, base_:base_ + 256], writes=wkeys(h_, 1))
                conv_w_q[(0, which_)] = (wv__, wkeys(h_, 1))
            for c in range(KC):
                memset("pool", hT[:, c, 0:1], 0.0, ["hT%d" % c, "hTB%d_0" % c, "hTB%d_1" % c, "hTB%d_2" % c])
                memset("pool", hT[:, c, 1027:1028], 0.0, ["hT%d" % c])
            for g in range(3):
                if g < 2:
                    tiles = []
                    for i in range(4):
                        t_ = g * 4 + i
                        tl = load_x(xw[t_ * 128:(t_ + 1) * 128, :], 128)
                        tiles.append(tl)
                        S.op("dve", lambda i=i, tl=tl: nc.vector.scalar_tensor_tensor(
                        out=XN[:, i, :], in0=tl[0], scalar=1.0, in1=tl[0], op0=ALU.mult, op1=ALU.mult,
                        accum_out=ssq[:, i:i + 1]), tl[2], ["xn%d" % i, "ssq"])
                        rsqrt_from(rs[:, i:i + 1], ssq[:, i:i + 1], 1.0 / D, ["ssq"], ["rs"])
                        act(XN[:, i, :], tl[0], AF.Copy, tl[2] + ["rs"], ["xn%d" % i], scale=rs[:, i:i + 1])
                    rws = [128] * 4
                else:
                    tl = load_x(xw[1024:1026, :], 2)
                    act(XN[0:2, 0, :], tl[0], AF.Square, tl[2], ["xn0", "ssq"], accum_out=ssq[0:2, 0:1])
                    rsqrt_from(rs[0:2, 0:1], ssq[0:2, 0:1], 1.0 / D, ["ssq"], ["rs"])
                    act(XN[0:2, 0, :], tl[0], AF.Copy, tl[2] + ["rs"], ["xn0"], scale=rs[0:2, 0:1])
                    rws = [2]
                ncols = sum(rws)
                col0 = 1 + g * 512
                for c in range(KC):
                    bank = next_tp()
                    q = bank
                    tpv = ps[:, bank, :].bitcast(BF16)[:, 0:512]
                    for i, r in enumerate(rws):
                        S.op("pe", lambda i=i, c=c, r=r, tpv=tpv: nc.tensor.transpose(
                            tpv[:, i * 128:i * 128 + r], XN[0:r, i, c * 128:(c + 1) * 128], ident[0:r, 0:r]),
                            ["xn%d" % i, "ident"], ["ps%d" % q], i == len(rws) - 1)
                    dst = hT[:, c, col0:col0 + ncols]
                    hks = ["hT%d" % c, "hTB%d_0" % c, "hTB%d_1" % c, "hTB%d_2" % c]
                    if c % 2 == 0:
                        act(dst, tpv[:, 0:ncols], AF.Identity, ["ps%d" % q, "AB"], hks,
                            scale=AB[:, 0, c:c + 1], bias=AB[:, 1, c:c + 1])
                    else:
                        ts("dve", dst, tpv[:, 0:ncols], AB[:, 0, c:c + 1], AB[:, 1, c:c + 1], ALU.mult, ALU.add,
                           ["ps%d" % q, "AB"], hks)

            checkpoint(31, [(P1f, 0)])
            HK = ["hT%d" % k for k in range(KC)]
            ob_rot = [0]

            def proj3(wv, wc, M, wk):
                o = ob_rot[0]
                ob_rot[0] ^= 1
                keys = ["ps%d" % (o * 3 + i) for i in range(3)]
                for pi, (c0, n) in enumerate(PC):
                    b = o * 3 + pi
                    for k in range(KC):
                        mm(ps[0:M, b, 0:n], wv[:, k, wc:wc + M], hT[:, k, c0:c0 + n], k == 0, k == KC - 1,
                           wk + ["hT%d" % k], ["ps%d" % b], k == KC - 1)
                return ps[0:M, o * 3:o * 3 + 3, 0:342], keys

            def v3(ap2d):
                return ap2d.rearrange("p (a n) -> p a n", a=3)

            st_rot = [0]

            def stats3(srcs, ones_m, dst_rstd, dst_key, sq_scr, sq_key):
                for pi in range(3):
                    b = 6 + st_rot[0]
                    st_rot[0] ^= 1
                    for si, (pv, keys) in enumerate(srcs):
                        act(sq_scr[si][:, pi * 342:(pi + 1) * 342], pv[:, pi, :], AF.Square, [keys[pi]], [sq_key[si]])
                        mm(ps[:, b, 0:342], ones_m[:], sq_scr[si][:, pi * 342:(pi + 1) * 342], si == 0, si == len(srcs) - 1,
                           [sq_key[si], "ones"], ["ps%d" % b], si == len(srcs) - 1)
                    rsqrt_from(dst_rstd[:, pi * 342:(pi + 1) * 342], ps[:, b, 0:342], 1.0, ["ps%d" % b], [dst_key])

            sqq = scr[4][:, 0:1026]
            for c in range(4):
                ob_rot[0] = 0
                pv, keys = proj3(wq, c * 128, 128, WQ)
                ts("dve", v3(qnT[:, c, 1:1027]), pv, vecs[:, V_GQA + c:V_GQA + c + 1], None, ALU.mult, None,
                   keys + ["vecs"], ["qnT"])
                act(v3(sqq), pv, AF.Square, keys, ["scr4"])
                for pi in range(3):
                    mm(ps[:, 3 + pi, 0:342], ones512[:], sqq[:, pi * 342:(pi + 1) * 342], c == 0, c == 3,
                       ["scr4", "ones"], ["ps%d" % (3 + pi)], c == 3)
            rq = scr[5][:, 0:1026]
            for pi in range(3):
                rsqrt_from(rq[:, pi * 342:(pi + 1) * 342], ps[:, 3 + pi, 0:342], 1.0, ["ps%d" % (3 + pi)], ["scr5"])
            for c in range(4):
                tt("pool" if c % 2 else "dve", qnT[:, c, 1:1027], qnT[:, c, 1:1027], rq, ALU.mult, ["qnT", "scr5"], ["qnT"])
            ob_rot[0] = 0

            def load_conv(gp, which, h):
                base = (1856, 2880, 832)[which]
                wv_ = ws_view(h, 1, KC, 256)
                S.dma("pool", wv_, w_in_v[:, :, base + gp * 256:base + gp * 256 + 256], writes=wkeys(h, 1))
                conv_w_q[(gp, which)] = (wv_, wkeys(h, 1))

            CONV_SLOTS = {0: (4, 5, 2), 1: (3, 0, 1), 2: (4, 5, 2), 3: (3, 0, 1)}
            load_conv(0, 2, 2)

            pk0, k0 = proj3(wkv, 0, 128, WKV)
            pk1, k1 = proj3(wkv, 128, 128, WKV)
            stats3([(pk0, k0), (pk1, k1)], ones256, scr[5][:, 0:1026], "scr5",
                   [scr[4][:, 0:1026], scr[6][:, 0:1026]], ["scr4", "scr6"])
            tmpk = scr[6][:, 0:1026]
            for c, (pv, keys) in enumerate([(pk0, k0), (pk1, k1)]):
                stt("dve", v3(tmpk), pv, vecs[:, V_GKV + c:V_GKV + c + 1], v3(scr[5][:, 0:1026]), ALU.mult, ALU.mult,
                    keys + ["scr5", "vecs"], ["scr6"])
                cp("pool", kvnT[:, c, 1280:2304], tmpk[:, 0:1024], ["scr6"], ["kvnT"])
            pA, kA = proj3(wkv, 256, 64, WKV)
            pB, kB = proj3(wkv, 320, 64, WKV)
            t1 = scr[4][0:64, 0:1026]
            t2 = scr[6][0:64, 0:1026]
            tt("dve", v3(t1), pA, v3(cosw[:, 1:1027]), ALU.mult, kA + ["rope"], ["scr4"])
            tt("dve", v3(t2), pB, v3(sinw[:, 1:1027]), ALU.mult, kB + ["rope"], ["scr6"])
            tt("pool", krT[:, 1280:2304], t1[:, 0:1024], t2[:, 0:1024], ALU.add, ["scr4", "scr6"], ["krT"])

            scr7 = BIG[:, 13712:13712 + BW]
            scr8 = BIG[:, 14740:14740 + BW]
            pb = scr[1]
            memset("pool", pb[:, 0:1], 0.0, ["scr1"])
            memset("pool", pb[:, 1027:1028], 0.0, ["scr1"])

            def conv_fin(g):
                co, cok = ((scr[3], "scr3"), (scr7, "scr7"))[g % 2]
                sqc, sqk = ((scr[4], "scr4"), (scr8, "scr8"))[g % 2]
                co = co[:, 0:1026]
                sqc = sqc[:, 0:1026]
                rc = scr[5][:, 0:1026]
                for pi in range(3):
                    b = 6 + st_rot[0]
                    st_rot[0] ^= 1
                    mm(ps[:, b, 0:342], ones128[:], sqc[:, pi * 342:(pi + 1) * 342], True, True,
                       [sqk, "ones"], ["ps%d" % b], True)
                    rsqrt_from(rc[:, pi * 342:(pi + 1) * 342], ps[:, b, 0:342], 1.0, ["ps%d" % b], ["scr5"])
                stt("dve", mixT[:, 8 + g, 1:1027], co, vecs[:, V_GMO + 8 + g:V_GMO + 9 + g], rc, ALU.mult, ALU.mult,
                    [cok, "scr5", "vecs"], ["mixT%d" % (8 + g)])

            for gp in range(4):
                if gp + 1 < 4:
                    for which in range(3):
                        load_conv(gp + 1, which, CONV_SLOTS[gp + 1][which])
                else:
                    S.dma("pool", ws_view(4, 2, 4, 1536), w_q_b_v, writes=wkeys(4, 2))
                    S.dma("pool", ws_view(2, 1, 4, 512), w_q_sw_v, writes=wkeys(2, 1))
                wts = [conv_w_q[(gp, w_)] for w_ in range(3)]
                for gl in range(2):
                    g = gp * 2 + gl
                    pcc, kcc = proj3(wts[0][0], gl * 128, 128, wts[0][1])
                    cp("act", v3(scr[0][:, 0:1026]), pcc, kcc, ["scr0"])
                    pch, kch = proj3(wts[1][0], gl * 128, 128, wts[1][1])
                    tt("dve", v3(pb[:, 1:1027]), pch, v3(scr[0][:, 0:1026]), ALU.mult, kch + ["scr0"], ["scr1"])
                    tb = scr[2][:, 0:1026]
                    cwc = V_CW + g * 3
                    act(tb, pb[:, 1:1027], AF.Identity, ["scr1", "vecs"], ["scr2"],
                        scale=vecs[:, cwc + 1:cwc + 2], bias=vecs[:, V_CB + g:V_CB + g + 1])
                    stt("dve", tb, pb[:, 0:1026], vecs[:, cwc:cwc + 1], tb, ALU.mult, ALU.add, ["scr1", "scr2", "vecs"], ["scr2"])
                    stt("dve", tb, pb[:, 2:1028], vecs[:, cwc + 2:cwc + 3], tb, ALU.mult, ALU.add, ["scr1", "scr2", "vecs"], ["scr2"])
                    pcb, kcb = proj3(wts[2][0], gl * 128, 128, wts[2][1])
                    co, cok = ((scr[3], "scr3"), (scr7, "scr7"))[g % 2]
                    sqc, sqk = ((scr[4], "scr4"), (scr8, "scr8"))[g % 2]
                    tt("dve", v3(co[:, 0:1026]), pcb, v3(tb), ALU.mult, kcb + ["scr2"], [cok])
                    act(sqc[:, 0:1026], co[:, 0:1026], AF.Square, [cok], [sqk])
                    if g > 0:
                        conv_fin(g - 1)
            conv_fin(7)
            S.dma("pool", ws_view(3, 1, 2, 2048), w_kv_b_v, writes=wkeys(3, 1))


            checkpoint(3, [(P2f, 0), (BIG[:, 7560:9616], 8224), (BIG[:, 0:2304], 10280), (BIG[0:64, 2304:3456], 12584)])
            S.barrier()
            wqb = ws_view(4, 2, 4, 1536)
            wqsw = ws_view(2, 1, 4, 512)
            wkvb = ws_view(3, 1, 2, 2048)
            WQB, WQS, WKB = wkeys(4, 2), wkeys(2, 1), wkeys(3, 1)
            tp_banks[0] = [6, 7]
            memset("dve", osb[:], 1.0, ["osb"])
            for i in range(2):
                memset("pool", qhr128[i][64:128, :], 0.0, ["qhr%d" % i])

            def fixed_pair():
                return 4
            mb = [6]

            def misc_bank():
                b = mb[0]
                mb[0] = 13 - b
                return b
            wout_q = {}

            def load_wout(cb, h0):
                wv_ = ws_view(h0, 2, KC, 512)
                S.dma("pool", wv_, w_out_v[:, :, cb * 512:(cb + 1) * 512], writes=wkeys(h0, 2))
                wout_q[cb] = (wv_, h0)

            sb_rot = [0]
            pt_rot = [0]
            acc_rot = [0]
            KP = [(0, 512), (512, 512), (1024, 512), (1536, 512), (2048, 256)]
            eb_rot = [0]
            EXPB = [0, 1, 2, 6, 7]

            def exp_bank():
                eb_rot[0] = (eb_rot[0] + 1) % len(EXPB)
                return EXPB[eb_rot[0]]

            def att_expand(h, bank_fn):
                i = h % 2
                for pi, (c0, n) in enumerate(KP):
                    b = bank_fn()
                    for kc in range(2):
                        mm(ps[:, b, 0:n], wkvb[:, kc, h * 256:h * 256 + 128], kvnT[:, kc, c0:c0 + n], kc == 0, kc == 1,
                           WKB + ["kvnT"], ["ps%d" % b], kc == 1)
                    cp("dve", Kh[i][:, c0:c0 + n], ps[:, b, 0:n], ["ps%d" % b], ["Kh%d" % i])
                    yield
                for k0_ in range(0, 18, 4):
                    nk = min(4, 18 - k0_)
                    b = bank_fn()
                    for kk in range(nk):
                        kt = k0_ + kk
                        for kc in range(2):
                            mm(ps[:, b, kk * 128:(kk + 1) * 128], kvnT[:, kc, kt * 128:(kt + 1) * 128],
                               wkvb[:, kc, h * 256 + 128:h * 256 + 256], kc == 0, kc == 1,
                               WKB + ["kvnT"], ["ps%d" % b], kk == nk - 1 and kc == 1)
                    cp("dve" if (k0_ // 4) % 2 else "act", Vh[i][:, k0_:k0_ + nk, 0:128],
                       ps[:, b, 0:nk * 128].rearrange("p (a n) -> p a n", a=nk), ["ps%d" % b], ["Vh%d" % i])
                    yield
                for pi, (c0, n) in enumerate(PC):
                    b = bank_fn()
                    for kc in range(4):
                        mm(ps[:, b, 0:n], wqb[:, kc, h * 192:h * 192 + 128], qnT[:, kc, c0:c0 + n], kc == 0, kc == 3,
                           WQB + ["qnT"], ["ps%d" % b], kc == 3)
                    cp("dve", qhn[i][:, c0:c0 + n], ps[:, b, 0:n], ["ps%d" % b], ["qhn%d" % i])
                    yield
                    bA = bank_fn()
                    for kc in range(4):
                        mm(ps[0:64, bA, 0:n], wqb[:, kc, h * 192 + 128:h * 192 + 192], qnT[:, kc, c0:c0 + n], kc == 0, kc == 3,
                           WQB + ["qnT"], ["ps%d" % bA], kc == 3)
                    tt("dve", rsc[:, 0, 0:n], ps[0:64, bA, 0:n], cosw[:, c0:c0 + n], ALU.mult, ["ps%d" % bA, "rope"], ["rsc0"])
                    yield
                    bB = bank_fn()
                    for kc in range(4):
                        mm(ps[0:64, bB, 0:n], wqsw[:, kc, h * 64:h * 64 + 64], qnT[:, kc, c0:c0 + n], kc == 0, kc == 3,
                           WQS + ["qnT"], ["ps%d" % bB], kc == 3)
                    tt("dve", rsc[:, 1, 0:n], ps[0:64, bB, 0:n], sinw[:, c0:c0 + n], ALU.mult, ["ps%d" % bB, "rope"], ["rsc1"])
                    tt("pool", qhr[i][:, c0:c0 + n], rsc[:, 0, 0:n], rsc[:, 1, 0:n], ALU.add, ["rsc0", "rsc1"], ["qhr%d" % i])
                    yield

            FB = [BIG[:, 9616 + k_ * 342:9616 + (k_ + 1) * 342] for k_ in range(4)]
            DACC = [BIG[:, 9616 + 1368 + k_ * 342:9616 + 1368 + (k_ + 1) * 342] for k_ in range(4)]
            PT2 = carve(BIG, 13712, 128, [4, 342], BF16)
            LAG = 2
            pend = []
            deferred = []

            def fin_a(h, ab, q0, nq, dk):
                S.op("pe", lambda: nc.tensor.matmul(ps[:, 3, 0:nq], lhsT=ones1[:], rhs=DACC[dk], start=False, stop=True),
                     ["dacc%d" % dk, "ones"], ["ps3"], True)
                cp("dve", FB[2], ps[:, 3, 0:nq], ["ps3"], ["F2"])
                cp("dve", FB[0], ps[:, ab, 0:nq], ["ps%d" % ab], ["F0"])
                deferred.append((h, q0, nq))

            def fin_b(h, q0, nq):
                tt("dve", FB[1], FB[0], FB[0], ALU.mult, ["F0"], ["F1"])
                b1 = misc_bank()
                mm(ps[:, b1, 0:nq], ones128[:], FB[1], True, True, ["F1", "ones"], ["ps%d" % b1], True)
                stt("dve", FB[3], FB[2], EPS, FB[2], ALU.mult, ALU.mult, ["F2"], ["F3"])
                tt("dve", FB[3], ps[:, b1, 0:nq], FB[3], ALU.add, ["ps%d" % b1, "F3"], ["F3"])
                act(FB[3], FB[3], AF.Ln, ["F3"], ["F3"])
                act(FB[3], FB[3], AF.Exp, ["F3"], ["F3"], scale=-0.5)
                stt("dve", mixT[:, h, q0:q0 + nq], FB[0], vecs[:, V_GMO + h:V_GMO + h + 1], FB[3], ALU.mult, ALU.mult,
                    ["F0", "F3", "vecs"], ["mixT%d" % h])

            def pv_step(h, ab, q0, nq, kt, slot, dk):
                i = h % 2
                mm(ps[:, ab, 0:nq], Vh[i][:, kt, 0:128], PT2[:, slot, 0:nq], kt == 0, kt == 17,
                   ["PT%d" % slot, "Vh%d" % i], ["ps%d" % ab], True)
                if kt % 2 == 1:
                    mm(ps[:, 3, 0:nq], onesb[:], PT2[:, slot, 0:nq], kt == 1, False, ["PT%d" % slot, "ones"], ["ps3"], True)
                elif kt < 1:
                    cp("dve", DACC[dk], PT2[:, slot, 0:nq], ["PT%d" % slot], ["dacc%d" % dk])
                else:
                    tt("dve", DACC[dk], DACC[dk], PT2[:, slot, 0:nq], ALU.add, ["PT%d" % slot, "dacc%d" % dk], ["dacc%d" % dk])
                if kt == 17:
                    fin_a(h, ab, q0, nq, dk)

            ada_issue()
            ada_issue()
            for _ in att_expand(0, exp_bank):
                pass
            pass_ctr = [0]
            for h in range(8):
                i = h % 2
                gen = att_expand(h + 1, misc_bank) if h + 1 < 8 else None
                step_idx = 0
                for (q0, nq) in PC:
                    ab = 4 + acc_rot[0]
                    acc_rot[0] ^= 1
                    dk = pass_ctr[0] % 2
                    pass_ctr[0] += 1
                    for kt in range(18):
                        sbk = sb_rot[0]
                        sb_rot[0] = (sbk + 1) % 3
                        mm(ps[:, sbk, 0:nq], Kh[i][:, kt * 128:(kt + 1) * 128], qhn[i][:, q0:q0 + nq], True, False,
                           ["Kh%d" % i, "qhn%d" % i], ["ps%d" % sbk], False)
                        mm(ps[:, sbk, 0:nq], krT128[:, kt * 128:(kt + 1) * 128], qhr128[i][:, q0:q0 + nq], False, True,
                           ["krT", "qhr%d" % i], ["ps%d" % sbk], True)
                        slot = pt_rot[0]
                        pt_rot[0] = (slot + 1) % 4
                        act(PT2[:, slot, 0:nq], ps[:, sbk, 0:nq], AF.Exp, ["ps%d" % sbk], ["PT%d" % slot], scale=SCALE)
                        pend.append((h, ab, q0, nq, kt, slot, dk))
                        if len(pend) > LAG:
                            pv_step(*pend.pop(0))
                        if kt == 6 and deferred:
                            fin_b(*deferred.pop(0))
                        step_idx += 1
                        if gen is not None and step_idx >= 8 and step_idx % 2 == 0:
                            if next(gen, "done") == "done":
                                gen = None
                                if h + 1 == 7:
                                    load_wout(0, 4)
                                    load_wout(1, 2)
                    ada_step()
                    if q0 == 1:
                        ada_step()
                if gen is not None:
                    for _ in gen:
                        pass
                    if h + 1 == 7:
                        load_wout(0, 4)
                        load_wout(1, 2)
            while pend:
                pv_step(*pend.pop(0))
            while deferred:
                fin_b(*deferred.pop(0))
            while ada_pending:
                ada_step()

            checkpoint(4, [(P2f, 0)])
            S.barrier()
            ws_allowed[0] = [0, 2, 4]
            ws_rot[0] = 0
            for t in range(8):
                S.dma("sp", x1[:, t, :], xw[t * 128:(t + 1) * 128, :], writes=["x1_%d" % t])
            S.dma("sp", xh, xw[1024:1025, :], writes=["xh"])
            ob8 = [0]
            for cb in range(4):
                wv, h0 = wout_q[cb]
                for t in range(9):
                    M = 128 if t < 8 else 1
                    c0 = 1 + 128 * t if t < 8 else 1025
                    b = ob8[0]
                    ob8[0] = (b + 1) % 8
                    for k in range(KC):
                        mm(ps[0:M, b, :], mixT[:, k, c0:c0 + M], wv[:, k, :], k == 0, k == KC - 1,
                           wkeys(h0, 2) + ["mixT%d" % k], ["ps%d" % b], k == KC - 1)
                    tb_ = tmpb[0:M, b % 2, :]
                    tt("dve", tb_, ps[0:M, b, :], gt_bc[0:M, cb * 512:(cb + 1) * 512], ALU.mult,
                       ["ps%d" % b, "gt_bc"], ["tmpb%d" % (b % 2)])
                    if t < 8:
                        tt("pool", x1[:, t, cb * 512:(cb + 1) * 512], x1[:, t, cb * 512:(cb + 1) * 512], tb_, ALU.add,
                           ["tmpb%d" % (b % 2), "x1_%d" % t], ["x1_%d" % t])
                    else:
                        tt("pool", x1h[:, cb * 512:(cb + 1) * 512], xh[:, cb * 512:(cb + 1) * 512], tb_, ALU.add,
                           ["tmpb%d" % (b % 2), "xh"], ["x1h"])
                if cb + 2 < 4:
                    load_wout(cb + 2, h0)
            checkpoint(5, [(BIG[:], 0)])
            S.barrier()
            ada_bc()

            hs_rot = [0]

            def next_half4():
                h = hs_rot[0]
                hs_rot[0] = (h + 1) % 4
                return h
            wd_rot = [0]
            db_rot = [0]
            up_w = {}

            def load_up(u):
                j0 = u * 2
                nj = min(2, NJ - j0)
                res = []
                for base in (0, FFN):
                    h = next_half4()
                    wv = ws_view(h, 1, KC, 256)
                    S.dma("pool", wv[:, :, 0:nj * 128], w_up_v[:, :, base + j0 * 128:base + (j0 + nj) * 128], writes=wkeys(h, 1))
                    res.append((wv, wkeys(h, 1)))
                up_w[u] = res

            wdq = {}
            tm_rot = [0]

            def load_wd(grp, cb):
                G_ = min(8, NJ - grp * 8)
                h = 4 + wd_rot[0]
                wd_rot[0] ^= 1
                wd = ws_view(h, 1, 8, 512)
                S.dma("pool", wd[:, 0:G_, :], w_down_v[:, grp * 8:grp * 8 + G_, cb * 512:(cb + 1) * 512], writes=wkeys(h, 1))
                wdq[(grp, cb)] = (wd, wkeys(h, 1))

            load_up(0)
            tp_banks[0] = [4, 5, 6, 7]
            act(XN[0:1, 0, :], x1h, AF.Square, ["x1h"], ["xn0", "ssq"], accum_out=ssq[0:1, 0:1])
            rsqrt_from(rs[0:1, 0:1], ssq[0:1, 0:1], 1.0 / D, ["ssq"], ["rs"])
            act(XN[0:1, 0, :], x1h, AF.Copy, ["x1h", "rs"], ["xn0"], scale=rs[0:1, 0:1])
            for c in range(KC):
                bank = next_tp()
                q = bank
                tpv = ps[:, bank, :].bitcast(BF16)[:, 0:512]
                S.op("pe", lambda c=c, tpv=tpv: nc.tensor.transpose(tpv[:, 0:1], XN[0:1, 0, c * 128:(c + 1) * 128], ident[0:1, 0:1]),
                     ["xn0", "ident"], ["ps%d" % q, "ps%d" % bank], True)
                ts("dve", hT[:, c, 1025:1026], tpv[:, 0:1], AB[:, 4, c:c + 1], AB[:, 5, c:c + 1], ALU.mult, ALU.add,
                   ["ps%d" % q, "ps%d" % bank, "AB"], ["hT%d" % c])
                memset("pool", hT[:, c, 0:1], 0.0, ["hT%d" % c])
                memset("pool", hT[:, c, 1026:1028], 0.0, ["hT%d" % c])
            for g in range(2):
                for i in range(4):
                    t = g * 4 + i
                    S.op("dve", lambda i=i, t=t: nc.vector.scalar_tensor_tensor(
                        out=XN[:, i, :], in0=x1[:, t, :], scalar=1.0, in1=x1[:, t, :], op0=ALU.mult, op1=ALU.mult,
                        accum_out=ssq[:, i:i + 1]), ["x1_%d" % t], ["xn%d" % i, "ssq"])
                rsqrt_from(rs[:, 0:4], ssq[:, 0:4], 1.0 / D, ["ssq"], ["rs"])
                for i in range(4):
                    t = g * 4 + i
                    act(XN[:, i, :], x1[:, t, :], AF.Copy, ["x1_%d" % t, "rs"], ["xn%d" % i], scale=rs[:, i:i + 1])
                for c in range(KC):
                    bank = next_tp()
                    q = bank
                    tpv = ps[:, bank, :].bitcast(BF16)[:, 0:512]
                    for i in range(4):
                        S.op("pe", lambda i=i, c=c, tpv=tpv: nc.tensor.transpose(
                            tpv[:, i * 128:(i + 1) * 128], XN[:, i, c * 128:(c + 1) * 128], ident[:]),
                            ["xn%d" % i, "ident"], ["ps%d" % q, "ps%d" % bank], i == 3)
                    dst = hT[:, c, 1 + g * 512:1 + g * 512 + 512]
                    if c % 2 == 0:
                        act(dst, tpv[:, 0:512], AF.Identity, ["ps%d" % q, "ps%d" % bank, "AB"], ["hT%d" % c],
                            scale=AB[:, 4, c:c + 1], bias=AB[:, 5, c:c + 1])
                    else:
                        ts("dve", dst, tpv[:, 0:512], AB[:, 4, c:c + 1], AB[:, 5, c:c + 1], ALU.mult, ALU.add,
                           ["ps%d" % q, "ps%d" % bank, "AB"], ["hT%d" % c])
            checkpoint(6, [(P1f, 0)])
            S.barrier()

            for j in range(NJ):
                u, jl2 = j // 2, j % 2
                if j % 8 == 0:
                    load_wd(j // 8, 0)
                    load_wd(j // 8, 1)
                if jl2 == 0 and u + 1 <= (NJ - 1) // 2:
                    load_up(u + 1)
                grp, jl = j // 8, j % 8
                cA, cG = cbuf[(j % 2) * 2], cbuf[(j % 2) * 2 + 1]
                kA_, kG_ = "cb%d" % ((j % 2) * 2), "cb%d" % ((j % 2) * 2 + 1)
                for which, (cbf, ck) in enumerate(((cA, kA_), (cG, kG_))):
                    wv, wk = up_w[u][which]
                    o = ob_rot[0]
                    ob_rot[0] ^= 1
                    keys = ["ps%d" % (o * 3 + pi) for pi in range(3)]
                    for pi in range(3):
                        b = o * 3 + pi
                        for k in range(KC):
                            mm(ps[:, b, 0:344], wv[:, k, jl2 * 128:(jl2 + 1) * 128], hT[:, k, 342 * pi:342 * pi + 344],
                               k == 0, k == KC - 1, wk + ["hT%d" % k], ["ps%d" % b], k == KC - 1)
                    jj = j + which * NJ
                    fw = V_FCW + jj * 3
                    pvw = ps[:, o * 3:o * 3 + 3, 0:344]
                    rb = (2 * j + which) % 2
                    rw, rk = rawb[rb], "raw%d" % rb
                    cp("act", rw, pvw, keys, [rk])
                    act(v3(cbf), rw[:, :, 1:343], AF.Identity, [rk, "vecs"], [ck],
                        scale=vecs[:, fw + 1:fw + 2], bias=vecs[:, V_FCB + jj:V_FCB + jj + 1])
                    stt("dve", v3(cbf), rw[:, :, 0:342], vecs[:, fw:fw + 1], v3(cbf), ALU.mult, ALU.add, [rk, ck, "vecs"], [ck])
                    stt("dve", v3(cbf), rw[:, :, 2:344], vecs[:, fw + 2:fw + 3], v3(cbf), ALU.mult, ALU.add, [rk, ck, "vecs"], [ck])
                act(cG, cG, AF.Silu, [kG_], [kG_])
                tt("pool", actT[:, jl, :], cA[:, 0:T], cG[:, 0:T], ALU.mult, [kA_, kG_], ["act%d" % jl])
                G = min(8, NJ - grp * 8)
                if jl == G - 1:
                    last_o = ob_rot[0] ^ 1
                    order = [6, 7] + [(1 - last_o) * 3 + i_ for i_ in range(3)] + [last_o * 3 + i_ for i_ in range(3)]
                    for cb in range(4):
                        wd, wk_ = wdq.pop((grp, cb))

                        def evac(t, b, cb=cb):
                            tsl = tm_rot[0]
                            tm_rot[0] ^= 1
                            tb_ = tmpb[:, tsl, :]
                            tt("dve", tb_, ps[:, b, :], gt_bc[:, cb * 512:(cb + 1) * 512], ALU.mult,
                               ["ps%d" % b, "gt_bc"], ["tmpb%d" % tsl])
                            tt("pool", x1[:, t, cb * 512:(cb + 1) * 512], x1[:, t, cb * 512:(cb + 1) * 512], tb_, ALU.add,
                               ["tmpb%d" % tsl, "x1_%d" % t], ["x1_%d" % t])

                        if cb == 0 and G > 1:
                            for t in range(8):
                                b = order[t]
                                for q_ in range(G - 1):
                                    mm(ps[:, b, :], actT[:, q_, t * 128:(t + 1) * 128], wd[:, q_, :], q_ == 0, False,
                                       wk_ + ["act%d" % q_], ["ps%d" % b], q_ == G - 2)
                            for t in range(8):
                                b = order[t]
                                mm(ps[:, b, :], actT[:, G - 1, t * 128:(t + 1) * 128], wd[:, G - 1, :], False, True,
                                   wk_ + ["act%d" % (G - 1)], ["ps%d" % b], True)
                                evac(t, b)
                        else:
                            for t in range(8):
                                b = 6 + db_rot[0]
                                db_rot[0] ^= 1
                                for q_ in range(G):
                                    mm(ps[:, b, :], actT[:, q_, t * 128:(t + 1) * 128], wd[:, q_, :], q_ == 0, q_ == G - 1,
                                       wk_ + ["act%d" % q_], ["ps%d" % b], q_ == G - 1)
                                evac(t, b)
                        if cb + 2 < 4:
                            load_wd(grp, cb + 2)

            gfv = wsf[:, 0, :]
            S.dma("sp", gfv, gfin, writes=["ws0"])
            for t in range(8):
                act(XNf[:, t % 2, :], x1[:, t, :], AF.Square,
                    ["x1_%d" % t], ["ws%d" % (2 + t % 2), "ssq"], accum_out=ssq[:, t:t + 1])
                rsqrt_from(rs[:, t:t + 1], ssq[:, t:t + 1], 1.0 / D, ["ssq"], ["rs"])
                stt("dve", x1[:, t, :], x1[:, t, :], rs[:, t:t + 1], gfv, ALU.mult, ALU.mult, ["x1_%d" % t, "rs", "ws0"], ["x1_%d" % t])
                S.dma("sp", y[t * 128:(t + 1) * 128, :], x1[:, t, :], reads=["x1_%d" % t], is_output=True)
            S.finish()
      except _Stop:
            S.finish()
    return nc


def _rope_tables(pos):
    n = len(pos)
    inv = (10000.0 ** (-np.arange(0, 32, 2, dtype=np.float32) / 32.0)).astype(np.float32)
    row = (pos // 64).astype(np.float32)
    col = (pos % 64).astype(np.float32)
    ang = np.concatenate([row[:, None] * inv[None, :], col[:, None] * inv[None, :]], axis=-1).astype(np.float32)
    cos = np.cos(ang).astype(np.float32)
    sin = np.sin(ang).astype(np.float32)
    cos2 = np.repeat(cos, 2, axis=1).T
    sin2 = np.repeat(sin, 2, axis=1).T.copy()
    sin2[0::2, :] *= -1.0
    return np.ascontiguousarray(cos2), np.ascontiguousarray(sin2)


_NC_CACHE = {}


def kernel(x, c, ctx, c_ctx, w_ada, b_ada, g_mix_norm, w_in, g_q_a, w_q_b, g_kv_a, w_kv_b,
           conv_w, conv_b, g_mix_out, w_out, g_ffn_norm, w_up, ffn_conv_w, ffn_conv_b, w_down, g_final):
    f = lambda a: np.ascontiguousarray(np.asarray(a, dtype=np.float32))
    x, c, ctx, c_ctx = f(x), f(c), f(ctx), f(c_ctx)
    w_ada, b_ada, w_in, w_q_b, w_kv_b, w_out, w_up, w_down = (f(w_ada[0]), f(b_ada[0]), f(w_in[0]), f(w_q_b[0]),
                                                              f(w_kv_b[0]), f(w_out[0]), f(w_up[0]), f(w_down[0]))
    g_mix_norm, g_q_a, g_kv_a, conv_w, conv_b, g_mix_out, g_ffn_norm, ffn_conv_w, ffn_conv_b = (
        f(g_mix_norm[0]), f(g_q_a[0]), f(g_kv_a[0]), f(conv_w[0]), f(conv_b[0]), f(g_mix_out[0]),
        f(g_ffn_norm[0]), f(ffn_conv_w[0]), f(ffn_conv_b[0]))
    g_final = f(g_final)
    return _prep_and_run(locals())


def _prep_and_run(v, stage=99, cores=None):
    (x, c, ctx, c_ctx, w_ada, b_ada, w_in, w_q_b, w_kv_b, w_out, w_up, w_down, g_mix_norm, g_q_a, g_kv_a, conv_w,
     conv_b, g_mix_out, g_ffn_norm, ffn_conv_w, ffn_conv_b, g_final) = [v[k] for k in (
        "x", "c", "ctx", "c_ctx", "w_ada", "b_ada", "w_in", "w_q_b", "w_kv_b", "w_out", "w_up", "w_down",
        "g_mix_norm", "g_q_a", "g_kv_a", "conv_w", "conv_b", "g_mix_out", "g_ffn_norm", "ffn_conv_w", "ffn_conv_b",
        "g_final")]
    if stage not in _NC_CACHE:
        _NC_CACHE[stage] = build_nc(stage)
    nc = _NC_CACHE[stage]

    pm = lambda v: np.ascontiguousarray(v.reshape(-1, 128).T)
    perm = np.arange(64).reshape(32, 2)[:, ::-1].reshape(-1)
    w_krs = np.ascontiguousarray(w_in[:, 768:832][:, perm])
    qcols = np.concatenate([h * 192 + 128 + perm for h in range(8)])
    w_q_sw = np.ascontiguousarray(w_q_b[:, qcols])
    ident = np.eye(128, dtype=np.float32)
    in_maps = []
    for core in range(8):
        b, half = core // 2, core % 2
        rev = half == 1
        if not rev:
            own = np.arange(0, 1024)
            halo = np.array([1024, 1025])
            oth = np.arange(1024, 2048)
        else:
            own = np.arange(2047, 1023, -1)
            halo = np.array([1023, 1022])
            oth = np.arange(0, 1024)
        win = np.concatenate([own, halo])
        xw = np.ascontiguousarray(x[b][win])
        xo = np.ascontiguousarray(np.concatenate([x[b][oth], ctx[b]], axis=0))
        cs = np.ascontiguousarray(np.stack([c[b], c_ctx], axis=-1).reshape(16, 128, 2).transpose(1, 0, 2))
        vec = np.zeros((128, 512), np.float32)
        vec[:, 0:16] = pm(g_mix_norm)
        vec[:, 16:32] = pm(g_ffn_norm)
        vec[:, 32:36] = pm(g_q_a)
        vec[:, 36:38] = pm(g_kv_a)
        cw = conv_w[::-1] if rev else conv_w
        vec[:, 38:62] = cw.reshape(3, 8, 128).transpose(2, 1, 0).reshape(128, 24)
        vec[:, 62:70] = pm(conv_b)
        vec[:, 70:86] = pm(g_mix_out)
        fw = ffn_conv_w[::-1] if rev else ffn_conv_w
        vec[:, 86:344] = fw.reshape(3, 86, 128).transpose(2, 1, 0).reshape(128, 258)
        vec[:, 344:430] = pm(ffn_conv_b)
        for si_, sg_ in enumerate((0, 1, 3, 4)):
            vec[:, 430 + si_ * 16:430 + si_ * 16 + 16] = pm(b_ada[sg_ * 2048:(sg_ + 1) * 2048])
        cw_, sw_ = _rope_tables(win)
        ropew = np.zeros((64, 2, BW), np.float32)
        ropew[:, 0, 1:1027] = cw_
        ropew[:, 1, 1:1027] = sw_
        co_, so_ = _rope_tables(oth)
        ropeo = np.ascontiguousarray(np.stack([co_, so_], axis=1))
        in_maps.append({
            "xw": xw, "xo": xo, "cs": cs, "w_ada": w_ada, "b_ada": b_ada.reshape(1, -1), "w_in": w_in,
            "w_krs": w_krs, "w_q_b": w_q_b, "w_q_sw": w_q_sw, "w_kv_b": w_kv_b, "w_out": w_out, "w_up": w_up,
            "w_down": w_down, "vec": vec, "gfin": np.ascontiguousarray(np.broadcast_to(g_final, (128, D))),
            "ropew": ropew, "ropeo": ropeo, "ident": ident.astype(ml_dtypes.bfloat16), "identf": ident,
        })
    if cores is not None:
        res = run_bass_kernel_spmd(nc, [in_maps[i] for i in cores], core_ids=list(range(len(cores))))
        return res
    res = run_bass_kernel_spmd(nc, in_maps, core_ids=list(range(8)))
    out = np.zeros((4, 2048, D), np.float32)
    for core in range(8):
        b, half = core // 2, core % 2
        yv = res.results[core]["y"]
        if half == 0:
            out[b, 0:1024] = yv
        else:
            out[b, 1024:2048] = yv[::-1]
    return out
```

```python
import numpy as np
import ml_dtypes
from contextlib import ExitStack
import concourse.bass as bass
import concourse.mybir as mybir
from concourse.bass_utils import run_bass_kernel_spmd

F32 = mybir.dt.float32
BF16 = mybir.dt.bfloat16
AF = mybir.ActivationFunctionType
ALU = mybir.AluOpType

D = 2048
KC = 16
T = 1024
BW = 1028
NKEY = 2304
FFN = 5504
NJ = 43
EPS = 1e-6
PC = [(1, 342), (343, 342), (685, 342)]
QP = [(1, 384, [0, 1, 2]), (385, 384, [3, 4, 5]), (769, 258, [6, 7, 8])]
SCALE = 192.0 ** -0.5


class Sched:
    def __init__(self, nc, stack, n_dma_sems=40):
        self.nc = nc
        self.eng = {"pe": nc.tensor, "act": nc.scalar, "dve": nc.vector, "pool": nc.gpsimd, "sp": nc.sync}
        self.sem = {}
        self.cnt = {}
        for e in ("pe", "act", "dve", "pool"):
            self.sem[e] = stack.enter_context(nc.semaphore("s_" + e))
            self.cnt[e] = 0
        self.dsem = [stack.enter_context(nc.semaphore("d%d" % i)) for i in range(n_dma_sems)]
        self.dcnt = [0] * n_dma_sems
        self.dnext = 0
        self.dnext_pool = 0
        self.known = {e: {} for e in self.eng}
        self.last_w = {}
        self.readers = {}
        self.out_events = []

    def _deps(self, reads, writes):
        deps = {}

        def add(ev):
            if ev is None:
                return
            s, v = ev
            k = id(s)
            if k not in deps or deps[k][1] < v:
                deps[k] = (s, v)

        for r in reads:
            add(self.last_w.get(r))
        for w in writes:
            add(self.last_w.get(w))
            for ev in self.readers.get(w, {}).values():
                add(ev)
        return deps

    def _wait(self, e, deps, skip_own=False):
        kn = self.known[e]
        own = self.sem.get(e)
        for k, (s, v) in deps.items():
            if skip_own and own is not None and s is own:
                continue
            if kn.get(k, 0) >= v:
                continue
            self.eng[e].wait_ge(s, v)
            kn[k] = v

    def _record(self, ev, reads, writes):
        s, v = ev
        for r in reads:
            self.readers.setdefault(r, {})[id(s)] = (s, v)
        for w in writes:
            self.last_w[w] = ev
            self.readers[w] = {}

    def op(self, e, fn, reads=(), writes=(), inc=True):
        psr = [r for r in reads if r.startswith("ps")]
        if psr:
            writes = list(writes) + psr
            reads = [r for r in reads if not r.startswith("ps")]
        deps = self._deps(reads, writes)
        self._wait(e, deps, skip_own=(e == "pe"))
        ins = fn()
        if inc:
            self.cnt[e] += 1
            ins.then_inc(self.sem[e], 1)
            ev = (self.sem[e], self.cnt[e])
        else:
            ev = (self.sem[e], self.cnt[e] + 1)
        self._record(ev, reads, writes)
        return ins

    def dma(self, q, out, in_, reads=(), writes=(), is_output=False):
        deps = self._deps(reads, writes)
        self._wait(q, deps)
        half = len(self.dsem) // 2
        if q == "pool":
            i = half + self.dnext_pool
            self.dnext_pool = (self.dnext_pool + 1) % half
        else:
            i = self.dnext
            self.dnext = (self.dnext + 1) % half
        s = self.dsem[i]
        if self.dcnt[i] > 0:
            self._wait(q, {id(s): (s, self.dcnt[i])})
        self.dcnt[i] += 16
        self.eng[q].dma_start(out=out, in_=in_).then_inc(s, 16)
        ev = (s, self.dcnt[i])
        self._record(ev, reads, writes)
        if is_output:
            self.out_events.append(ev)
        return ev

    def barrier(self):
        deps = {}
        for e in ("pe", "act", "dve", "pool"):
            if self.cnt[e] > 0:
                deps[id(self.sem[e])] = (self.sem[e], self.cnt[e])
        for i, s in enumerate(self.dsem):
            if self.dcnt[i] > 0:
                deps[id(s)] = (s, self.dcnt[i])
        for e in self.eng:
            self._wait(e, deps)

    def finish(self):
        deps = {}
        for (s, v) in self.out_events:
            k = id(s)
            if k not in deps or deps[k][1] < v:
                deps[k] = (s, v)
        self._wait("sp", deps)


class _Stop(Exception):
    pass


def build_nc(stage=99):
    nc = bass.Bass("TRN2", target_bir_lowering=False)

    def din(name, shape, dt=F32):
        return nc.dram_tensor(name, list(shape), dt, kind="ExternalInput").ap()

    xw = din("xw", [1026, D])
    xo = din("xo", [1280, D])
    cs = din("cs", [128, KC, 2])
    w_ada = din("w_ada", [D, 6 * D])
    b_ada = din("b_ada", [1, 6 * D])
    w_in = din("w_in", [D, 3904])
    w_krs = din("w_krs", [D, 64])
    w_q_b = din("w_q_b", [512, 1536])
    w_q_sw = din("w_q_sw", [512, 512])
    w_kv_b = din("w_kv_b", [256, 2048])
    w_out = din("w_out", [D, D])
    w_up = din("w_up", [D, 2 * FFN])
    w_down = din("w_down", [FFN, D])
    vec = din("vec", [128, 512])
    gfin = din("gfin", [128, D])
    ropew = din("ropew", [64, 2, BW])
    ropeo = din("ropeo", [64, 2, T])
    ident_d = din("ident", [128, 128], BF16)
    identf_d = din("identf", [128, 128])
    y = nc.dram_tensor("y", [T, D], F32, kind="ExternalOutput").ap()
    dbg = None
    if stage != 99:
        dbg = nc.dram_tensor("dbg", [128, 16384], F32, kind="ExternalOutput").ap()

    w_ada_v = w_ada.rearrange("(k p) n -> p k n", p=128)
    w_in_v = w_in.rearrange("(k p) n -> p k n", p=128)
    w_krs_v = w_krs.rearrange("(k p) n -> p k n", p=128)
    w_q_b_v = w_q_b.rearrange("(k p) n -> p k n", p=128)
    w_q_sw_v = w_q_sw.rearrange("(k p) n -> p k n", p=128)
    w_kv_b_v = w_kv_b.rearrange("(k p) n -> p k n", p=128)
    w_out_v = w_out.rearrange("(k p) n -> p k n", p=128)
    w_up_v = w_up.rearrange("(k p) n -> p k n", p=128)
    w_down_v = w_down.rearrange("(j p) n -> p j n", p=128)

    with ExitStack() as st:
      S = Sched(nc, st)
      try:

            def sb(name, shape, dt):
                return st.enter_context(nc.sbuf_tensor(name, list(shape), dt))

            def ps_dump(b):
                S.barrier()
                cp("dve", gt_bc[:, 512 * (b % 4):512 * (b % 4) + 512], ps[:, b, :], [], [])
                return gt_bc[:, 512 * (b % 4):512 * (b % 4) + 512]

            def checkpoint(k, items):
                if stage != k:
                    return
                if callable(items):
                    items = items()
                S.barrier()
                for ap, col0 in items:
                    S.dma("sp", dbg[0:ap.shape[0], col0:col0 + ap.shape[1]], ap, is_output=True)
                raise _Stop()

            BIG = sb("BIG", [128, 16384], F32)
            P1 = sb("P1", [128, KC * BW], BF16)
            P2 = sb("P2", [128, KC * BW], BF16)
            WS = sb("WS", [128, 6, 4096], BF16)
            gt_bc = sb("gt_bc", [128, D], F32)
            modrow_t = sb("modrow", [128, 2064], F32)
            modrow = modrow_t[0:3, 0:D]
            rawb = [modrow_t[:, r * 1032:(r + 1) * 1032].rearrange("p (a n) -> p a n", a=3) for r in range(2)]
            ident = sb("identb", [128, 128], BF16)
            identf = sb("identf32", [128, 128], F32)
            ones128 = sb("ones128", [128, 128], F32)
            ones256 = sb("ones256", [128, 128], F32)
            ones512 = sb("ones512", [128, 128], F32)
            ones1 = sb("ones1", [128, 128], F32)
            onesb = sb("onesb", [128, 128], BF16)
            vecs = sb("vecs", [128, 512], F32)
            sel3 = sb("sel3", [3, 128], F32)
            rhs3 = sb("rhs3", [3, 2], F32)
            epst = sb("epst", [128, 1], F32)
            cst = sb("cst", [128, KC, 2], F32)
            sT = sb("sT", [128, KC, 2], BF16)
            modT = sb("modT", [128, KC, 2], F32)
            AB = sb("AB", [128, 6, KC], F32)
            gtfT = sb("gtfT", [128, KC], F32)
            ssq = sb("ssq", [128, 16], F32)
            rs = sb("rs", [128, 16], F32)
            ost = sb("ost", [128, 4, 9], F32)
            rsc = sb("rsc", [64, 2, 342], F32)
            tmpb = sb("tmpb", [128, 2, 512], F32)
            ps = st.enter_context(nc.psum_tensor("ps", [128, 8, 512], F32))

            def carve(arena, col0, parts, shape, dt):
                n = 1
                for s_ in shape:
                    n *= s_
                nf = n if dt == F32 else n // 2
                v = arena[0:parts, col0:col0 + nf]
                if dt != F32:
                    v = v.bitcast(dt)
                if len(shape) == 2:
                    v = v.rearrange("p (a b) -> p a b", a=shape[0])
                elif len(shape) == 3:
                    v = v.rearrange("p (a b c) -> p a b c", a=shape[0], b=shape[1])
                return v

            P1f = P1[:].bitcast(F32)
            P2f = P2[:].bitcast(F32)
            hT = P1[:].rearrange("p (c n) -> p c n", c=KC)
            mixT = P2[:].rearrange("p (c n) -> p c n", c=KC)
            XN = carve(P2f, 0, 128, [4, D], BF16)
            kvnT = carve(BIG, 0, 128, [2, NKEY], BF16)
            krT = carve(BIG, 2304, 64, [NKEY], BF16)
            krT128 = carve(BIG, 2304, 128, [NKEY], BF16)
            cosw = BIG[0:64, 3456:3456 + BW]
            sinw = BIG[0:64, 4484:4484 + BW]
            coso = BIG[0:64, 5512:5512 + T]
            sino = BIG[0:64, 6536:6536 + T]
            qnT = carve(BIG, 7560, 128, [4, BW], BF16)
            XT = carve(BIG, 9616, 128, [2, D], F32)
            PT = carve(BIG, 13712, 128, [4, 384], BF16)
            osb = carve(BIG, 14480, 128, [9, 129], F32)
            mixn = carve(BIG, 15641, 128, [9, 128], BF16)
            x1 = BIG[:].rearrange("p (t n) -> p t n", t=8)
            scr = [P2f[:, i * BW:(i + 1) * BW] for i in range(4)] + \
                  [BIG[:, 9616 + i * BW:9616 + (i + 1) * BW] for i in range(3)]
            x1h = P1f[0:1, 6176:6176 + D]
            xh = P1f[0:1, 4000:4000 + D]
            Kh = [carve(P1f, 0 + i * 1152, 128, [NKEY], BF16) for i in range(2)]
            Vh = [carve(P1f, 2304 + i * 1161, 128, [18, 129], BF16) for i in range(2)]
            qhn = [carve(P1f, 4626 + i * 514, 128, [BW], BF16) for i in range(2)]
            qhr = [carve(P1f, 5654 + i * 514, 64, [BW], BF16) for i in range(2)]
            qhr128 = [carve(P1f, 5654 + i * 514, 128, [BW], BF16) for i in range(2)]
            actT = carve(P2f, 0, 128, [8, T], BF16)
            cbuf = [P2f[:, 4096 + i * 1026:4096 + (i + 1) * 1026] for i in range(4)]
            wsf = WS[:].bitcast(F32)
            XNf = WS[:, 2:4, 0:D]
            scrB = [P2f[:, 6160 + i * 512:6160 + (i + 1) * 512] for i in range(4)]

            def ws_view(h0, nh, k, n):
                v = WS[:, h0:h0 + nh, :].rearrange("p h n -> p (h n)")[:, 0:k * n]
                return v.rearrange("p (k n) -> p k n", k=k)

            def wkeys(h0, nh):
                return ["ws%d" % i for i in range(h0, h0 + nh)]

            def mm(out, lhsT, rhs, start, stop, reads, writes, inc):
                S.op("pe", lambda: nc.tensor.matmul(out, lhsT=lhsT, rhs=rhs, start=start, stop=stop),
                     reads, writes, inc)

            def act(out, in_, func, reads, writes, **kw):
                S.op("act", lambda: nc.scalar.activation(out=out, in_=in_, func=func, **kw), reads, writes)

            def tt(e, out, in0, in1, op, reads, writes):
                eng = nc.vector if e == "dve" else nc.gpsimd
                S.op(e, lambda: eng.tensor_tensor(out=out, in0=in0, in1=in1, op=op), reads, writes)

            def stt(e, out, in0, scalar, in1, op0, op1, reads, writes):
                e = "dve"
                eng = nc.vector
                S.op(e, lambda: eng.scalar_tensor_tensor(out=out, in0=in0, scalar=scalar, in1=in1, op0=op0, op1=op1),
                     reads, writes)

            def ts(e, out, in0, s1, s2, op0, op1, reads, writes):
                eng = nc.vector if e == "dve" else nc.gpsimd
                if op1 is None:
                    S.op(e, lambda: eng.tensor_scalar(out=out, in0=in0, scalar1=s1, scalar2=None, op0=op0), reads, writes)
                else:
                    S.op(e, lambda: eng.tensor_scalar(out=out, in0=in0, scalar1=s1, scalar2=s2, op0=op0, op1=op1),
                         reads, writes)

            def cp(e, out, in_, reads, writes):
                if e == "act":
                    S.op("act", lambda: nc.scalar.copy(out=out, in_=in_), reads, writes)
                else:
                    eng = nc.vector if e == "dve" else nc.gpsimd
                    S.op(e, lambda: eng.tensor_copy(out=out, in_=in_), reads, writes)

            def memset(e, ap, val, writes):
                eng = nc.vector if e == "dve" else nc.gpsimd
                S.op(e, lambda: eng.memset(ap, val), (), writes)

            def rsqrt_from(out, in_, scale, reads, writes):
                np_ = out.shape[0]
                act(out, in_, AF.Ln, reads + ["epst"], writes, scale=scale, bias=epst[0:np_, :])
                act(out, out, AF.Exp, writes, writes, scale=-0.5)

            V_GMN, V_GFN, V_GQA, V_GKV, V_CW, V_CB, V_GMO, V_FCW, V_FCB = 0, 16, 32, 36, 38, 62, 70, 86, 344
            V_BADA = 430

            S.dma("sp", ident[:], ident_d, writes=["ident"])
            S.dma("sp", identf[:], identf_d, writes=["identf"])
            S.dma("sp", vecs[:], vec, writes=["vecs"])
            S.dma("sp", cst[:], cs, writes=["cst"])
            S.dma("sp", BIG[0:64, 3456:3456 + 2 * BW].rearrange("p (a n) -> p a n", a=2), ropew, writes=["rope"])
            S.dma("sp", BIG[0:64, 5512:5512 + 2 * T].rearrange("p (a n) -> p a n", a=2), ropeo, writes=["rope"])
            memset("dve", ones128[:], 1.0 / 128, ["ones"])
            memset("dve", ones256[:], 1.0 / 256, ["ones"])
            memset("dve", ones512[:], 1.0 / 512, ["ones"])
            memset("dve", ones1[:], 1.0, ["ones"])
            memset("dve", onesb[:], 1.0, ["ones"])
            memset("dve", epst[:], EPS, ["epst"])
            memset("dve", sel3[:], 1.0, ["sel3"])
            memset("dve", rhs3[:], 1.0, ["rhs3"])
            S.op("dve", lambda: nc.vector.tensor_copy(out=rhs3[0:2, 0:2], in_=identf[0:2, 0:2]), ["identf"], ["rhs3"])
            S.op("dve", lambda: nc.vector.tensor_copy(out=sel3[0:2, :], in_=identf[0:2, 0:1].to_broadcast([2, 128])),
                 ["identf"], ["sel3"])
            act(sT[:], cst[:], AF.Silu, ["cst"], ["sT"])
            memset("pool", krT128[64:128, :], 0.0, ["krT"])

            ws_rot = [0]

            ws_allowed = [[0, 2, 4]]

            def next_pair():
                al = ws_allowed[0]
                ws_rot[0] = (ws_rot[0] + 1) % len(al)
                return al[ws_rot[0]]

            ada_bank = [0]

            def ada_seg(seg):
                for blk in range(4):
                    h0 = next_pair()
                    wv = ws_view(h0, 2, KC, 512)
                    c0 = seg * D + blk * 512
                    S.dma("pool", wv, w_ada_v[:, :, c0:c0 + 512], writes=wkeys(h0, 2))
                    b = ada_bank[0]
                    ada_bank[0] ^= 1
                    for k in range(KC):
                        mm(ps[0:2, b, :], sT[:, k, :], wv[:, k, :], k == 0, k == KC - 1,
                           wkeys(h0, 2) + ["sT"], ["ps%d" % b], k == KC - 1)
                    cp("dve", modrow[0:2, blk * 512:(blk + 1) * 512], ps[0:2, b, :], ["ps%d" % b], ["modrow"])

            ada_tasks = [(seg_, hb_) for seg_ in (2, 3, 4, 5) for hb_ in range(8)]
            ada_pending = []
            ada_half = [0]
            ada_misc = [False]

            def ada_issue():
                if not ada_tasks:
                    return
                seg, hb = ada_tasks.pop(0)
                h = 0 + ada_half[0]
                ada_half[0] ^= 1
                wv = ws_view(h, 1, KC, 256)
                c0 = seg * D + hb * 256
                S.dma("pool", wv, w_ada_v[:, :, c0:c0 + 256], writes=wkeys(h, 1))
                ada_pending.append((seg, hb, wv, wkeys(h, 1)))

            def ada_step():
                if not ada_pending:
                    return
                seg, hb, wv, wk_ = ada_pending.pop(0)
                if hb == 0 and seg in (2, 5):
                    S.dma("sp", modrow[2:3, :], b_ada[0:1, seg * D:(seg + 1) * D], writes=["modrow"])
                b = misc_bank()
                for k in range(KC):
                    mm(ps[0:2, b, 0:256], sT[:, k, :], wv[:, k, :], k == 0, k == KC - 1,
                       wk_ + ["sT"], ["ps%d" % b], k == KC - 1)
                cp("dve", modrow[0:2, hb * 256:(hb + 1) * 256], ps[0:2, b, 0:256], ["ps%d" % b], ["modrow"])
                ada_issue()
                if hb == 7:
                    ada_misc[0] = True
                    if seg == 2:
                        ada_bc()
                    elif seg == 3:
                        ada_T(misc_bank(), 3)
                        cp("dve", AB[:, 5, :], modT[:, :, 0], ["modT"], ["AB"])
                    elif seg == 4:
                        ada_T(misc_bank(), 4)
                        ada_AB(V_GFN, 4, None)
                    ada_misc[0] = False

            def ada_T(bank=2, seg=0):
                pv = ps[:, bank, 0:32].rearrange("p (c n) -> p c n", c=KC)
                for c in range(KC):
                    mm(pv[:, c, :], modrow[0:2, c * 128:(c + 1) * 128], rhs3[0:2, :], True, True,
                       ["modrow", "rhs3"], ["ps%d" % bank], c == KC - 1)
                si = {0: 0, 1: 1, 3: 2, 4: 3}[seg]
                bcol = vecs[:, V_BADA + si * 16:V_BADA + si * 16 + 16]
                for j_ in range(2):
                    tt("dve", modT[:, :, j_], pv[:, :, j_], bcol, ALU.add, ["ps%d" % bank, "vecs"], ["modT"])

            def ada_bc():
                for i in range(4):
                    b = misc_bank() if ada_misc[0] else 2 + (i % 2)
                    mm(ps[:, b, :], sel3[:], modrow[0:3, i * 512:(i + 1) * 512], True, True,
                       ["modrow", "sel3"], ["ps%d" % b], True)
                    cp("dve", gt_bc[:, i * 512:(i + 1) * 512], ps[:, b, :], ["ps%d" % b], ["gt_bc"])

            def ada_AB(gcol, ia, ictx):
                ts("dve", AB[:, ia, :], modT[:, :, 0], 1.0, None, ALU.add, None, ["modT"], ["AB"])
                tt("dve", AB[:, ia, :], AB[:, ia, :], vecs[:, gcol:gcol + KC], ALU.mult, ["AB", "vecs"], ["AB"])
                if ictx is not None:
                    ts("dve", AB[:, ictx, :], modT[:, :, 1], 1.0, None, ALU.add, None, ["modT"], ["AB"])
                    tt("dve", AB[:, ictx, :], AB[:, ictx, :], vecs[:, gcol:gcol + KC], ALU.mult, ["AB", "vecs"], ["AB"])


            tp_rot = [0]
            tp_banks = [[5, 6, 7]]

            def next_tp():
                al = tp_banks[0]
                tp_rot[0] = (tp_rot[0] + 1) % len(al)
                return al[tp_rot[0]]

            def norm_to_T(tiles, iA, iB, dst_fn, dst_keys_fn, gcol_keys=()):
                n = len(tiles)
                rows = tiles[0][1]
                for i, (ap, r, keys) in enumerate(tiles):
                    act(XN[0:r, i, :], ap, AF.Square, keys, ["xn%d" % i, "ssq"], accum_out=ssq[0:r, i:i + 1])
                rsqrt_from(rs[0:rows, 0:n], ssq[0:rows, 0:n], 1.0 / D, ["ssq"], ["rs"])
                for i, (ap, r, keys) in enumerate(tiles):
                    act(XN[0:r, i, :], ap, AF.Copy, keys + ["rs"], ["xn%d" % i], scale=rs[0:r, i:i + 1])
                ncols = sum(t[1] for t in tiles)
                for c in range(KC):
                    bank = next_tp()
                    q = bank
                    tpv = ps[:, bank, :].bitcast(BF16)[:, 0:512]
                    col = 0
                    for i, (ap, r, keys) in enumerate(tiles):
                        S.op("pe", lambda i=i, r=r, col=col: nc.tensor.transpose(
                            tpv[:, col:col + r], XN[0:r, i, c * 128:(c + 1) * 128], ident[0:r, 0:r]),
                            ["xn%d" % i, "ident"], ["ps%d" % q], i == n - 1)
                        col += r
                    dst = dst_fn(c)
                    if c % 2 == 0:
                        act(dst, tpv[:, 0:ncols], AF.Identity, ["ps%d" % q, "AB"], dst_keys_fn(c),
                            scale=AB[:, iA, c:c + 1], bias=AB[:, iB, c:c + 1])
                    else:
                        ts("dve", dst, tpv[:, 0:ncols], AB[:, iA, c:c + 1], AB[:, iB, c:c + 1], ALU.mult, ALU.add,
                           ["ps%d" % q, "AB"], dst_keys_fn(c))

            def kv_project(hsrc, hkeys, pieces, wkv, wk, key_cols, rope, scr_sq, scr_rstd, scr_t):
                outs = [("kva0", 0, 128), ("kva1", 128, 128), ("krA", 256, 64), ("krB", 320, 64)]
                if rope is None:
                    outs = outs[:3]
                np_ = len(pieces)
                banks = {}
                for oi, (nm, wc, M) in enumerate(outs):
                    for pi, (c0, n) in enumerate(pieces):
                        b = (oi * np_ + pi)
                        banks[(nm, pi)] = b
                        for k in range(KC):
                            mm(ps[0:M, b, 0:n], wkv[:, k, wc:wc + M], hsrc(k, c0, n), k == 0, k == KC - 1,
                               wk + hkeys(k), ["ps%d" % b], k == KC - 1)
                return banks


            xt_rot = [0]

            def load_x(src_ap, rows):
                i = xt_rot[0]
                xt_rot[0] ^= 1
                S.dma("sp", XT[0:rows, i, :], src_ap, writes=["xt%d" % i])
                return (XT[0:rows, i, :], rows, ["xt%d" % i])

            hTc = carve(P2f, 4112, 128, [KC, 256], BF16)

            def gbuf(g, c, ncols):
                if g < 2:
                    return hT[:, c, g * 512:g * 512 + ncols]
                return hTc[:, c, 0:ncols]

            sq_s = scr[2]
            for g in range(3):
                ntile = 4 if g < 2 else 2
                ncols = ntile * 128
                iA, iB = (0, 1) if g < 2 else (2, 3)
                tiles = []
                n = ntile
                for i in range(n):
                    tl = load_x(xo[(g * 4 + i) * 128:(g * 4 + i + 1) * 128, :], 128)
                    tiles.append(tl)
                    S.op("dve", lambda i=i, tl=tl: nc.vector.scalar_tensor_tensor(
                        out=XN[:, i, :], in0=tl[0], scalar=1.0, in1=tl[0], op0=ALU.mult, op1=ALU.mult,
                        accum_out=ssq[:, i:i + 1]), tl[2], ["xn%d" % i, "ssq"])
                    rsqrt_from(rs[:, i:i + 1], ssq[:, i:i + 1], 1.0 / D, ["ssq"], ["rs"])
                    act(XN[:, i, :], tl[0], AF.Copy, tl[2] + ["rs"], ["xn%d" % i], scale=rs[:, i:i + 1])
                for c in range(KC):
                    bank = next_tp()
                    q = bank
                    tpv = ps[:, bank, :].bitcast(BF16)[:, 0:512]
                    for i in range(n):
                        S.op("pe", lambda i=i, c=c, tpv=tpv: nc.tensor.transpose(
                            tpv[:, i * 128:(i + 1) * 128], XN[:, i, c * 128:(c + 1) * 128], ident[:]),
                            ["xn%d" % i, "ident"], ["ps%d" % q], i == n - 1)
                    hk = "hTB%d_%d" % (c, g)
                    dst = gbuf(g, c, ncols)
                    cp("act" if c % 2 == 0 else "dve", dst, tpv[:, 0:ncols], ["ps%d" % q], [hk])

            ada_seg(0)
            ada_T(2, 0)
            cp("dve", AB[:, 1, :], modT[:, :, 0], ["modT"], ["AB"])
            cp("dve", AB[:, 3, :], modT[:, :, 1], ["modT"], ["AB"])
            ada_seg(1)
            ada_T(2, 1)
            ada_AB(V_GMN, 0, 2)
            checkpoint(1, [(AB[:].rearrange("p a b -> p (a b)"), 0), (modT[:].rearrange("p a b -> p (a b)"), 96)])
            h0kv = 0
            wkv = ws_view(0, 2, KC, 384)
            ws_allowed[0] = [2, 4]
            ws_rot[0] = 0
            S.dma("pool", wkv[:, :, 0:320], w_in_v[:, :, 512:832], writes=wkeys(0, 2))
            S.dma("pool", wkv[:, :, 320:384], w_krs_v, writes=wkeys(0, 2))
            WKV = wkeys(0, 2)
            for g in range(3):
                ncols = 512 if g < 2 else 256
                iA, iB = (0, 1) if g < 2 else (2, 3)
                for c in range(KC):
                    hk = "hTB%d_%d" % (c, g)
                    dst = gbuf(g, c, ncols)
                    if c % 2 == 0:
                        act(dst, dst, AF.Identity, [hk, "AB"], [hk], scale=AB[:, iA, c:c + 1], bias=AB[:, iB, c:c + 1])
                    else:
                        ts("dve", dst, dst, AB[:, iA, c:c + 1], AB[:, iB, c:c + 1], ALU.mult, ALU.add, [hk, "AB"], [hk])

            for g in range(3):
                ntile = 4 if g < 2 else 2
                ncols = ntile * 128
                rope = g < 2
                outs = [(0, 128, 0), (128, 128, 1), (256, 64, 2)] + ([(320, 64, 3)] if rope else [])
                for (wc, M, b) in outs:
                    for k in range(KC):
                        mm(ps[0:M, b, 0:ncols], wkv[:, k, wc:wc + M], gbuf(g, k, ncols), k == 0, k == KC - 1,
                           WKV + ["hTB%d_%d" % (k, g)], ["ps%d" % b], k == KC - 1)
                if g == 0:
                    checkpoint(22, lambda: [(ps_dump(0), 0), (ps_dump(1), 512), (ps_dump(2), 1024), (ps_dump(3), 1536)])
                kc0 = g * 512
                for c in range(2):
                    sqb = scrB[c]
                    act(sqb[:, 0:ncols], ps[:, c, 0:ncols], AF.Square, ["ps%d" % c], ["scrB%d" % c])
                    mm(ps[:, 4, 0:ncols], ones256[:], sqb[:, 0:ncols], c == 0, c == 1, ["scrB%d" % c, "ones"], ["ps4"], c == 1)
                rsqrt_from(scrB[2][:, 0:ncols], ps[:, 4, 0:ncols], 1.0, ["ps4"], ["scrB2"])
                for c in range(2):
                    stt("dve", kvnT[:, c, kc0:kc0 + ncols], ps[:, c, 0:ncols], vecs[:, V_GKV + c:V_GKV + c + 1],
                        scrB[2][:, 0:ncols], ALU.mult, ALU.mult, ["ps%d" % c, "scrB2", "vecs"], ["kvnT"])
                if g == 0:
                    checkpoint(23, [(BIG[:, 0:2304], 0), (scrB[2], 2304)])
                if rope:
                    t1 = scrB[3][0:64, 0:ncols]
                    t2 = scrB[0][0:64, 0:ncols]
                    tt("dve", t1, ps[0:64, 2, 0:ncols], coso[:, g * 512:g * 512 + ncols], ALU.mult,
                       ["ps2", "rope"], ["scrB3"])
                    tt("dve", t2, ps[0:64, 3, 0:ncols], sino[:, g * 512:g * 512 + ncols], ALU.mult,
                       ["ps3", "rope"], ["scrB0"])
                    tt("pool", krT[:, kc0:kc0 + ncols], t1, t2, ALU.add, ["scrB3", "scrB0"], ["krT"])
                else:
                    cp("act", krT[:, kc0:kc0 + ncols], ps[0:64, 2, 0:ncols], ["ps2"], ["krT"])

            checkpoint(2, [(BIG[:, 0:2304], 0), (BIG[0:64, 2304:3456], 2304)])
            tp_banks[0] = [4, 5, 6, 7]
            wq = ws_view(2, 2, KC, 512)
            S.dma("pool", wq, w_in_v[:, :, 0:512], writes=wkeys(2, 2))
            WQ = wkeys(2, 2)
            conv_w_q = {}
            for which_ in range(2):
                h_ = (4, 5)[which_]
                base_ = (1856, 2880)[which_]
                wv__ = ws_view(h_, 1, KC, 256)
                S.dma("pool", wv__, w_in_v[:, :, base_:base_ + 256], writes=wkeys(h_, 1))
                conv_w_q[(0, which_)] = (wv__, wkeys(h_, 1))
            for c in range(KC):
                memset("pool", hT[:, c, 0:1], 0.0, ["hT%d" % c, "hTB%d_0" % c, "hTB%d_1" % c, "hTB%d_2" % c])
                memset("pool", hT[:, c, 1027:1028], 0.0, ["hT%d" % c])
            for g in range(3):
                if g < 2:
                    tiles = []
                    for i in range(4):
                        t_ = g * 4 + i
                        tl = load_x(xw[t_ * 128:(t_ + 1) * 128, :], 128)
                        tiles.append(tl)
                        S.op("dve", lambda i=i, tl=tl: nc.vector.scalar_tensor_tensor(
                        out=XN[:, i, :], in0=tl[0], scalar=1.0, in1=tl[0], op0=ALU.mult, op1=ALU.mult,
                        accum_out=ssq[:, i:i + 1]), tl[2], ["xn%d" % i, "ssq"])
                        rsqrt_from(rs[:, i:i + 1], ssq[:, i:i + 1], 1.0 / D, ["ssq"], ["rs"])
                        act(XN[:, i, :], tl[0], AF.Copy, tl[2] + ["rs"], ["xn%d" % i], scale=rs[:, i:i + 1])
                    rws = [128] * 4
                else:
                    tl = load_x(xw[1024:1026, :], 2)
                    act(XN[0:2, 0, :], tl[0], AF.Square, tl[2], ["xn0", "ssq"], accum_out=ssq[0:2, 0:1])
                    rsqrt_from(rs[0:2, 0:1], ssq[0:2, 0:1], 1.0 / D, ["ssq"], ["rs"])
                    act(XN[0:2, 0, :], tl[0], AF.Copy, tl[2] + ["rs"], ["xn0"], scale=rs[0:2, 0:1])
                    rws = [2]
                ncols = sum(rws)
                col0 = 1 + g * 512
                for c in range(KC):
                    bank = next_tp()
                    q = bank
                    tpv = ps[:, bank, :].bitcast(BF16)[:, 0:512]
                    for i, r in enumerate(rws):
                        S.op("pe", lambda i=i, c=c, r=r, tpv=tpv: nc.tensor.transpose(
                            tpv[:, i * 128:i * 128 + r], XN[0:r, i, c * 128:(c + 1) * 128], ident[0:r, 0:r]),
                            ["xn%d" % i, "ident"], ["ps%d" % q], i == len(rws) - 1)
                    dst = hT[:, c, col0:col0 + ncols]
                    hks = ["hT%d" % c, "hTB%d_0" % c, "hTB%d_1" % c, "hTB%d_2" % c]
                    if c % 2 == 0:
                        act(dst, tpv[:, 0:ncols], AF.Identity, ["ps%d" % q, "AB"], hks,
                            scale=AB[:, 0, c:c + 1], bias=AB[:, 1, c:c + 1])
                    else:
                        ts("dve", dst, tpv[:, 0:ncols], AB[:, 0, c:c + 1], AB[:, 1, c:c + 1], ALU.mult, ALU.add,
                           ["ps%d" % q, "AB"], hks)

            checkpoint(31, [(P1f, 0)])
            HK = ["hT%d" % k for k in range(KC)]
            ob_rot = [0]

            def proj3(wv, wc, M, wk):
                o = ob_rot[0]
                ob_rot[0] ^= 1
                keys = ["ps%d" % (o * 3 + i) for i in range(3)]
                for pi, (c0, n) in enumerate(PC):
                    b = o * 3 + pi
                    for k in range(KC):
                        mm(ps[0:M, b, 0:n], wv[:, k, wc:wc + M], hT[:, k, c0:c0 + n], k == 0, k == KC - 1,
                           wk + ["hT%d" % k], ["ps%d" % b], k == KC - 1)
                return ps[0:M, o * 3:o * 3 + 3, 0:342], keys

            def v3(ap2d):
                return ap2d.rearrange("p (a n) -> p a n", a=3)

            st_rot = [0]

            def stats3(srcs, ones_m, dst_rstd, dst_key, sq_scr, sq_key):
                for pi in range(3):
                    b = 6 + st_rot[0]
                    st_rot[0] ^= 1
                    for si, (pv, keys) in enumerate(srcs):
                        act(sq_scr[si][:, pi * 342:(pi + 1) * 342], pv[:, pi, :], AF.Square, [keys[pi]], [sq_key[si]])
                        mm(ps[:, b, 0:342], ones_m[:], sq_scr[si][:, pi * 342:(pi + 1) * 342], si == 0, si == len(srcs) - 1,
                           [sq_key[si], "ones"], ["ps%d" % b], si == len(srcs) - 1)
                    rsqrt_from(dst_rstd[:, pi * 342:(pi + 1) * 342], ps[:, b, 0:342], 1.0, ["ps%d" % b], [dst_key])

            sqq = scr[4][:, 0:1026]
            for c in range(4):
                ob_rot[0] = 0
                pv, keys = proj3(wq, c * 128, 128, WQ)
                ts("dve", v3(qnT[:, c, 1:1027]), pv, vecs[:, V_GQA + c:V_GQA + c + 1], None, ALU.mult, None,
                   keys + ["vecs"], ["qnT"])
                act(v3(sqq), pv, AF.Square, keys, ["scr4"])
                for pi in range(3):
                    mm(ps[:, 3 + pi, 0:342], ones512[:], sqq[:, pi * 342:(pi + 1) * 342], c == 0, c == 3,
                       ["scr4", "ones"], ["ps%d" % (3 + pi)], c == 3)
            rq = scr[5][:, 0:1026]
            for pi in range(3):
                rsqrt_from(rq[:, pi * 342:(pi + 1) * 342], ps[:, 3 + pi, 0:342], 1.0, ["ps%d" % (3 + pi)], ["scr5"])
            for c in range(4):
                tt("pool" if c % 2 else "dve", qnT[:, c, 1:1027], qnT[:, c, 1:1027], rq, ALU.mult, ["qnT", "scr5"], ["qnT"])
            ob_rot[0] = 0

            def load_conv(gp, which, h):
                base = (1856, 2880, 832)[which]
                wv_ = ws_view(h, 1, KC, 256)
                S.dma("pool", wv_, w_in_v[:, :, base + gp * 256:base + gp * 256 + 256], writes=wkeys(h, 1))
                conv_w_q[(gp, which)] = (wv_, wkeys(h, 1))

            CONV_SLOTS = {0: (4, 5, 2), 1: (3, 0, 1), 2: (4, 5, 2), 3: (3, 0, 1)}
            load_conv(0, 2, 2)

            pk0, k0 = proj3(wkv, 0, 128, WKV)
            pk1, k1 = proj3(wkv, 128, 128, WKV)
            stats3([(pk0, k0), (pk1, k1)], ones256, scr[5][:, 0:1026], "scr5",
                   [scr[4][:, 0:1026], scr[6][:, 0:1026]], ["scr4", "scr6"])
            tmpk = scr[6][:, 0:1026]
            for c, (pv, keys) in enumerate([(pk0, k0), (pk1, k1)]):
                stt("dve", v3(tmpk), pv, vecs[:, V_GKV + c:V_GKV + c + 1], v3(scr[5][:, 0:1026]), ALU.mult, ALU.mult,
                    keys + ["scr5", "vecs"], ["scr6"])
                cp("pool", kvnT[:, c, 1280:2304], tmpk[:, 0:1024], ["scr6"], ["kvnT"])
            pA, kA = proj3(wkv, 256, 64, WKV)
            pB, kB = proj3(wkv, 320, 64, WKV)
            t1 = scr[4][0:64, 0:1026]
            t2 = scr[6][0:64, 0:1026]
            tt("dve", v3(t1), pA, v3(cosw[:, 1:1027]), ALU.mult, kA + ["rope"], ["scr4"])
            tt("dve", v3(t2), pB, v3(sinw[:, 1:1027]), ALU.mult, kB + ["rope"], ["scr6"])
            tt("pool", krT[:, 1280:2304], t1[:, 0:1024], t2[:, 0:1024], ALU.add, ["scr4", "scr6"], ["krT"])

            scr7 = BIG[:, 13712:13712 + BW]
            scr8 = BIG[:, 14740:14740 + BW]
            pb = scr[1]
            memset("pool", pb[:, 0:1], 0.0, ["scr1"])
            memset("pool", pb[:, 1027:1028], 0.0, ["scr1"])

            def conv_fin(g):
                co, cok = ((scr[3], "scr3"), (scr7, "scr7"))[g % 2]
                sqc, sqk = ((scr[4], "scr4"), (scr8, "scr8"))[g % 2]
                co = co[:, 0:1026]
                sqc = sqc[:, 0:1026]
                rc = scr[5][:, 0:1026]
                for pi in range(3):
                    b = 6 + st_rot[0]
                    st_rot[0] ^= 1
                    mm(ps[:, b, 0:342], ones128[:], sqc[:, pi * 342:(pi + 1) * 342], True, True,
                       [sqk, "ones"], ["ps%d" % b], True)
                    rsqrt_from(rc[:, pi * 342:(pi + 1) * 342], ps[:, b, 0:342], 1.0, ["ps%d" % b], ["scr5"])
                stt("dve", mixT[:, 8 + g, 1:1027], co, vecs[:, V_GMO + 8 + g:V_GMO + 9 + g], rc, ALU.mult, ALU.mult,
                    [cok, "scr5", "vecs"], ["mixT%d" % (8 + g)])

            for gp in range(4):
                if gp + 1 < 4:
                    for which in range(3):
                        load_conv(gp + 1, which, CONV_SLOTS[gp + 1][which])
                else:
                    S.dma("pool", ws_view(4, 2, 4, 1536), w_q_b_v, writes=wkeys(4, 2))
                    S.dma("pool", ws_view(2, 1, 4, 512), w_q_sw_v, writes=wkeys(2, 1))
                wts = [conv_w_q[(gp, w_)] for w_ in range(3)]
                for gl in range(2):
                    g = gp * 2 + gl
                    pcc, kcc = proj3(wts[0][0], gl * 128, 128, wts[0][1])
                    cp("act", v3(scr[0][:, 0:1026]), pcc, kcc, ["scr0"])
                    pch, kch = proj3(wts[1][0], gl * 128, 128, wts[1][1])
                    tt("dve", v3(pb[:, 1:1027]), pch, v3(scr[0][:, 0:1026]), ALU.mult, kch + ["scr0"], ["scr1"])
                    tb = scr[2][:, 0:1026]
                    cwc = V_CW + g * 3
                    act(tb, pb[:, 1:1027], AF.Identity, ["scr1", "vecs"], ["scr2"],
                        scale=vecs[:, cwc + 1:cwc + 2], bias=vecs[:, V_CB + g:V_CB + g + 1])
                    stt("dve", tb, pb[:, 0:1026], vecs[:, cwc:cwc + 1], tb, ALU.mult, ALU.add, ["scr1", "scr2", "vecs"], ["scr2"])
                    stt("dve", tb, pb[:, 2:1028], vecs[:, cwc + 2:cwc + 3], tb, ALU.mult, ALU.add, ["scr1", "scr2", "vecs"], ["scr2"])
                    pcb, kcb = proj3(wts[2][0], gl * 128, 128, wts[2][1])
                    co, cok = ((scr[3], "scr3"), (scr7, "scr7"))[g % 2]
                    sqc, sqk = ((scr[4], "scr4"), (scr8, "scr8"))[g % 2]
                    tt("dve", v3(co[:, 0:1026]), pcb, v3(tb), ALU.mult, kcb + ["scr2"], [cok])
                    act(sqc[:, 0:1026], co[:, 0:1026], AF.Square, [cok], [sqk])
                    if g > 0:
                        conv_fin(g - 1)
            conv_fin(7)
            S.dma("pool", ws_view(3, 1, 2, 2048), w_kv_b_v, writes=wkeys(3, 1))


            checkpoint(3, [(P2f, 0), (BIG[:, 7560:9616], 8224), (BIG[:, 0:2304], 10280), (BIG[0:64, 2304:3456], 12584)])
            S.barrier()
            wqb = ws_view(4, 2, 4, 1536)
            wqsw = ws_view(2, 1, 4, 512)
            wkvb = ws_view(3, 1, 2, 2048)
            WQB, WQS, WKB = wkeys(4, 2), wkeys(2, 1), wkeys(3, 1)
            tp_banks[0] = [6, 7]
            memset("dve", osb[:], 1.0, ["osb"])
            for i in range(2):
                memset("pool", qhr128[i][64:128, :], 0.0, ["qhr%d" % i])

            def fixed_pair():
                return 4
            mb = [6]

            def misc_bank():
                b = mb[0]
                mb[0] = 13 - b
                return b
            wout_q = {}

            def load_wout(cb, h0):
                wv_ = ws_view(h0, 2, KC, 512)
                S.dma("pool", wv_, w_out_v[:, :, cb * 512:(cb + 1) * 512], writes=wkeys(h0, 2))
                wout_q[cb] = (wv_, h0)

            sb_rot = [0]
            pt_rot = [0]
            acc_rot = [0]
            KP = [(0, 512), (512, 512), (1024, 512), (1536, 512), (2048, 256)]
            eb_rot = [0]
            EXPB = [0, 1, 2, 6, 7]

            def exp_bank():
                eb_rot[0] = (eb_rot[0] + 1) % len(EXPB)
                return EXPB[eb_rot[0]]

            def att_expand(h, bank_fn):
                i = h % 2
                for pi, (c0, n) in enumerate(KP):
                    b = bank_fn()
                    for kc in range(2):
                        mm(ps[:, b, 0:n], wkvb[:, kc, h * 256:h * 256 + 128], kvnT[:, kc, c0:c0 + n], kc == 0, kc == 1,
                           WKB + ["kvnT"], ["ps%d" % b], kc == 1)
                    cp("dve", Kh[i][:, c0:c0 + n], ps[:, b, 0:n], ["ps%d" % b], ["Kh%d" % i])
                    yield
                for k0_ in range(0, 18, 4):
                    nk = min(4, 18 - k0_)
                    b = bank_fn()
                    for kk in range(nk):
                        kt = k0_ + kk
                        for kc in range(2):
                            mm(ps[:, b, kk * 128:(kk + 1) * 128], kvnT[:, kc, kt * 128:(kt + 1) * 128],
                               wkvb[:, kc, h * 256 + 128:h * 256 + 256], kc == 0, kc == 1,
                               WKB + ["kvnT"], ["ps%d" % b], kk == nk - 1 and kc == 1)
                    cp("dve" if (k0_ // 4) % 2 else "act", Vh[i][:, k0_:k0_ + nk, 0:128],
                       ps[:, b, 0:nk * 128].rearrange("p (a n) -> p a n", a=nk), ["ps%d" % b], ["Vh%d" % i])
                    yield
                for pi, (c0, n) in enumerate(PC):
                    b = bank_fn()
                    for kc in range(4):
                        mm(ps[:, b, 0:n], wqb[:, kc, h * 192:h * 192 + 128], qnT[:, kc, c0:c0 + n], kc == 0, kc == 3,
                           WQB + ["qnT"], ["ps%d" % b], kc == 3)
                    cp("dve", qhn[i][:, c0:c0 + n], ps[:, b, 0:n], ["ps%d" % b], ["qhn%d" % i])
                    yield
                    bA = bank_fn()
                    for kc in range(4):
                        mm(ps[0:64, bA, 0:n], wqb[:, kc, h * 192 + 128:h * 192 + 192], qnT[:, kc, c0:c0 + n], kc == 0, kc == 3,
                           WQB + ["qnT"], ["ps%d" % bA], kc == 3)
                    tt("dve", rsc[:, 0, 0:n], ps[0:64, bA, 0:n], cosw[:, c0:c0 + n], ALU.mult, ["ps%d" % bA, "rope"], ["rsc0"])
                    yield
                    bB = bank_fn()
                    for kc in range(4):
                        mm(ps[0:64, bB, 0:n], wqsw[:, kc, h * 64:h * 64 + 64], qnT[:, kc, c0:c0 + n], kc == 0, kc == 3,
                           WQS + ["qnT"], ["ps%d" % bB], kc == 3)
                    tt("dve", rsc[:, 1, 0:n], ps[0:64, bB, 0:n], sinw[:, c0:c0 + n], ALU.mult, ["ps%d" % bB, "rope"], ["rsc1"])
                    tt("pool", qhr[i][:, c0:c0 + n], rsc[:, 0, 0:n], rsc[:, 1, 0:n], ALU.add, ["rsc0", "rsc1"], ["qhr%d" % i])
                    yield

            FB = [BIG[:, 9616 + k_ * 342:9616 + (k_ + 1) * 342] for k_ in range(4)]
            DACC = [BIG[:, 9616 + 1368 + k_ * 342:9616 + 1368 + (k_ + 1) * 342] for k_ in range(4)]
            PT2 = carve(BIG, 13712, 128, [4, 342], BF16)
            LAG = 2
            pend = []
            deferred = []

            def fin_a(h, ab, q0, nq, dk):
                S.op("pe", lambda: nc.tensor.matmul(ps[:, 3, 0:nq], lhsT=ones1[:], rhs=DACC[dk], start=False, stop=True),
                     ["dacc%d" % dk, "ones"], ["ps3"], True)
                cp("dve", FB[2], ps[:, 3, 0:nq], ["ps3"], ["F2"])
                cp("dve", FB[0], ps[:, ab, 0:nq], ["ps%d" % ab], ["F0"])
                deferred.append((h, q0, nq))

            def fin_b(h, q0, nq):
                tt("dve", FB[1], FB[0], FB[0], ALU.mult, ["F0"], ["F1"])
                b1 = misc_bank()
                mm(ps[:, b1, 0:nq], ones128[:], FB[1], True, True, ["F1", "ones"], ["ps%d" % b1], True)
                stt("dve", FB[3], FB[2], EPS, FB[2], ALU.mult, ALU.mult, ["F2"], ["F3"])
                tt("dve", FB[3], ps[:, b1, 0:nq], FB[3], ALU.add, ["ps%d" % b1, "F3"], ["F3"])
                act(FB[3], FB[3], AF.Ln, ["F3"], ["F3"])
                act(FB[3], FB[3], AF.Exp, ["F3"], ["F3"], scale=-0.5)
                stt("dve", mixT[:, h, q0:q0 + nq], FB[0], vecs[:, V_GMO + h:V_GMO + h + 1], FB[3], ALU.mult, ALU.mult,
                    ["F0", "F3", "vecs"], ["mixT%d" % h])

            def pv_step(h, ab, q0, nq, kt, slot, dk):
                i = h % 2
                mm(ps[:, ab, 0:nq], Vh[i][:, kt, 0:128], PT2[:, slot, 0:nq], kt == 0, kt == 17,
                   ["PT%d" % slot, "Vh%d" % i], ["ps%d" % ab], True)
                if kt % 2 == 1:
                    mm(ps[:, 3, 0:nq], onesb[:], PT2[:, slot, 0:nq], kt == 1, False, ["PT%d" % slot, "ones"], ["ps3"], True)
                elif kt < 1:
                    cp("dve", DACC[dk], PT2[:, slot, 0:nq], ["PT%d" % slot], ["dacc%d" % dk])
                else:
                    tt("dve", DACC[dk], DACC[dk], PT2[:, slot, 0:nq], ALU.add, ["PT%d" % slot, "dacc%d" % dk], ["dacc%d" % dk])
                if kt == 17:
                    fin_a(h, ab, q0, nq, dk)

            ada_issue()
            ada_issue()
            for _ in att_expand(0, exp_bank):
                pass
            pass_ctr = [0]
            for h in range(8):
                i = h % 2
                gen = att_expand(h + 1, misc_bank) if h + 1 < 8 else None
                step_idx = 0
                for (q0, nq) in PC:
                    ab = 4 + acc_rot[0]
                    acc_rot[0] ^= 1
                    dk = pass_ctr[0] % 2
                    pass_ctr[0] += 1
                    for kt in range(18):
                        sbk = sb_rot[0]
                        sb_rot[0] = (sbk + 1) % 3
                        mm(ps[:, sbk, 0:nq], Kh[i][:, kt * 128:(kt + 1) * 128], qhn[i][:, q0:q0 + nq], True, False,
                           ["Kh%d" % i, "qhn%d" % i], ["ps%d" % sbk], False)
                        mm(ps[:, sbk, 0:nq], krT128[:, kt * 128:(kt + 1) * 128], qhr128[i][:, q0:q0 + nq], False, True,
                           ["krT", "qhr%d" % i], ["ps%d" % sbk], True)
                        slot = pt_rot[0]
                        pt_rot[0] = (slot + 1) % 4
                        act(PT2[:, slot, 0:nq], ps[:, sbk, 0:nq], AF.Exp, ["ps%d" % sbk], ["PT%d" % slot], scale=SCALE)
                        pend.append((h, ab, q0, nq, kt, slot, dk))
                        if len(pend) > LAG:
                            pv_step(*pend.pop(0))
                        if kt == 6 and deferred:
                            fin_b(*deferred.pop(0))
                        step_idx += 1
                        if gen is not None and step_idx >= 8 and step_idx % 2 == 0:
                            if next(gen, "done") == "done":
                                gen = None
                                if h + 1 == 7:
                                    load_wout(0, 4)
                                    load_wout(1, 2)
                    ada_step()
                    if q0 == 1:
                        ada_step()
                if gen is not None:
                    for _ in gen:
                        pass
                    if h + 1 == 7:
                        load_wout(0, 4)
                        load_wout(1, 2)
            while pend:
                pv_step(*pend.pop(0))
            while deferred:
                fin_b(*deferred.pop(0))
            while ada_pending:
                ada_step()

            checkpoint(4, [(P2f, 0)])
            S.barrier()
            ws_allowed[0] = [0, 2, 4]
            ws_rot[0] = 0
            for t in range(8):
                S.dma("sp", x1[:, t, :], xw[t * 128:(t + 1) * 128, :], writes=["x1_%d" % t])
            S.dma("sp", xh, xw[1024:1025, :], writes=["xh"])
            ob8 = [0]
            for cb in range(4):
                wv, h0 = wout_q[cb]
                for t in range(9):
                    M = 128 if t < 8 else 1
                    c0 = 1 + 128 * t if t < 8 else 1025
                    b = ob8[0]
                    ob8[0] = (b + 1) % 8
                    for k in range(KC):
                        mm(ps[0:M, b, :], mixT[:, k, c0:c0 + M], wv[:, k, :], k == 0, k == KC - 1,
                           wkeys(h0, 2) + ["mixT%d" % k], ["ps%d" % b], k == KC - 1)
                    tb_ = tmpb[0:M, b % 2, :]
                    tt("dve", tb_, ps[0:M, b, :], gt_bc[0:M, cb * 512:(cb + 1) * 512], ALU.mult,
                       ["ps%d" % b, "gt_bc"], ["tmpb%d" % (b % 2)])
                    if t < 8:
                        tt("pool", x1[:, t, cb * 512:(cb + 1) * 512], x1[:, t, cb * 512:(cb + 1) * 512], tb_, ALU.add,
                           ["tmpb%d" % (b % 2), "x1_%d" % t], ["x1_%d" % t])
                    else:
                        tt("pool", x1h[:, cb * 512:(cb + 1) * 512], xh[:, cb * 512:(cb + 1) * 512], tb_, ALU.add,
                           ["tmpb%d" % (b % 2), "xh"], ["x1h"])
                if cb + 2 < 4:
                    load_wout(cb + 2, h0)
            checkpoint(5, [(BIG[:], 0)])
            S.barrier()
            ada_bc()

            hs_rot = [0]

            def next_half4():
                h = hs_rot[0]
                hs_rot[0] = (h + 1) % 4
                return h
            wd_rot = [0]
            db_rot = [0]
            up_w = {}

            def load_up(u):
                j0 = u * 2
                nj = min(2, NJ - j0)
                res = []
                for base in (0, FFN):
                    h = next_half4()
                    wv = ws_view(h, 1, KC, 256)
                    S.dma("pool", wv[:, :, 0:nj * 128], w_up_v[:, :, base + j0 * 128:base + (j0 + nj) * 128], writes=wkeys(h, 1))
                    res.append((wv, wkeys(h, 1)))
                up_w[u] = res

            wdq = {}
            tm_rot = [0]

            def load_wd(grp, cb):
                G_ = min(8, NJ - grp * 8)
                h = 4 + wd_rot[0]
                wd_rot[0] ^= 1
                wd = ws_view(h, 1, 8, 512)
                S.dma("pool", wd[:, 0:G_, :], w_down_v[:, grp * 8:grp * 8 + G_, cb * 512:(cb + 1) * 512], writes=wkeys(h, 1))
                wdq[(grp, cb)] = (wd, wkeys(h, 1))

            load_up(0)
            tp_banks[0] = [4, 5, 6, 7]
            act(XN[0:1, 0, :], x1h, AF.Square, ["x1h"], ["xn0", "ssq"], accum_out=ssq[0:1, 0:1])
            rsqrt_from(rs[0:1, 0:1], ssq[0:1, 0:1], 1.0 / D, ["ssq"], ["rs"])
            act(XN[0:1, 0, :], x1h, AF.Copy, ["x1h", "rs"], ["xn0"], scale=rs[0:1, 0:1])
            for c in range(KC):
                bank = next_tp()
                q = bank
                tpv = ps[:, bank, :].bitcast(BF16)[:, 0:512]
                S.op("pe", lambda c=c, tpv=tpv: nc.tensor.transpose(tpv[:, 0:1], XN[0:1, 0, c * 128:(c + 1) * 128], ident[0:1, 0:1]),
                     ["xn0", "ident"], ["ps%d" % q, "ps%d" % bank], True)
                ts("dve", hT[:, c, 1025:1026], tpv[:, 0:1], AB[:, 4, c:c + 1], AB[:, 5, c:c + 1], ALU.mult, ALU.add,
                   ["ps%d" % q, "ps%d" % bank, "AB"], ["hT%d" % c])
                memset("pool", hT[:, c, 0:1], 0.0, ["hT%d" % c])
                memset("pool", hT[:, c, 1026:1028], 0.0, ["hT%d" % c])
            for g in range(2):
                for i in range(4):
                    t = g * 4 + i
                    S.op("dve", lambda i=i, t=t: nc.vector.scalar_tensor_tensor(
                        out=XN[:, i, :], in0=x1[:, t, :], scalar=1.0, in1=x1[:, t, :], op0=ALU.mult, op1=ALU.mult,
                        accum_out=ssq[:, i:i + 1]), ["x1_%d" % t], ["xn%d" % i, "ssq"])
                rsqrt_from(rs[:, 0:4], ssq[:, 0:4], 1.0 / D, ["ssq"], ["rs"])
                for i in range(4):
                    t = g * 4 + i
                    act(XN[:, i, :], x1[:, t, :], AF.Copy, ["x1_%d" % t, "rs"], ["xn%d" % i], scale=rs[:, i:i + 1])
                for c in range(KC):
                    bank = next_tp()
                    q = bank
                    tpv = ps[:, bank, :].bitcast(BF16)[:, 0:512]
                    for i in range(4):
                        S.op("pe", lambda i=i, c=c, tpv=tpv: nc.tensor.transpose(
                            tpv[:, i * 128:(i + 1) * 128], XN[:, i, c * 128:(c + 1) * 128], ident[:]),
                            ["xn%d" % i, "ident"], ["ps%d" % q, "ps%d" % bank], i == 3)
                    dst = hT[:, c, 1 + g * 512:1 + g * 512 + 512]
                    if c % 2 == 0:
                        act(dst, tpv[:, 0:512], AF.Identity, ["ps%d" % q, "ps%d" % bank, "AB"], ["hT%d" % c],
                            scale=AB[:, 4, c:c + 1], bias=AB[:, 5, c:c + 1])
                    else:
                        ts("dve", dst, tpv[:, 0:512], AB[:, 4, c:c + 1], AB[:, 5, c:c + 1], ALU.mult, ALU.add,
                           ["ps%d" % q, "ps%d" % bank, "AB"], ["hT%d" % c])
            checkpoint(6, [(P1f, 0)])
            S.barrier()

            for j in range(NJ):
                u, jl2 = j // 2, j % 2
                if j % 8 == 0:
                    load_wd(j // 8, 0)
                    load_wd(j // 8, 1)
                if jl2 == 0 and u + 1 <= (NJ - 1) // 2:
                    load_up(u + 1)
                grp, jl = j // 8, j % 8
                cA, cG = cbuf[(j % 2) * 2], cbuf[(j % 2) * 2 + 1]
                kA_, kG_ = "cb%d" % ((j % 2) * 2), "cb%d" % ((j % 2) * 2 + 1)
                for which, (cbf, ck) in enumerate(((cA, kA_), (cG, kG_))):
                    wv, wk = up_w[u][which]
                    o = ob_rot[0]
                    ob_rot[0] ^= 1
                    keys = ["ps%d" % (o * 3 + pi) for pi in range(3)]
                    for pi in range(3):
                        b = o * 3 + pi
                        for k in range(KC):
                            mm(ps[:, b, 0:344], wv[:, k, jl2 * 128:(jl2 + 1) * 128], hT[:, k, 342 * pi:342 * pi + 344],
                               k == 0, k == KC - 1, wk + ["hT%d" % k], ["ps%d" % b], k == KC - 1)
                    jj = j + which * NJ
                    fw = V_FCW + jj * 3
                    pvw = ps[:, o * 3:o * 3 + 3, 0:344]
                    rb = (2 * j + which) % 2
                    rw, rk = rawb[rb], "raw%d" % rb
                    cp("act", rw, pvw, keys, [rk])
                    act(v3(cbf), rw[:, :, 1:343], AF.Identity, [rk, "vecs"], [ck],
                        scale=vecs[:, fw + 1:fw + 2], bias=vecs[:, V_FCB + jj:V_FCB + jj + 1])
                    stt("dve", v3(cbf), rw[:, :, 0:342], vecs[:, fw:fw + 1], v3(cbf), ALU.mult, ALU.add, [rk, ck, "vecs"], [ck])
                    stt("dve", v3(cbf), rw[:, :, 2:344], vecs[:, fw + 2:fw + 3], v3(cbf), ALU.mult, ALU.add, [rk, ck, "vecs"], [ck])
                act(cG, cG, AF.Silu, [kG_], [kG_])
                tt("pool", actT[:, jl, :], cA[:, 0:T], cG[:, 0:T], ALU.mult, [kA_, kG_], ["act%d" % jl])
                G = min(8, NJ - grp * 8)
                if jl == G - 1:
                    last_o = ob_rot[0] ^ 1
                    order = [6, 7] + [(1 - last_o) * 3 + i_ for i_ in range(3)] + [last_o * 3 + i_ for i_ in range(3)]
                    for cb in range(4):
                        wd, wk_ = wdq.pop((grp, cb))

                        def evac(t, b, cb=cb):
                            tsl = tm_rot[0]
                            tm_rot[0] ^= 1
                            tb_ = tmpb[:, tsl, :]
                            tt("dve", tb_, ps[:, b, :], gt_bc[:, cb * 512:(cb + 1) * 512], ALU.mult,
                               ["ps%d" % b, "gt_bc"], ["tmpb%d" % tsl])
                            tt("pool", x1[:, t, cb * 512:(cb + 1) * 512], x1[:, t, cb * 512:(cb + 1) * 512], tb_, ALU.add,
                               ["tmpb%d" % tsl, "x1_%d" % t], ["x1_%d" % t])

                        if cb == 0 and G > 1:
                            for t in range(8):
                                b = order[t]
                                for q_ in range(G - 1):
                                    mm(ps[:, b, :], actT[:, q_, t * 128:(t + 1) * 128], wd[:, q_, :], q_ == 0, False,
                                       wk_ + ["act%d" % q_], ["ps%d" % b], q_ == G - 2)
                            for t in range(8):
                                b = order[t]
                                mm(ps[:, b, :], actT[:, G - 1, t * 128:(t + 1) * 128], wd[:, G - 1, :], False, True,
                                   wk_ + ["act%d" % (G - 1)], ["ps%d" % b], True)
                                evac(t, b)
                        else:
                            for t in range(8):
                                b = 6 + db_rot[0]
                                db_rot[0] ^= 1
                                for q_ in range(G):
                                    mm(ps[:, b, :], actT[:, q_, t * 128:(t + 1) * 128], wd[:, q_, :], q_ == 0, q_ == G - 1,
                                       wk_ + ["act%d" % q_], ["ps%d" % b], q_ == G - 1)
                                evac(t, b)
                        if cb + 2 < 4:
                            load_wd(grp, cb + 2)

            gfv = wsf[:, 0, :]
            S.dma("sp", gfv, gfin, writes=["ws0"])
            for t in range(8):
                act(XNf[:, t % 2, :], x1[:, t, :], AF.Square,
                    ["x1_%d" % t], ["ws%d" % (2 + t % 2), "ssq"], accum_out=ssq[:, t:t + 1])
                rsqrt_from(rs[:, t:t + 1], ssq[:, t:t + 1], 1.0 / D, ["ssq"], ["rs"])
                stt("dve", x1[:, t, :], x1[:, t, :], rs[:, t:t + 1], gfv, ALU.mult, ALU.mult, ["x1_%d" % t, "rs", "ws0"], ["x1_%d" % t])
                S.dma("sp", y[t * 128:(t + 1) * 128, :], x1[:, t, :], reads=["x1_%d" % t], is_output=True)
            S.finish()
      except _Stop:
            S.finish()
    return nc


def _rope_tables(pos):
    inv = 10000.0 ** (-np.arange(0, 32, 2, dtype=np.float64) / 32.0)
    row = (pos // 64).astype(np.float64)
    col = (pos % 64).astype(np.float64)
    ang = np.concatenate([row[:, None] * inv[None, :], col[:, None] * inv[None, :]], axis=-1)
    cos = np.cos(ang)
    sin = np.sin(ang)
    cos2 = np.repeat(cos, 2, axis=1).T
    sin2 = np.repeat(sin, 2, axis=1).T.copy()
    sin2[0::2, :] *= -1.0
    return np.ascontiguousarray(cos2.astype(np.float32)), np.ascontiguousarray(sin2.astype(np.float32))


_NC_CACHE = {}


def kernel(x, c, ctx, c_ctx, w_ada, b_ada, g_mix_norm, w_in, g_q_a, w_q_b, g_kv_a, w_kv_b,
           conv_w, conv_b, g_mix_out, w_out, g_ffn_norm, w_up, ffn_conv_w, ffn_conv_b, w_down, g_final):
    f = lambda a: np.ascontiguousarray(np.asarray(a, dtype=np.float32))
    x, c, ctx, c_ctx = f(x), f(c), f(ctx), f(c_ctx)
    w_ada, b_ada, w_in, w_q_b, w_kv_b, w_out, w_up, w_down = (f(w_ada[0]), f(b_ada[0]), f(w_in[0]), f(w_q_b[0]),
                                                              f(w_kv_b[0]), f(w_out[0]), f(w_up[0]), f(w_down[0]))
    g_mix_norm, g_q_a, g_kv_a, conv_w, conv_b, g_mix_out, g_ffn_norm, ffn_conv_w, ffn_conv_b = (
        f(g_mix_norm[0]), f(g_q_a[0]), f(g_kv_a[0]), f(conv_w[0]), f(conv_b[0]), f(g_mix_out[0]),
        f(g_ffn_norm[0]), f(ffn_conv_w[0]), f(ffn_conv_b[0]))
    g_final = f(g_final)
    return _prep_and_run(locals())


def _prep_and_run(v, stage=99, cores=None):
    (x, c, ctx, c_ctx, w_ada, b_ada, w_in, w_q_b, w_kv_b, w_out, w_up, w_down, g_mix_norm, g_q_a, g_kv_a, conv_w,
     conv_b, g_mix_out, g_ffn_norm, ffn_conv_w, ffn_conv_b, g_final) = [v[k] for k in (
        "x", "c", "ctx", "c_ctx", "w_ada", "b_ada", "w_in", "w_q_b", "w_kv_b", "w_out", "w_up", "w_down",
        "g_mix_norm", "g_q_a", "g_kv_a", "conv_w", "conv_b", "g_mix_out", "g_ffn_norm", "ffn_conv_w", "ffn_conv_b",
        "g_final")]
    if stage not in _NC_CACHE:
        _NC_CACHE[stage] = build_nc(stage)
    nc = _NC_CACHE[stage]

    pm = lambda v: np.ascontiguousarray(v.reshape(-1, 128).T)
    perm = np.arange(64).reshape(32, 2)[:, ::-1].reshape(-1)
    w_krs = np.ascontiguousarray(w_in[:, 768:832][:, perm])
    qcols = np.concatenate([h * 192 + 128 + perm for h in range(8)])
    w_q_sw = np.ascontiguousarray(w_q_b[:, qcols])
    ident = np.eye(128, dtype=np.float32)
    in_maps = []
    for core in range(8):
        b, half = core // 2, core % 2
        rev = half == 1
        if not rev:
            own = np.arange(0, 1024)
            halo = np.array([1024, 1025])
            oth = np.arange(1024, 2048)
        else:
            own = np.arange(2047, 1023, -1)
            halo = np.array([1023, 1022])
            oth = np.arange(0, 1024)
        win = np.concatenate([own, halo])
        xw = np.ascontiguousarray(x[b][win])
        xo = np.ascontiguousarray(np.concatenate([x[b][oth], ctx[b]], axis=0))
        cs = np.ascontiguousarray(np.stack([c[b], c_ctx], axis=-1).reshape(16, 128, 2).transpose(1, 0, 2))
        vec = np.zeros((128, 512), np.float32)
        vec[:, 0:16] = pm(g_mix_norm)
        vec[:, 16:32] = pm(g_ffn_norm)
        vec[:, 32:36] = pm(g_q_a)
        vec[:, 36:38] = pm(g_kv_a)
        cw = conv_w[::-1] if rev else conv_w
        vec[:, 38:62] = cw.reshape(3, 8, 128).transpose(2, 1, 0).reshape(128, 24)
        vec[:, 62:70] = pm(conv_b)
        vec[:, 70:86] = pm(g_mix_out)
        fw = ffn_conv_w[::-1] if rev else ffn_conv_w
        vec[:, 86:344] = fw.reshape(3, 86, 128).transpose(2, 1, 0).reshape(128, 258)
        vec[:, 344:430] = pm(ffn_conv_b)
        for si_, sg_ in enumerate((0, 1, 3, 4)):
            vec[:, 430 + si_ * 16:430 + si_ * 16 + 16] = pm(b_ada[sg_ * 2048:(sg_ + 1) * 2048])
        cw_, sw_ = _rope_tables(win)
        ropew = np.zeros((64, 2, BW), np.float32)
        ropew[:, 0, 1:1027] = cw_
        ropew[:, 1, 1:1027] = sw_
        co_, so_ = _rope_tables(oth)
        ropeo = np.ascontiguousarray(np.stack([co_, so_], axis=1))
        in_maps.append({
            "xw": xw, "xo": xo, "cs": cs, "w_ada": w_ada, "b_ada": b_ada.reshape(1, -1), "w_in": w_in,
            "w_krs": w_krs, "w_q_b": w_q_b, "w_q_sw": w_q_sw, "w_kv_b": w_kv_b, "w_out": w_out, "w_up": w_up,
            "w_down": w_down, "vec": vec, "gfin": np.ascontiguousarray(np.broadcast_to(g_final, (128, D))),
            "ropew": ropew, "ropeo": ropeo, "ident": ident.astype(ml_dtypes.bfloat16), "identf": ident,
        })
    if cores is not None:
        res = run_bass_kernel_spmd(nc, [in_maps[i] for i in cores], core_ids=list(range(len(cores))))
        return res
    res = run_bass_kernel_spmd(nc, in_maps, core_ids=list(range(8)))
    out = np.zeros((4, 2048, D), np.float32)
    for core in range(8):
        b, half = core // 2, core % 2
        yv = res.results[core]["y"]
        if half == 0:
            out[b, 0:1024] = yv
        else:
            out[b, 1024:2048] = yv[::-1]
    return out
```
